# Optimizing a Trainium2 kernel written in Bass

```python
import jax, jax.numpy as jnp
from jax import lax
import numpy as np

D_MODEL = 1024
BATCH = 8
SEQ = 2048
DEPTH = 4
DEC_BATCH = 128
DEC_SEQ = 8
PAST_LEN = 16384
PAGE_SIZE = 128

N_POOL_GROUPS = 4
POOL_WINDOWS = (2, 4, 8, 16)
POOL_WIDTH = D_MODEL
POOL_GROUP_DIM = POOL_WIDTH // N_POOL_GROUPS
POOL_BUF = max(POOL_WINDOWS) - 1
CONV_WIDTH = D_MODEL
CONV_KERNEL = 31
CONV_BUF = CONV_KERNEL - 1
N_MEM = 256
XATTN_HEADS = 4
XATTN_HEAD_DIM = D_MODEL // XATTN_HEADS
XATTN_WIDTH = XATTN_HEADS * XATTN_HEAD_DIM
N_BRANCHES = 3
D_FF = 2816
EPS = 1e-6
IN_COLS = POOL_WIDTH + 2 * CONV_WIDTH + XATTN_WIDTH + N_BRANCHES * D_MODEL

kernel_name = "hybrid_pool_conv_xattn_macaron_decoder_step"


def rmsnorm(x, g):
    xf = x.astype(jnp.float32)
    y = xf * lax.rsqrt(jnp.mean(xf * xf, axis=-1, keepdims=True) + EPS)
    return (y * g.astype(jnp.float32)).astype(x.dtype)


def layernorm(x, g, b):
    xf = x.astype(jnp.float32)
    mu = jnp.mean(xf, axis=-1, keepdims=True)
    var = jnp.mean(jnp.square(xf - mu), axis=-1, keepdims=True)
    y = (xf - mu) * lax.rsqrt(var + EPS)
    return (y * g.astype(jnp.float32) + b.astype(jnp.float32)).astype(x.dtype)


def swiglu_ffn(h, w_gu, w_down):
    gate, up = jnp.split(h @ w_gu, 2, axis=-1)
    return (jax.nn.silu(gate) * up) @ w_down


def pool_mixer(u, buf, pos0, w_pool, pool_scale):
    B, T, _ = u.shape
    ext = jnp.concatenate([buf.astype(u.dtype), u], axis=1)
    cs = jnp.cumsum(ext.astype(jnp.float32), axis=1)
    cs = jnp.concatenate([jnp.zeros_like(cs[:, :1]), cs], axis=1)
    pos = pos0 + jnp.arange(T)
    means = []
    for gi, w in enumerate(POOL_WINDOWS):
        sl = slice(gi * POOL_GROUP_DIM, (gi + 1) * POOL_GROUP_DIM)
        hi = cs[:, POOL_BUF + 1:POOL_BUF + 1 + T, sl]
        lo = cs[:, POOL_BUF + 1 - w:POOL_BUF + 1 - w + T, sl]
        cnt = jnp.minimum(pos + 1, w).astype(jnp.float32)[None, :, None]
        means.append((hi - lo) / cnt)
    mean = jnp.stack(means, axis=2)
    pooled = (mean - u.reshape(B, T, N_POOL_GROUPS, POOL_GROUP_DIM).astype(jnp.float32)).astype(u.dtype)
    y = jnp.einsum('btgc,gcd->btgd', pooled, w_pool).reshape(B, T, POOL_WIDTH)
    return y * pool_scale, ext[:, -POOL_BUF:]


def conv_module(v, buf, w_dw, b_dw, ln_g, ln_b, w_pw):
    a, gate = jnp.split(v, 2, axis=-1)
    z = a * jax.nn.sigmoid(gate)
    ext = jnp.concatenate([buf.astype(z.dtype), z], axis=1)
    y = lax.conv_general_dilated(
        ext, w_dw[:, None, :].astype(z.dtype), window_strides=(1,), padding='VALID',
        dimension_numbers=('NWC', 'WIO', 'NWC'), feature_group_count=CONV_WIDTH)
    y = jax.nn.silu(layernorm(y + b_dw, ln_g, ln_b))
    return y @ w_pw, ext[:, -CONV_BUF:]


def memory_kv(mem, mem_g, w_kv):
    B = mem.shape[0]
    k, v = jnp.split(rmsnorm(mem, mem_g) @ w_kv, 2, axis=-1)
    shp = (B, N_MEM, XATTN_HEADS, XATTN_HEAD_DIM)
    return k.reshape(shp), v.reshape(shp)


def cross_attend(q, k, v):
    B, T = q.shape[:2]
    s = jnp.einsum('bthd,bmhd->bhtm', q, k.astype(q.dtype),
                   preferred_element_type=jnp.float32) * (XATTN_HEAD_DIM ** -0.5)
    p = jax.nn.softmax(s, axis=-1).astype(v.dtype)
    o = jnp.einsum('bhtm,bmhd->bthd', p, v)
    return o.reshape(B, T, XATTN_WIDTH).astype(q.dtype)


def trunk_layer(x, pos0, pool_buf, conv_buf, mem_k, mem_v,
                ffn1_norm, ffn1_w_gu, ffn1_w_down, mix_norm, w_in,
                w_pool, pool_scale, w_dw, b_dw, conv_ln_g, conv_ln_b, w_pw,
                w_out, ffn2_norm, ffn2_w_gu, ffn2_w_down):
    B, T, _ = x.shape
    x = x + 0.5 * swiglu_ffn(rmsnorm(x, ffn1_norm), ffn1_w_gu, ffn1_w_down)
    h = rmsnorm(x, mix_norm)
    u, v, q, gl = jnp.split(h @ w_in, [POOL_WIDTH, POOL_WIDTH + 2 * CONV_WIDTH,
                                        POOL_WIDTH + 2 * CONV_WIDTH + XATTN_WIDTH], axis=-1)
    y_pool, new_pool = pool_mixer(u, pool_buf, pos0, w_pool, pool_scale)
    y_conv, new_conv = conv_module(v, conv_buf, w_dw, b_dw, conv_ln_g, conv_ln_b, w_pw)
    y_att = cross_attend(q.reshape(B, T, XATTN_HEADS, XATTN_HEAD_DIM), mem_k, mem_v)
    gates = jax.nn.sigmoid(gl.reshape(B, T, N_BRANCHES, D_MODEL))
    merged = gates[:, :, 0] * y_pool + gates[:, :, 1] * y_conv + gates[:, :, 2] * y_att
    x = x + merged @ w_out
    x = x + 0.5 * swiglu_ffn(rmsnorm(x, ffn2_norm), ffn2_w_gu, ffn2_w_down)
    return x, new_pool, new_conv


def setup_inputs(seed: int = 0) -> dict:
    key = jax.random.key(seed)
    ks = jax.random.split(key, 32)
    f32 = jnp.float32

    def nrm(k, shape, scale):
        return jax.random.normal(k, shape, f32) * scale

    def gain(k, shape):
        return 1.0 + 0.05 * jax.random.normal(k, shape, f32)

    return {
        "x_prompt": nrm(ks[0], (BATCH, SEQ, D_MODEL), 1.0),
        "x_sample": nrm(ks[1], (DEC_BATCH, DEC_SEQ, D_MODEL), 1.0),
        "mem_prompt": nrm(ks[2], (BATCH, N_MEM, D_MODEL), 1.0),
        "state_pool": nrm(ks[3], (DEPTH, DEC_BATCH, POOL_BUF, POOL_WIDTH), 1.0),
        "state_conv": nrm(ks[4], (DEPTH, DEC_BATCH, CONV_BUF, CONV_WIDTH), 0.5),
        "cache_mem_k": nrm(ks[5], (DEPTH, DEC_BATCH, N_MEM, XATTN_HEADS, XATTN_HEAD_DIM), 1.0),
        "cache_mem_v": nrm(ks[6], (DEPTH, DEC_BATCH, N_MEM, XATTN_HEADS, XATTN_HEAD_DIM), 1.0),
        "ffn1_norm": gain(ks[7], (DEPTH, D_MODEL)),
        "ffn1_w_gu": nrm(ks[8], (DEPTH, D_MODEL, 2 * D_FF), D_MODEL ** -0.5),
        "ffn1_w_down": nrm(ks[9], (DEPTH, D_FF, D_MODEL), D_FF ** -0.5),
        "mix_norm": gain(ks[10], (DEPTH, D_MODEL)),
        "w_in": nrm(ks[11], (DEPTH, D_MODEL, IN_COLS), D_MODEL ** -0.5),
        "mem_norm": gain(ks[12], (DEPTH, D_MODEL)),
        "w_mem_kv": nrm(ks[13], (DEPTH, D_MODEL, 2 * XATTN_WIDTH), D_MODEL ** -0.5),
        "w_pool": nrm(ks[14], (DEPTH, N_POOL_GROUPS, POOL_GROUP_DIM, POOL_GROUP_DIM), POOL_GROUP_DIM ** -0.5),
        "pool_scale": 0.5 + 0.1 * jax.random.normal(ks[15], (DEPTH, POOL_WIDTH), f32),
        "w_dw": nrm(ks[16], (DEPTH, CONV_KERNEL, CONV_WIDTH), CONV_KERNEL ** -0.5),
        "b_dw": nrm(ks[17], (DEPTH, CONV_WIDTH), 0.02),
        "conv_ln_g": gain(ks[18], (DEPTH, CONV_WIDTH)),
        "conv_ln_b": nrm(ks[19], (DEPTH, CONV_WIDTH), 0.02),
        "w_pw": nrm(ks[20], (DEPTH, CONV_WIDTH, D_MODEL), CONV_WIDTH ** -0.5),
        "w_out": nrm(ks[21], (DEPTH, D_MODEL, D_MODEL), D_MODEL ** -0.5),
        "ffn2_norm": gain(ks[22], (DEPTH, D_MODEL)),
        "ffn2_w_gu": nrm(ks[23], (DEPTH, D_MODEL, 2 * D_FF), D_MODEL ** -0.5),
        "ffn2_w_down": nrm(ks[24], (DEPTH, D_FF, D_MODEL), D_FF ** -0.5),
        "final_norm": gain(ks[25], (D_MODEL,)),
    }


def reference(x_prompt, x_sample, mem_prompt, state_pool, state_conv, cache_mem_k, cache_mem_v,
              ffn1_norm, ffn1_w_gu, ffn1_w_down, mix_norm, w_in, mem_norm, w_mem_kv,
              w_pool, pool_scale, w_dw, b_dw, conv_ln_g, conv_ln_b, w_pw, w_out,
              ffn2_norm, ffn2_w_gu, ffn2_w_down, final_norm):
    xp, xs = x_prompt, x_sample
    pool_buf_p = jnp.zeros((BATCH, POOL_BUF, POOL_WIDTH), xp.dtype)
    conv_buf_p = jnp.zeros((BATCH, CONV_BUF, CONV_WIDTH), xp.dtype)
    pool_p, conv_p, memk_p, memv_p, pool_s, conv_s = [], [], [], [], [], []
    for l in range(DEPTH):
        shared = (ffn1_norm[l], ffn1_w_gu[l], ffn1_w_down[l], mix_norm[l], w_in[l],
                  w_pool[l], pool_scale[l], w_dw[l], b_dw[l], conv_ln_g[l], conv_ln_b[l], w_pw[l],
                  w_out[l], ffn2_norm[l], ffn2_w_gu[l], ffn2_w_down[l])
        mk, mv = memory_kv(mem_prompt, mem_norm[l], w_mem_kv[l])
        xp, npool, nconv = trunk_layer(xp, 0, pool_buf_p, conv_buf_p, mk, mv, *shared)
        pool_p.append(npool)
        conv_p.append(nconv)
        memk_p.append(mk)
        memv_p.append(mv)
        xs, spool, sconv = trunk_layer(xs, PAST_LEN, state_pool[l], state_conv[l],
                                       cache_mem_k[l], cache_mem_v[l], *shared)
        pool_s.append(spool)
        conv_s.append(sconv)
    y_prompt = rmsnorm(xp, final_norm)
    y_sample = rmsnorm(xs, final_norm)
    return (y_prompt, y_sample, jnp.stack(pool_p), jnp.stack(conv_p), jnp.stack(memk_p),
            jnp.stack(memv_p), jnp.stack(pool_s), jnp.stack(conv_s))
```

```python
import contextlib
import numpy as np
import concourse.bass as bass
import concourse.mybir as mybir
from concourse.bass_utils import run_bass_kernel_spmd

F32 = mybir.dt.float32
BF16 = mybir.dt.bfloat16
AF = mybir.ActivationFunctionType
ALU = mybir.AluOpType
AX = mybir.AxisListType

DEPTH = 4
NCORES = 8
D = 1024
DFF = 2816
NFF = 22
SEQ = 2048
NPASS = 2
TP = SEQ // NPASS
NSEQ = 16
NS = NSEQ // NPASS
ST = 8
NST = NS * ST
NTOK = TP + NST
NMEM = 256
PB, CB = 15, 30
INC = 7168
GL = 4096
EPS = 1e-6
NSLOT = 4
SLOTE = 4096
TBLK = [(0, 512), (512, 512), (1024, NST)]
PBLK = [(0, 512), (512, 512)]
PL = PB + TP
PW = PB + ST
CL = CB + TP
CW = CB + ST
ENGS = ("pe", "act", "dve", "pool", "sp")
import os
DBG_LAYERS = int(os.environ.get("DBG_LAYERS", DEPTH))
DBG_STAGE = int(os.environ.get("DBG_STAGE", 3))
DBG_PASSES = int(os.environ.get("DBG_PASSES", NPASS))
DBG_CORES = int(os.environ.get("DBG_CORES", NCORES))
VROWS = ["ffn1_norm", "mix_norm", "pool_scale", "b_dw", "conv_ln_g", "conv_ln_b", "ffn2_norm", "mem_norm"]


class Op:
    __slots__ = ("eng", "fn", "deps", "is_dma", "sem", "semval", "signal", "cnt", "idx", "lab")

    def __init__(self, eng, fn, is_dma=False):
        self.eng, self.fn, self.is_dma = eng, fn, is_dma
        self.deps = []
        self.sem = None
        self.semval = 0
        self.signal = False
        self.cnt = 0
        self.idx = 0


class Prog:
    def __init__(self, arenas=()):
        self.ops = {e: [] for e in ENGS}
        self.last_w = {}
        self.readers = {}
        self.dma_cnt = {}
        self.dma_last = {}
        self.arenas = set(arenas)
        self.arena_ops = {a: {} for a in arenas}
        self.fence = {a: {} for a in arenas}
        self.nops = 0
        self.cur = ""

    @staticmethod
    def _slot(op):
        return ("d", op.sem) if op.is_dma else ("e", op.eng)

    def _add(self, table, op):
        k = self._slot(op)
        o = table.get(k)
        if o is None or o.idx < op.idx:
            table[k] = op

    def _track(self, op, reads, writes):
        deps = {}

        def add(d):
            if d is op:
                return
            if d.eng == "pe" and op.eng == "pe" and not d.is_dma and not op.is_dma:
                return
            k = self._slot(d)
            o = deps.get(k)
            if o is None or o.idx < d.idx:
                deps[k] = d

        for k in list(reads) + list(writes):
            if isinstance(k, tuple) and k[0] in self.arenas:
                for d in self.fence[k[0]].values():
                    add(d)
        for k in reads:
            w = self.last_w.get(k)
            if w is not None:
                add(w)
            if isinstance(k, tuple) and k[0] == "pb":
                for r in self.readers.get(k, {}).values():
                    if r.eng != op.eng:
                        add(r)
        for k in writes:
            w = self.last_w.get(k)
            if w is not None:
                add(w)
            for r in self.readers.get(k, {}).values():
                add(r)
        op.deps = list(deps.values())
        for k in reads:
            self._add(self.readers.setdefault(k, {}), op)
        for k in writes:
            self.last_w[k] = op
            self.readers[k] = {}
        for k in list(reads) + list(writes):
            if isinstance(k, tuple) and k[0] in self.arenas:
                self._add(self.arena_ops[k[0]], op)

    def op(self, eng, fn, reads=(), writes=()):
        o = Op(eng, fn)
        o.lab = self.cur
        self.nops += 1
        o.idx = self.nops
        self._track(o, reads, writes)
        self.ops[eng].append(o)
        return o

    def dma(self, eng, fn, semkey, reads=(), writes=()):
        o = Op(eng, fn, True)
        o.lab = self.cur
        self.nops += 1
        o.idx = self.nops
        o.sem = semkey
        self._track(o, reads, writes)
        prev = self.dma_last.get(semkey)
        if prev is not None and all(d is not prev for d in o.deps):
            o.deps.append(prev)
        self.dma_cnt[semkey] = self.dma_cnt.get(semkey, 0) + 16
        o.semval = self.dma_cnt[semkey]
        self.dma_last[semkey] = o
        self.ops[eng].append(o)
        return o

    def handoff(self, arena):
        f = dict(self.fence[arena])
        for op in self.arena_ops[arena].values():
            self._add(f, op)
        self.fence[arena] = f
        self.arena_ops[arena] = {}
        for tab in (self.last_w, self.readers):
            for k in [k for k in tab if isinstance(k, tuple) and k[0] == arena]:
                del tab[k]

    def emit(self, nc, final_ops):
        for e in ENGS:
            for o in self.ops[e]:
                for d in o.deps:
                    if not d.is_dma:
                        d.signal = True
        for o in final_ops:
            if not o.is_dma:
                o.signal = True
        for e in ENGS:
            c = 0
            for o in self.ops[e]:
                if not o.is_dma and o.signal:
                    c += 1
                    o.cnt = c
        keys = sorted(self.dma_cnt.keys(), key=str)
        with contextlib.ExitStack() as st:
            esem = {e: st.enter_context(nc.semaphore("s_" + e)) for e in ENGS}
            dsem = {k: st.enter_context(nc.semaphore("d%d" % i)) for i, k in enumerate(keys)}
            block = st.enter_context(nc.Block())

            def mk(e):
                def body(engine):
                    waited = {}

                    def wait(d):
                        if d.is_dma:
                            key, val, sem = ("d", d.sem), d.semval, dsem[d.sem]
                        else:
                            key, val, sem = ("e", d.eng), d.cnt, esem[d.eng]
                        if waited.get(key, 0) >= val:
                            return
                        engine.wait_ge(sem, val)
                        waited[key] = val

                    for o in self.ops[e]:
                        for d in o.deps:
                            wait(d)
                        ins = o.fn(engine)
                        if o.is_dma:
                            ins.then_inc(dsem[o.sem], 16)
                        elif o.signal:
                            ins.then_inc(esem[e], 1)
                    if e == "sp":
                        for o in final_ops:
                            wait(o)
                return body

            block.tensor(mk("pe"))
            block.scalar(mk("act"))
            block.vector(mk("dve"))
            block.gpsimd(mk("pool"))
            block.sync(mk("sp"))


class WStream:
    def __init__(self, P, slots, plan):
        self.P, self.slots, self.plan = P, slots, plan
        self.rec = []
        self.nreg = 0
        self.i = 0

    def _register(self, j):
        s = j % NSLOT
        for di, (dstfn, src) in enumerate(self.plan[j]):
            dst = dstfn(self.slots[s])
            self.P.dma("pool", lambda e, dst=dst, src=src: e.dma_start(out=dst, in_=src),
                       ("w", s, di), writes=[("ws", s, di)])

    def get(self, dmas):
        j = self.i
        self.i += 1
        self.rec.append(dmas)
        if self.plan is not None:
            lim = min(j + NSLOT - 2, len(self.plan) - 1)
            while self.nreg <= lim:
                self._register(self.nreg)
                self.nreg += 1
        else:
            s = j % NSLOT
            for di, (dstfn, src) in enumerate(dmas):
                dst = dstfn(self.slots[s])
                self.P.dma("pool", lambda e, dst=dst, src=src: e.dma_start(out=dst, in_=src),
                           ("w", s, di), writes=[("ws", s, di)])
        s = j % NSLOT
        return self.slots[s], [("ws", s, di) for di in range(len(dmas))]


def kx_view(cols, off=0, width=None):
    def f(slot, cols=cols, off=off, width=width):
        W = width
        v = slot[:, 0:8 * W].rearrange("p (k c) -> p k c", k=8)
        return v[:, :, off:off + cols]
    return f


def build_program(nc, T, plan):
    P = Prog(arenas=("BIG", "Y"))
    A = T["alloc"]
    xT, hT, slots, BIGt, Yt = A["xT"], A["hT"], A["slots"], A["BIG"], A["Y"]
    identf, identb, onesm, epsT, vecT, wdwT, rcnt = (A[k] for k in ("identf", "identb", "onesm", "epsT", "vecT", "wdwT", "rcnt"))
    memT, poolst, convst, io, psall, vstage, wstage = (A[k] for k in ("memT", "poolst", "convst", "io", "psall", "vstage", "wstage"))
    WS = WStream(P, slots, plan)
    finals = []

    st = {"b": 0, "io": 0, "act": 0}

    reserved = set()

    def bank(n=1):
        b = st["b"]
        while True:
            if n == 2 and b % 2:
                b += 1
            if b + n > 8:
                b = 0
            if all((b + i) not in reserved for i in range(n)):
                break
            b += 1
        st["b"] = (b + n) % 8
        return b

    def pb(b, n=1):
        return psall[:, b * 512:(b + n) * 512]

    def pbk(b, n=1):
        return [("pb", b + i) for i in range(n)]

    def ioslot():
        s = st["io"]
        st["io"] = 1 - s
        return io[s], ("io", s)

    class Arena:
        def __init__(self, name, tile, nbytes):
            self.name, self.tile, self.nbytes, self.off = name, tile, nbytes, 0

        def reset(self, keep=0):
            P.handoff(self.name)
            self.off = keep

        def alloc(self, shape, dt):
            esz = 4 if dt == F32 else 2
            n = int(np.prod(shape))
            nb = (n * esz + 31) // 32 * 32
            assert self.off + nb <= self.nbytes, (self.name, self.off, nb, self.nbytes)
            e0 = self.off // 2
            v = self.tile[:, e0:e0 + nb // 2]
            self.off += nb
            if dt == F32:
                v = v.bitcast(F32)
            v = v[:, 0:n]
            if len(shape) == 2:
                v = v.rearrange("p (a b) -> p a b", a=shape[0])
            elif len(shape) == 3:
                v = v.rearrange("p (a b c) -> p a b c", a=shape[0], b=shape[1])
            return v

    BIG = Arena("BIG", BIGt, A["BIG_bytes"])
    Y = Arena("Y", Yt, A["Y_bytes"])

    def evac_eng():
        st["act"] ^= 1
        return "act" if st["act"] else "dve"

    def copy_op(eng, out, in_, reads, writes):
        if eng == "act":
            return P.op("act", lambda e: e.copy(out=out, in_=in_), reads=reads, writes=writes)
        return P.op(eng, lambda e: e.tensor_copy(out=out, in_=in_), reads=reads, writes=writes)

    P.op("pool", lambda e: e.memset(identf[:], 0.0), writes=["identf"])
    P.op("pool", lambda e: e.affine_select(out=identf[:], in_=identf[:], pattern=[[-1, 128]], compare_op=ALU.not_equal,
                                           fill=1.0, base=0, channel_multiplier=1), reads=["identf"], writes=["identf"])
    P.op("dve", lambda e: e.tensor_copy(out=identb[:], in_=identf[:]), reads=["identf"], writes=["identb"])
    P.op("pool", lambda e: e.memset(onesm[:], 1.0 / 1024.0), writes=["onesm"])
    P.op("pool", lambda e: e.memset(epsT[:], EPS), writes=["epsT"])
    P.op("pool", lambda e: e.memset(poolst[:], 0.0), writes=["poolst"])
    P.op("pool", lambda e: e.memset(convst[:], 0.0), writes=["convst"])
    for g in range(4):
        w = 2 << g
        P.op("pool", lambda e, g=g, w=w: e.memset(rcnt[:, g, :], 1.0 / w), reads=["rcnt"], writes=["rcnt"])
        for t in range(w - 1):
            P.op("pool", lambda e, g=g, t=t: e.memset(rcnt[:, g, t:t + 1], 1.0 / (t + 1)), reads=["rcnt"], writes=["rcnt"])
    for i, nm in enumerate(VROWS):
        P.dma("sp", lambda e, i=i, nm=nm: e.dma_start(out=vstage[i * 4:(i + 1) * 4, :], in_=T[nm]), ("vs", i), writes=[("io", 0)])
    P.dma("sp", lambda e: e.dma_start(out=vstage[32:33, :], in_=T["final_norm"]), ("vs", 8), writes=[("io", 0)])
    b = bank()
    for k in range(8):
        P.op("pe", lambda e, k=k, b=b: e.transpose(pb(b)[:, k * 64:k * 64 + 33], vstage[0:33, k * 128:(k + 1) * 128], identf[0:33, 0:33]),
             reads=[("io", 0), "identf"], writes=pbk(b))
    P.op("dve", lambda e, b=b: e.tensor_copy(out=vecT[:], in_=pb(b).rearrange("p (k f) -> p k f", k=8)[:, :, 0:33]), reads=pbk(b), writes=["vecT"])
    P.dma("sp", lambda e: e.dma_start(out=wstage[0:124, :], in_=T["w_dw"].rearrange("l j c -> (l j) c")), ("vs", 9), writes=[("io", 1)])
    for hlf in range(2):
        b = bank()
        for kk in range(4):
            k = hlf * 4 + kk
            P.op("pe", lambda e, k=k, kk=kk, b=b: e.transpose(pb(b)[:, kk * 124:(kk + 1) * 124], wstage[0:124, k * 128:(k + 1) * 128], identf[0:124, 0:124]),
                 reads=[("io", 1), "identf"], writes=pbk(b))
        P.op("dve", lambda e, b=b, hlf=hlf: e.tensor_copy(out=wdwT[:, hlf * 4:(hlf + 1) * 4, :], in_=pb(b)[:, 0:496].rearrange("p (k f) -> p k f", k=4)),
             reads=pbk(b), writes=["wdwT"])
    for mt in range(2):
        t_io, kio = ioslot()
        P.dma("sp", lambda e, mt=mt, t_io=t_io: e.dma_start(out=t_io[:, :], in_=T["mem"][mt * 128:(mt + 1) * 128, :]), kio, writes=[kio])
        for hlf in range(2):
            b = bank()
            for kk in range(4):
                k = hlf * 4 + kk
                P.op("pe", lambda e, k=k, kk=kk, b=b, t_io=t_io: e.transpose(pb(b)[:, kk * 128:(kk + 1) * 128], t_io[:, k * 128:(k + 1) * 128], identf[:]),
                     reads=[kio, "identf"], writes=pbk(b))
            P.op("dve", lambda e, b=b, hlf=hlf, mt=mt: e.tensor_copy(out=memT[:, hlf * 4:(hlf + 1) * 4, mt * 128:(mt + 1) * 128],
                                                                    in_=pb(b).rearrange("p (k f) -> p k f", k=4)), reads=pbk(b), writes=["memT"])

    def vcol(name, l, k):
        r = 32 if name == "final_norm" else VROWS.index(name) * 4 + l
        return vecT[:, k, r:r + 1]

    def rmsnorm(src_fn, src_keys_fn, blks, gname, l, dst_fn, dst_keys_fn, tmp):
        for (t0, n) in blks:
            b = bank()
            for k in range(8):
                sq = tmp["sq"][k % 2]
                sk = ("Y", "sq", k % 2)
                P.op("act", lambda e, k=k, sq=sq, t0=t0, n=n: e.activation(out=sq[:, 0:n], in_=src_fn(k, t0, n), func=AF.Square),
                     reads=src_keys_fn(k, t0), writes=[sk])
                P.op("pe", lambda e, k=k, sq=sq, n=n, b=b: e.matmul(pb(b)[:, 0:n], lhsT=onesm[:], rhs=sq[:, 0:n], start=(k == 0), stop=(k == 7)),
                     reads=[sk, "onesm"], writes=pbk(b))
            rt, rstd = tmp["rt"], tmp["rstd"]
            P.op("act", lambda e, n=n, b=b, rt=rt: e.activation(out=rt[:, 0:n], in_=pb(b)[:, 0:n], func=AF.Sqrt, bias=epsT[:, 0:1], scale=1.0),
                 reads=pbk(b) + ["epsT"], writes=[("Y", "rt")])
            P.op("dve", lambda e, n=n, rt=rt, rstd=rstd: e.reciprocal(out=rstd[:, 0:n], in_=rt[:, 0:n]), reads=[("Y", "rt")], writes=[("Y", "rstd")])
            for k in range(8):
                P.op("dve", lambda e, k=k, t0=t0, n=n, rstd=rstd: e.scalar_tensor_tensor(
                    out=dst_fn(k, t0, n), in0=src_fn(k, t0, n), scalar=vcol(gname, l, k), in1=rstd[:, 0:n], op0=ALU.mult, op1=ALU.mult),
                    reads=src_keys_fn(k, t0) + [("Y", "rstd"), "vecT"], writes=dst_keys_fn(k, t0))

    def norm_tmp():
        return {"sq": [Y.alloc([512], BF16), Y.alloc([512], BF16)], "rt": Y.alloc([512], F32), "rstd": Y.alloc([512], F32)}

    xk = lambda k, t0: [("x", k, t0)]
    hk = lambda k, t0: [("h", k, t0)]
    x_fn = lambda k, t0, n: xT[:, k, t0:t0 + n]
    h_fn = lambda k, t0, n: hT[:, k, t0:t0 + n]

    def proj_kx(slot, skeys, off, blk, rhs_fn, rhs_keys_fn, nk=8, W=512):
        t0, n = blk
        b = bank()
        v = slot[:, 0:nk * W].rearrange("p (k c) -> p k c", k=nk)
        for k in range(nk):
            P.op("pe", lambda e, k=k, b=b, v=v: e.matmul(pb(b)[:, 0:n], lhsT=v[:, k, off:off + 128], rhs=rhs_fn(k, t0, n),
                                                        start=(k == 0), stop=(k == nk - 1)),
                 reads=skeys + rhs_keys_fn(k, t0), writes=pbk(b))
        return b

    def ffn(l, which):
        wgu, wdn = T["ffn%d_w_gu" % which][l], T["ffn%d_w_down" % which][l]
        P.cur = "ffn%d" % which
        Y.reset()
        tmp = norm_tmp()
        rmsnorm(x_fn, xk, TBLK, "ffn%d_norm" % which, l, h_fn, hk, tmp)
        sg = [Y.alloc([512], BF16), Y.alloc([512], BF16)]
        BIG.reset()
        hid = BIG.alloc([12, NTOK], BF16)
        for (f0, nf) in ((0, 12), (12, 10)):
            for pi in range(nf // 2):
                fa = f0 + 2 * pi
                slot, sk = WS.get([(kx_view(256, 0, 512), wgu[:, fa * 128:fa * 128 + 256].rearrange("(k p) c -> p k c", p=128)),
                                   (kx_view(256, 256, 512), wgu[:, DFF + fa * 128:DFF + fa * 128 + 256].rearrange("(k p) c -> p k c", p=128))])
                for j in range(2):
                    fi = 2 * pi + j
                    for bi, blk in enumerate(TBLK):
                        t0, n = blk
                        bg = proj_kx(slot, sk, j * 128, blk, h_fn, hk)
                        bu = proj_kx(slot, sk, 256 + j * 128, blk, h_fn, hk)
                        s = sg[(fi * 3 + bi) % 2]
                        skk = ("Y", "sg", (fi * 3 + bi) % 2)
                        P.op("act", lambda e, bg=bg, n=n, s=s: e.activation(out=s[:, 0:n], in_=pb(bg)[:, 0:n], func=AF.Silu), reads=pbk(bg), writes=[skk])
                        P.op("dve", lambda e, bu=bu, n=n, s=s, fi=fi, t0=t0: e.tensor_tensor(out=hid[:, fi, t0:t0 + n], in0=pb(bu)[:, 0:n], in1=s[:, 0:n], op=ALU.mult),
                             reads=pbk(bu) + [skk], writes=[("BIG", "hid", fi, t0)])
            for q in range(4):
                slot, sk = WS.get([(lambda sl, nf=nf: sl[:, 0:nf * 256].rearrange("p (f c) -> p f c", f=nf),
                                    wdn[f0 * 128:(f0 + nf) * 128, q * 256:(q + 1) * 256].rearrange("(f p) c -> p f c", p=128))])
                v = slot[:, 0:nf * 256].rearrange("p (f c) -> p f c", f=nf)
                for cc in range(2):
                    c = 2 * q + cc
                    for (t0, n) in TBLK:
                        b = bank()
                        for fi in range(nf):
                            P.op("pe", lambda e, fi=fi, b=b, v=v, cc=cc, t0=t0, n=n: e.matmul(pb(b)[:, 0:n], lhsT=v[:, fi, cc * 128:(cc + 1) * 128], rhs=hid[:, fi, t0:t0 + n],
                                                                                            start=(fi == 0), stop=(fi == nf - 1)),
                                 reads=sk + [("BIG", "hid", fi, t0)], writes=pbk(b))
                        P.op("dve", lambda e, b=b, c=c, t0=t0, n=n: e.scalar_tensor_tensor(out=xT[:, c, t0:t0 + n], in0=pb(b)[:, 0:n], scalar=0.5, in1=xT[:, c, t0:t0 + n],
                                                                                           op0=ALU.mult, op1=ALU.add),
                             reads=pbk(b) + xk(c, t0), writes=xk(c, t0))

    def transpose_out(src_fn, src_keys, nrow, dst_fn_list):
        b = bank(2)
        for k in range(8):
            P.op("pe", lambda e, k=k, b=b: e.transpose(pb(b, 2)[0:nrow, k * 128:(k + 1) * 128], src_fn(k), identf[:]),
                 reads=src_keys + ["identf"], writes=pbk(b, 2))
        t_io, kio = ioslot()
        copy_op(evac_eng(), t_io[0:nrow, :], pb(b, 2)[0:nrow, :], pbk(b, 2), [kio])
        outs = []
        for (r0, nr, dst, semk) in dst_fn_list:
            outs.append(P.dma("sp", lambda e, r0=r0, nr=nr, dst=dst, t_io=t_io: e.dma_start(out=dst, in_=t_io[r0:r0 + nr, :]), semk, reads=[kio]))
        finals.extend(outs)

    def load_rows_T(src_ap, nrow, dst_fn, dst_keys):
        t_io, kio = ioslot()
        P.dma("sp", lambda e, t_io=t_io: e.dma_start(out=t_io[0:nrow, :], in_=src_ap), kio, writes=[kio])
        for hlf in range(2):
            b = bank()
            for kk in range(4):
                k = hlf * 4 + kk
                P.op("pe", lambda e, k=k, kk=kk, b=b, t_io=t_io: e.transpose(pb(b)[:, kk * nrow:(kk + 1) * nrow], t_io[0:nrow, k * 128:(k + 1) * 128], identf[0:nrow, 0:nrow]),
                     reads=[kio, "identf"], writes=pbk(b))
            copy_op(evac_eng(), dst_fn(hlf), pb(b)[:, 0:4 * nrow].rearrange("p (k f) -> p k f", k=4), pbk(b), dst_keys)

    def mixer(l, p):
        last = (p == NPASS - 1)
        win = T["w_in"][l]
        s0 = p * NS
        P.cur = "mixnorm"
        Y.reset()
        tmp = norm_tmp()
        rmsnorm(x_fn, xk, TBLK, "mix_norm", l, h_fn, hk, tmp)
        BIG.reset()
        merged = BIG.alloc([8, NTOK], F32)
        mk = lambda c, t0: [("BIG", "m", c, t0)]

        def win_piece(c0a, c0b):
            return WS.get([(kx_view(256, 0, 512), win[:, c0a:c0a + 256].rearrange("(k p) c -> p k c", p=128)),
                           (kx_view(256, 256, 512), win[:, c0b:c0b + 256].rearrange("(k p) c -> p k c", p=128))])

        P.cur = "A"
        Y.reset()
        spoolT = Y.alloc([8, NS * PB], F32)
        ustage = Y.alloc([8, NST], F32)
        LA = PL + NS * PW
        uext = Y.alloc([LA], F32)
        sAB = [Y.alloc([LA], F32), Y.alloc([LA], F32)]
        pooled2 = [Y.alloc([2, NTOK], BF16), Y.alloc([2, NTOK], BF16)]
        sigA2 = [[Y.alloc([NTOK], BF16), Y.alloc([NTOK], BF16)] for _ in range(2)]
        t15 = Y.alloc([16], F32)
        load_rows_T(T["spool"][l, s0:s0 + NS].rearrange("s r c -> (s r) c"), NS * PB,
                    lambda hlf: spoolT[:, hlf * 4:(hlf + 1) * 4, :], [("Y", "spoolT")])
        finals.append(P.dma("sp", lambda e: e.dma_start(out=T["o_pool_s"][l, s0:s0 + NS, 0:PB - ST, :], in_=T["spool"][l, s0:s0 + NS, ST:PB, :]), ("dd", 0)))
        def stagePA(g):
            w = 2 << g
            pooled, sigA = pooled2[g % 2], sigA2[g % 2]
            slot, sk = win_piece(g * 256, GL + g * 256)
            for j in range(2):
                c = 2 * g + j
                for (t0, n) in TBLK:
                    b = proj_kx(slot, sk, j * 128, (t0, n), h_fn, hk)
                    if t0 < TP:
                        copy_op("act", uext[:, PB + t0:PB + t0 + n], pb(b)[:, 0:n], pbk(b), [("Y", "uext")])
                    else:
                        copy_op("act", uext[:, PL:LA].rearrange("p (s w) -> p s w", w=PW)[:, :, PB:PW],
                                pb(b)[:, 0:NST].rearrange("p (s t) -> p s t", t=ST), pbk(b), [("Y", "uext")])
                P.op("pool", lambda e, c=c: e.tensor_copy(out=uext[:, 0:PB], in_=poolst[:, l, c, :]), reads=[("pst", l, c)], writes=[("Y", "uext")])
                P.op("pool", lambda e, c=c: e.tensor_copy(out=uext[:, PL:LA].rearrange("p (s w) -> p s w", w=PW)[:, :, 0:PB],
                                                          in_=spoolT[:, c, :].rearrange("p (s r) -> p s r", r=PB)), reads=[("Y", "spoolT")], writes=[("Y", "uext")])
                P.op("pool", lambda e, c=c: e.tensor_copy(out=poolst[:, l, c, :], in_=uext[:, TP:TP + PB]), reads=[("Y", "uext")], writes=[("pst", l, c)])
                P.op("pool", lambda e, c=c: e.tensor_copy(out=ustage[:, c, :].rearrange("p (s t) -> p s t", t=ST),
                                                          in_=uext[:, PL:LA].rearrange("p (s w) -> p s w", w=PW)[:, :, PB:PW]), reads=[("Y", "uext")], writes=[("Y", "ustage")])
                src, srck = uext, ("Y", "uext")
                for stp in range(g + 1):
                    sh = 1 << stp
                    dstt, dk = sAB[stp % 2], ("Y", "sAB", stp % 2)
                    P.op("dve", lambda e, sh=sh, src=src, dstt=dstt: e.tensor_tensor(out=dstt[:, sh:LA], in0=src[:, sh:LA], in1=src[:, 0:LA - sh], op=ALU.add),
                         reads=[srck], writes=[dk])
                    src, srck = dstt, dk
                pk = ("Y", "pooled", g % 2, j)
                P.op("dve", lambda e, src=src, j=j, w=w: e.scalar_tensor_tensor(out=pooled[:, j, 0:TP], in0=src[:, PB:PL], scalar=1.0 / w, in1=uext[:, PB:PL],
                                                                               op0=ALU.mult, op1=ALU.subtract), reads=[srck, ("Y", "uext")], writes=[pk])
                if p == 0:
                    P.op("dve", lambda e, src=src, g=g: e.tensor_tensor(out=t15[:, 0:PB], in0=src[:, PB:2 * PB], in1=rcnt[:, g, 0:PB], op=ALU.mult),
                         reads=[srck, "rcnt"], writes=[("Y", "t15")])
                    P.op("dve", lambda e, j=j: e.tensor_tensor(out=pooled[:, j, 0:PB], in0=t15[:, 0:PB], in1=uext[:, PB:2 * PB], op=ALU.subtract),
                         reads=[("Y", "t15"), ("Y", "uext"), pk], writes=[pk])
                P.op("dve", lambda e, src=src, j=j, w=w: e.scalar_tensor_tensor(
                    out=pooled[:, j, TP:NTOK].rearrange("p (s t) -> p s t", t=ST),
                    in0=src[:, PL:LA].rearrange("p (s w) -> p s w", w=PW)[:, :, PB:PW], scalar=1.0 / w,
                    in1=uext[:, PL:LA].rearrange("p (s w) -> p s w", w=PW)[:, :, PB:PW], op0=ALU.mult, op1=ALU.subtract),
                    reads=[srck, ("Y", "uext"), pk], writes=[pk])
            for j in range(2):
                for (t0, n) in TBLK:
                    b = proj_kx(slot, sk, 256 + j * 128, (t0, n), h_fn, hk)
                    P.op("act", lambda e, b=b, j=j, t0=t0, n=n: e.activation(out=sigA[j][:, t0:t0 + n], in_=pb(b)[:, 0:n], func=AF.Sigmoid),
                         reads=pbk(b), writes=[("Y", "sigA", g % 2, j, t0)])
        def stageVA(g):
            pooled, sigA = pooled2[g % 2], sigA2[g % 2]
            wps, wpk = WS.get([(lambda sl: sl[:, 0:512].rearrange("p (k c) -> p k c", k=2),
                                T["w_pool"][l, g].rearrange("(k p) c -> p k c", p=128))])
            wpv = wps[:, 0:512].rearrange("p (k c) -> p k c", k=2)
            for j2 in range(2):
                c = 2 * g + j2
                for (t0, n) in TBLK:
                    b = bank()
                    for kk in range(2):
                        P.op("pe", lambda e, kk=kk, b=b, wpv=wpv, j2=j2, t0=t0, n=n: e.matmul(pb(b)[:, 0:n], lhsT=wpv[:, kk, j2 * 128:(j2 + 1) * 128],
                                                                                        rhs=pooled[:, kk, t0:t0 + n], start=(kk == 0), stop=(kk == 1)),
                             reads=wpk + [("Y", "pooled", g % 2, kk)], writes=pbk(b))
                    P.op("dve", lambda e, b=b, c=c, j2=j2, t0=t0, n=n: e.scalar_tensor_tensor(out=merged[:, c, t0:t0 + n], in0=pb(b)[:, 0:n], scalar=vcol("pool_scale", l, c),
                                                                                             in1=sigA[j2][:, t0:t0 + n], op0=ALU.mult, op1=ALU.mult),
                         reads=pbk(b) + [("Y", "sigA", g % 2, j2, t0), "vecT"], writes=mk(c, t0))
        stagePA(0)
        for g in range(4):
            if g + 1 < 4:
                stagePA(g + 1)
            stageVA(g)
        transpose_out(lambda k: ustage[:, k, :], [("Y", "ustage")], NST,
                      [(s * ST, ST, T["o_pool_s"][l, s0 + s, PB - ST:PB, :], ("ops", s)) for s in range(NS)])
        if last:
            transpose_out(lambda k: poolst[:, l, k, :], [("pst", l, k) for k in range(8)], PB, [(0, PB, T["o_pool_p"][l], ("opp", 0))])

        P.cur = "B"
        Y.reset()
        yb = Y.alloc([8, NTOK], BF16)
        keepB = Y.off
        zstage = Y.alloc([8, NST], F32)
        sconvT = Y.alloc([8, NS * CB], F32)
        LB = CL + NS * CW
        zc = [Y.alloc([LB], BF16), Y.alloc([LB], BF16)]
        diag2 = [Y.alloc([31, 128], BF16), Y.alloc([31, 128], BF16)]
        sz = Y.alloc([NTOK], BF16)
        for hf in range(2):
            load_rows_T(T["sconv"][l, s0 + hf * 4:s0 + hf * 4 + 4].rearrange("s r c -> (s r) c"), 4 * CB,
                        lambda hlf, hf=hf: sconvT[:, hlf * 4:(hlf + 1) * 4, hf * 4 * CB:(hf + 1) * 4 * CB], [("Y", "sconvT")])
        finals.append(P.dma("sp", lambda e: e.dma_start(out=T["o_conv_s"][l, s0:s0 + NS, 0:CB - ST, :], in_=T["sconv"][l, s0:s0 + NS, ST:CB, :]), ("dd", 1)))
        pieceB = {}

        def stagePB(c):
            q, j = c // 2, c % 2
            if j == 0:
                pieceB["s"] = win_piece(1024 + q * 256, 2048 + q * 256)
            slot, sk = pieceB["s"]
            zt, zk = zc[c % 2], ("Y", "zc", c % 2)
            diag, dgk = diag2[c % 2], ("Y", "diag", c % 2)
            for (t0, n) in TBLK:
                bgt = proj_kx(slot, sk, 256 + j * 128, (t0, n), h_fn, hk)
                P.op("act", lambda e, bgt=bgt, t0=t0, n=n: e.activation(out=sz[:, t0:t0 + n], in_=pb(bgt)[:, 0:n], func=AF.Sigmoid), reads=pbk(bgt), writes=[("Y", "sz", t0)])
                ba = proj_kx(slot, sk, j * 128, (t0, n), h_fn, hk)
                if t0 < TP:
                    P.op("dve", lambda e, ba=ba, zt=zt, t0=t0, n=n: e.tensor_tensor(out=zt[:, CB + t0:CB + t0 + n], in0=pb(ba)[:, 0:n], in1=sz[:, t0:t0 + n], op=ALU.mult),
                         reads=pbk(ba) + [("Y", "sz", t0)], writes=[zk])
                    if t0 + n == TP:
                        P.op("pool", lambda e, zt=zt, c=c: e.tensor_copy(out=zt[:, 0:CB], in_=convst[:, l, c, :]), reads=[("cst", l, c)], writes=[zk])
                        P.op("dve", lambda e, ba=ba, c=c, n=n: e.tensor_tensor(out=convst[:, l, c, :], in0=pb(ba)[:, n - CB:n], in1=sz[:, TP - CB:TP], op=ALU.mult),
                             reads=pbk(ba) + [("Y", "sz", t0)], writes=[("cst", l, c)])
                else:
                    P.op("dve", lambda e, ba=ba, zt=zt: e.tensor_tensor(out=zt[:, CL:LB].rearrange("p (s w) -> p s w", w=CW)[:, :, CB:CW],
                                                                      in0=pb(ba)[:, 0:NST].rearrange("p (s t) -> p s t", t=ST),
                                                                      in1=sz[:, TP:NTOK].rearrange("p (s t) -> p s t", t=ST), op=ALU.mult),
                         reads=pbk(ba) + [("Y", "sz", t0)], writes=[zk])
                    P.op("dve", lambda e, ba=ba, c=c: e.tensor_tensor(out=zstage[:, c, :], in0=pb(ba)[:, 0:NST], in1=sz[:, TP:NTOK], op=ALU.mult),
                         reads=pbk(ba) + [("Y", "sz", t0)], writes=[("Y", "zstage")])
            P.op("pool", lambda e, zt=zt, c=c: e.tensor_copy(out=zt[:, CL:LB].rearrange("p (s w) -> p s w", w=CW)[:, :, 0:CB],
                                                             in_=sconvT[:, c, :].rearrange("p (s r) -> p s r", r=CB)), reads=[("Y", "sconvT")], writes=[zk])
            P.op("pool", lambda e, c=c: e.tensor_tensor(out=diag[:, :, :], in0=bass.AP(identb, 0, [[128, 128], [0, 31], [1, 128]]),
                                                        in1=bass.AP(wdwT, c * 124 + l * 31, [[992, 128], [1, 31], [0, 128]]), op=ALU.mult),
                 reads=["identb", "wdwT"], writes=[dgk])
        def stageVB(c):
            zt, zk = zc[c % 2], ("Y", "zc", c % 2)
            diag, dgk = diag2[c % 2], ("Y", "diag", c % 2)
            for (t0, n) in TBLK:
                b = bank()
                for jj in range(31):
                    if t0 < TP:
                        rhs_fn = lambda jj=jj, zt=zt, t0=t0, n=n: zt[:, t0 + jj:t0 + jj + n]
                    else:
                        rhs_fn = lambda jj=jj, zt=zt: zt[:, CL:LB].rearrange("p (s w) -> p s w", w=CW)[:, :, jj:jj + ST]
                    P.op("pe", lambda e, jj=jj, b=b, n=n, rhs_fn=rhs_fn: e.matmul(pb(b)[:, 0:n], lhsT=diag[:, jj, :], rhs=rhs_fn(), start=(jj == 0), stop=(jj == 30)),
                         reads=[zk, dgk], writes=pbk(b))
                P.op("act", lambda e, b=b, c=c, t0=t0, n=n: e.activation(out=yb[:, c, t0:t0 + n], in_=pb(b)[:, 0:n], func=AF.Identity, bias=vcol("b_dw", l, c), scale=1.0),
                     reads=pbk(b) + ["vecT"], writes=[("Y", "yb", c, t0)])
        stagePB(0)
        for c in range(8):
            if c + 1 < 8:
                stagePB(c + 1)
            stageVB(c)
        transpose_out(lambda k: zstage[:, k, :], [("Y", "zstage")], NST,
                      [(s * ST, ST, T["o_conv_s"][l, s0 + s, CB - ST:CB, :], ("ocs", s)) for s in range(NS)])
        if last:
            transpose_out(lambda k: convst[:, l, k, :], [("cst", l, k) for k in range(8)], CB, [(0, CB, T["o_conv_p"][l], ("ocp", 0))])
        P.cur = "B_ln"
        Y.reset(keepB)
        sq2 = [Y.alloc([512], BF16), Y.alloc([512], BF16)]
        meanb, m2, rt, rstd, nmr = (Y.alloc([512], F32) for _ in range(5))
        t1 = [Y.alloc([512], F32), Y.alloc([512], F32), Y.alloc([512], F32), Y.alloc([512], F32)]
        sigB8 = Y.alloc([8, NTOK], BF16)
        tmpB = t1[0:2]
        yb_fn = lambda k, t0, n: yb[:, k, t0:t0 + n]
        ybk = lambda k, t0: [("Y", "yb", k, t0)]
        statb = []
        for (t0, n) in TBLK:
            bm, be = bank(), bank()
            reserved.update((bm, be))
            statb.append((bm, be))
            for k in range(8):
                sq = sq2[k % 2]
                sqk = ("Y", "sq2", k % 2)
                P.op("act", lambda e, k=k, sq=sq, t0=t0, n=n: e.activation(out=sq[:, 0:n], in_=yb[:, k, t0:t0 + n], func=AF.Square), reads=[("Y", "yb", k, t0)], writes=[sqk])
                P.op("pe", lambda e, k=k, bm=bm, t0=t0, n=n: e.matmul(pb(bm)[:, 0:n], lhsT=onesm[:], rhs=yb[:, k, t0:t0 + n], start=(k == 0), stop=(k == 7)),
                     reads=[("Y", "yb", k, t0), "onesm"], writes=pbk(bm))
                P.op("pe", lambda e, k=k, be=be, sq=sq, n=n: e.matmul(pb(be)[:, 0:n], lhsT=onesm[:], rhs=sq[:, 0:n], start=(k == 0), stop=(k == 7)),
                     reads=[sqk, "onesm"], writes=pbk(be))

        def gate_b(q):
            sg_, sgk = WS.get([(kx_view(512, 0, 512), win[:, GL + 1024 + q * 512:GL + 1024 + (q + 1) * 512].rearrange("(k p) c -> p k c", p=128))])
            for cc in range(4):
                c = 4 * q + cc
                for (t0, n) in TBLK:
                    b = proj_kx(sg_, sgk, cc * 128, (t0, n), h_fn, hk)
                    P.op("act", lambda e, b=b, c=c, t0=t0, n=n: e.activation(out=sigB8[:, c, t0:t0 + n], in_=pb(b)[:, 0:n], func=AF.Sigmoid), reads=pbk(b), writes=[("Y", "sigB", c, t0)])

        gate_b(0)
        for bi, (t0, n) in enumerate(TBLK):
            bm, be = statb[bi]
            P.op("act", lambda e, bm=bm, n=n: e.copy(out=meanb[:, 0:n], in_=pb(bm)[:, 0:n]), reads=pbk(bm), writes=[("Y", "meanb")])
            P.op("dve", lambda e, n=n: e.tensor_tensor(out=m2[:, 0:n], in0=meanb[:, 0:n], in1=meanb[:, 0:n], op=ALU.mult), reads=[("Y", "meanb")], writes=[("Y", "m2")])
            P.op("dve", lambda e, be=be, n=n: e.tensor_tensor(out=m2[:, 0:n], in0=pb(be)[:, 0:n], in1=m2[:, 0:n], op=ALU.subtract), reads=pbk(be) + [("Y", "m2")], writes=[("Y", "m2")])
            reserved.difference_update((bm, be))
            P.op("dve", lambda e, n=n: e.tensor_scalar(out=m2[:, 0:n], in0=m2[:, 0:n], scalar1=0.0, scalar2=0.0, op0=ALU.max, op1=ALU.add), reads=[("Y", "m2")], writes=[("Y", "m2")])
            P.op("act", lambda e, n=n: e.activation(out=rt[:, 0:n], in_=m2[:, 0:n], func=AF.Sqrt, bias=epsT[:, 0:1], scale=1.0), reads=[("Y", "m2"), "epsT"], writes=[("Y", "rt2")])
            P.op("dve", lambda e, n=n: e.reciprocal(out=rstd[:, 0:n], in_=rt[:, 0:n]), reads=[("Y", "rt2")], writes=[("Y", "rstd2")])
            P.op("dve", lambda e, n=n: e.scalar_tensor_tensor(out=nmr[:, 0:n], in0=meanb[:, 0:n], scalar=-1.0, in1=rstd[:, 0:n], op0=ALU.mult, op1=ALU.mult),
                 reads=[("Y", "meanb"), ("Y", "rstd2")], writes=[("Y", "nmr")])
            for k0 in range(0, 8, 2):
                eng = "pool" if (k0 // 2) % 2 == 1 and n > 64 else "dve"
                for k in (k0, k0 + 1):
                    tt, tk = t1[k % 4], ("Y", "t1", k % 4)
                    P.op(eng, lambda e, k=k, tt=tt, t0=t0, n=n: e.tensor_tensor(out=tt[:, 0:n], in0=yb[:, k, t0:t0 + n], in1=rstd[:, 0:n], op=ALU.mult),
                         reads=[("Y", "yb", k, t0), ("Y", "rstd2")], writes=[tk])
                for k in (k0, k0 + 1):
                    tt, tk = t1[k % 4], ("Y", "t1", k % 4)
                    P.op(eng, lambda e, tt=tt, n=n: e.tensor_tensor(out=tt[:, 0:n], in0=tt[:, 0:n], in1=nmr[:, 0:n], op=ALU.add), reads=[tk, ("Y", "nmr")], writes=[tk])
                    P.op("act", lambda e, k=k, tt=tt, t0=t0, n=n: e.activation(out=yb[:, k, t0:t0 + n], in_=tt[:, 0:n], func=AF.Silu, bias=vcol("conv_ln_b", l, k), scale=vcol("conv_ln_g", l, k)),
                         reads=[tk, "vecT"], writes=[("Y", "yb", k, t0)])
        gate_b(1)
        P.cur = "B_pw"
        ti = 0
        for q in range(2):
            sp_, spk = WS.get([(kx_view(512, 0, 512), T["w_pw"][l][:, q * 512:(q + 1) * 512].rearrange("(k p) c -> p k c", p=128))])
            for cc in range(4):
                c = 4 * q + cc
                for (t0, n) in TBLK:
                    b2 = proj_kx(sp_, spk, cc * 128, (t0, n), yb_fn, ybk)
                    tb, tbk = tmpB[ti % 2], ("Y", "t1", ti % 2)
                    ti += 1
                    P.op("dve", lambda e, b2=b2, tb=tb, c=c, t0=t0, n=n: e.tensor_tensor(out=tb[:, 0:n], in0=pb(b2)[:, 0:n], in1=sigB8[:, c, t0:t0 + n], op=ALU.mult),
                         reads=pbk(b2) + [("Y", "sigB", c, t0)], writes=[tbk])
                    P.op("pool", lambda e, tb=tb, c=c, t0=t0, n=n: e.tensor_tensor(out=merged[:, c, t0:t0 + n], in0=merged[:, c, t0:t0 + n], in1=tb[:, 0:n], op=ALU.add),
                         reads=[tbk] + mk(c, t0), writes=mk(c, t0))

        P.cur = "C_kv"
        Y.reset()
        KTp = Y.alloc([8, NMEM], BF16)
        Vp = Y.alloc([2, D], BF16)
        keepC = Y.off
        memn = Y.alloc([8, NMEM], BF16)
        kvst = [Y.alloc([512], F32), Y.alloc([512], F32)]
        tmpn = norm_tmp()
        rmsnorm(lambda k, t0, n: memT[:, k, t0:t0 + n], lambda k, t0: ["memT"], [(0, NMEM)], "mem_norm", l,
                lambda k, t0, n: memn[:, k, t0:t0 + n], lambda k, t0: [("Y", "memn", k)], tmpn)
        mn_fn = lambda k, t0, n: memn[:, k, t0:t0 + n]
        mnk = lambda k, t0: [("Y", "memn", k)]
        wkv = T["w_mem_kv"][l]
        kvi = 0
        for q in range(4):
            slot, sk = WS.get([(kx_view(512, 0, 512), wkv[:, q * 512:(q + 1) * 512].rearrange("(k p) c -> p k c", p=128))])
            v = slot[:, 0:4096].rearrange("p (k c) -> p k c", k=8)
            if q < 2:
                for cc in range(4):
                    c = 4 * q + cc
                    b = proj_kx(slot, sk, cc * 128, (0, NMEM), mn_fn, mnk)
                    copy_op(evac_eng(), KTp[:, c, :], pb(b)[:, 0:NMEM], pbk(b), [("Y", "KTp", c)])
            if q >= 2 or p == 0:
                for mt in range(2):
                    b = bank()
                    for k in range(8):
                        P.op("pe", lambda e, k=k, b=b, mt=mt, v=v: e.matmul(pb(b)[:, :], lhsT=memn[:, k, mt * 128:(mt + 1) * 128], rhs=v[:, k, :], start=(k == 0), stop=(k == 7)),
                             reads=sk + [("Y", "memn", k)], writes=pbk(b))
                    if q >= 2 and p != 0:
                        copy_op("dve", Vp[:, mt, (q - 2) * 512:(q - 1) * 512], pb(b)[:, :], pbk(b), [("Y", "Vp", mt, q - 2)])
                    if p == 0:
                        ks, kk_ = kvst[kvi % 2], ("Y", "kvst", kvi % 2)
                        kvi += 1
                        copy_op("act", ks[:, :], pb(b)[:, :], pbk(b), [kk_])
                        if q >= 2:
                            copy_op("dve", Vp[:, mt, (q - 2) * 512:(q - 1) * 512], ks[:, :], [kk_], [("Y", "Vp", mt, q - 2)])
                        dst = (T["o_mk"] if q < 2 else T["o_mv"])[l, mt * 128:(mt + 1) * 128, (q % 2) * 512:(q % 2 + 1) * 512]
                        finals.append(P.dma("sp", lambda e, dst=dst, ks=ks: e.dma_start(out=dst, in_=ks[:, :]), ("okv", kvi % 2), reads=[kk_]))
        P.cur = "C_prompt"
        Y.reset(keepC)
        qTs = Y.alloc([8, NST], BF16)
        sigCs = Y.alloc([8, NST], BF16)
        qmask = Y.alloc([NS, 8, NST], BF16)
        PTm = Y.alloc([NS, 8, NST], BF16)
        keepC2 = Y.off
        qT2 = [Y.alloc([2, TP], BF16), Y.alloc([2, TP], BF16)]
        sigC2 = [[Y.alloc([TP], BF16), Y.alloc([TP], BF16)] for _ in range(2)]
        PT = Y.alloc([2, TP], BF16)
        Pe = [Y.alloc([NMEM], BF16), Y.alloc([NMEM], BF16)]
        Pn = [Y.alloc([NMEM], BF16), Y.alloc([NMEM], BF16)]
        smx = Y.alloc([4, 8], F32)
        tmpC = [Y.alloc([512], F32), Y.alloc([512], F32)]
        P.op("pool", lambda e: e.memset(qmask[:, :, :, :], 0.0), writes=[("Y", "qmask")])
        P.op("pool", lambda e: e.memset(PTm[:, :, :, :], 0.0), writes=[("Y", "PTm")])
        ci = [0]

        def proj_gen(h):
            slot, sk = win_piece(3072 + h * 256, GL + 2048 + h * 256)
            qT, sigC = qT2[h % 2], sigC2[h % 2]
            for j in range(2):
                for (t0, n) in TBLK:
                    b = proj_kx(slot, sk, j * 128, (t0, n), h_fn, hk)
                    if t0 < TP:
                        P.op("act", lambda e, b=b, j=j, t0=t0, n=n, qT=qT: e.mul(out=qT[:, j, t0:t0 + n], in_=pb(b)[:, 0:n], mul=0.0625), reads=pbk(b), writes=[("Y", "qT", h % 2, j, t0)])
                    else:
                        P.op("act", lambda e, b=b, j=j, h=h: e.mul(out=qTs[:, 2 * h + j, :], in_=pb(b)[:, 0:NST], mul=0.0625), reads=pbk(b), writes=[("Y", "qTs")])
                    yield
                    b = proj_kx(slot, sk, 256 + j * 128, (t0, n), h_fn, hk)
                    if t0 < TP:
                        P.op("act", lambda e, b=b, j=j, t0=t0, n=n, sigC=sigC: e.activation(out=sigC[j][:, t0:t0 + n], in_=pb(b)[:, 0:n], func=AF.Sigmoid), reads=pbk(b), writes=[("Y", "sigC", h % 2, j, t0)])
                    else:
                        P.op("act", lambda e, b=b, j=j, h=h: e.activation(out=sigCs[:, 2 * h + j, :], in_=pb(b)[:, 0:NST], func=AF.Sigmoid), reads=pbk(b), writes=[("Y", "sigCs")])
                    yield

        def pull(gen, k):
            if gen is None:
                return
            for _ in range(k):
                try:
                    next(gen)
                except StopIteration:
                    return

        def scores(h, i):
            qT = qT2[h % 2]
            t0b = (i * 128) // 512 * 512
            b = bank()
            reserved.add(b)
            for dc in range(2):
                P.op("pe", lambda e, dc=dc, b=b, i=i, h=h, qT=qT: e.matmul(pb(b)[:, 0:NMEM], lhsT=qT[:, dc, i * 128:(i + 1) * 128], rhs=KTp[:, 2 * h + dc, :], start=(dc == 0), stop=(dc == 1)),
                     reads=[("Y", "qT", h % 2, dc, t0b), ("Y", "KTp", 2 * h + dc)], writes=pbk(b))
            return b

        for _ in proj_gen(0):
            pass
        for h in range(4):
            sigC = sigC2[h % 2]
            gen = proj_gen(h + 1) if h + 1 < 4 else None
            nt = TP // 128
            sb = {0: scores(h, 0)}
            for i in range(nt):
                b = sb.pop(i)
                pe_, pek = Pe[i % 2], ("Y", "Pe", i % 2)
                pn_, pnk = Pn[i % 2], ("Y", "Pn", i % 2)
                mi = i % 2
                P.op("dve", lambda e, b=b, mi=mi: e.tensor_reduce(out=smx[:, 0, mi:mi + 1], in_=pb(b)[:, 0:NMEM], axis=AX.X, op=ALU.max), reads=pbk(b), writes=[("Y", "mx", mi)])
                P.op("dve", lambda e, mi=mi: e.tensor_scalar(out=smx[:, 1, mi:mi + 1], in0=smx[:, 0, mi:mi + 1], scalar1=-1.0, scalar2=0.0, op0=ALU.mult, op1=ALU.add), reads=[("Y", "mx", mi)], writes=[("Y", "nmx", mi)])
                P.op("act", lambda e, b=b, pe_=pe_, mi=mi: e.activation(out=pe_[:, :], in_=pb(b)[:, 0:NMEM], func=AF.Exp, bias=smx[:, 1, mi:mi + 1], scale=1.0, accum_out=smx[:, 2, mi:mi + 1]),
                     reads=pbk(b) + [("Y", "nmx", mi)], writes=[pek, ("Y", "rs", mi)])
                reserved.discard(b)
                P.op("dve", lambda e, mi=mi: e.reciprocal(out=smx[:, 3, mi:mi + 1], in_=smx[:, 2, mi:mi + 1]), reads=[("Y", "rs", mi)], writes=[("Y", "rr", mi)])
                P.op("dve", lambda e, pe_=pe_, pn_=pn_, mi=mi: e.tensor_scalar(out=pn_[:, :], in0=pe_[:, :], scalar1=smx[:, 3, mi:mi + 1], scalar2=0.0, op0=ALU.mult, op1=ALU.add),
                     reads=[pek, ("Y", "rr", mi)], writes=[pnk])
                if i + 1 < nt:
                    sb[i + 1] = scores(h, i + 1)
                pull(gen, 2 if i % 2 == 0 else 1)
                b2 = bank()
                pst = pb(b2).bitcast(BF16)
                for mc in range(2):
                    P.op("pe", lambda e, mc=mc, pst=pst, pn_=pn_: e.transpose(pst[:, mc * 128:(mc + 1) * 128], pn_[:, mc * 128:(mc + 1) * 128], identb[:]),
                         reads=[pnk, "identb"], writes=pbk(b2))
                copy_op("act", PT[:, :, i * 128:(i + 1) * 128], pst[:, 0:256].rearrange("p (m t) -> p m t", m=2), pbk(b2), [("Y", "PT", i)])
            pull(gen, 100)
            for j in range(2):
                c = 2 * h + j
                for bi, (t0, n) in enumerate(PBLK):
                    b = bank()
                    for mc in range(2):
                        P.op("pe", lambda e, mc=mc, b=b, c=c, t0=t0, n=n: e.matmul(pb(b)[:, 0:n], lhsT=Vp[:, mc, c * 128:(c + 1) * 128], rhs=PT[:, mc, t0:t0 + n], start=(mc == 0), stop=(mc == 1)),
                             reads=[("Y", "Vp", mc, c // 4)] + [("Y", "PT", i) for i in range(t0 // 128, (t0 + n) // 128)], writes=pbk(b))
                    tb, tbk = tmpC[ci[0] % 2], ("Y", "tmpC", ci[0] % 2)
                    ci[0] += 1
                    P.op("dve", lambda e, b=b, tb=tb, j=j, t0=t0, n=n, sigC=sigC: e.tensor_tensor(out=tb[:, 0:n], in0=pb(b)[:, 0:n], in1=sigC[j][:, t0:t0 + n], op=ALU.mult),
                         reads=pbk(b) + [("Y", "sigC", h % 2, j, t0)], writes=[tbk])
                    P.op("pool", lambda e, tb=tb, c=c, t0=t0, n=n: e.tensor_tensor(out=merged[:, c, t0:t0 + n], in0=merged[:, c, t0:t0 + n], in1=tb[:, 0:n], op=ALU.add),
                         reads=[tbk] + mk(c, t0), writes=mk(c, t0))
        P.cur = "C_sample"
        Y.reset(keepC2)
        KTs = [Y.alloc([8, NMEM], BF16), Y.alloc([8, NMEM], BF16)]
        Pes = Y.alloc([4, NMEM], BF16)
        Pns = Y.alloc([4, NMEM], BF16)
        PTs = Y.alloc([8, NST], BF16)
        smy = Y.alloc([4, 8], F32)
        for s in range(NS):
            P.op("pool", lambda e, s=s: e.tensor_copy(out=qmask[:, s, :, s * ST:(s + 1) * ST], in_=qTs[:, :, s * ST:(s + 1) * ST]), reads=[("Y", "qTs"), ("Y", "qmask")], writes=[("Y", "qmask")])
        bs = bank(2)
        reserved.update((bs, bs + 1))
        for s in range(NS):
            slot, sk = WS.get([(lambda sl: sl[:, 0:2048].rearrange("p (m c) -> p m c", m=2), T["ck"][l, s0 + s].rearrange("(m p) c -> p m c", p=128))])
            kv = slot[:, 0:2048].rearrange("p (m c) -> p m c", m=2)
            kt, ktk = KTs[s % 2], ("Y", "KTs", s % 2)
            bt = bank(2)
            pst = pb(bt, 2).bitcast(BF16)
            for ch in range(8):
                for mc in range(2):
                    P.op("pe", lambda e, ch=ch, mc=mc, pst=pst, kv=kv: e.transpose(pst[:, ch * 256 + mc * 128:ch * 256 + (mc + 1) * 128], kv[:, mc, ch * 128:(ch + 1) * 128], identb[:]),
                         reads=sk + ["identb"], writes=pbk(bt, 2))
            copy_op(evac_eng(), kt[:, :, :], pst[:, :].rearrange("p (c m) -> p c m", c=8), pbk(bt, 2), [ktk])
            for h in range(4):
                for dc in range(2):
                    P.op("pe", lambda e, h=h, dc=dc, s=s, kt=kt: e.matmul(pb(bs, 2)[0:NST, h * 256:(h + 1) * 256], lhsT=qmask[:, s, 2 * h + dc, :], rhs=kt[:, 2 * h + dc, :],
                                                                     start=(s == 0 and dc == 0 and h % 2 == 0), stop=(s == NS - 1 and dc == 1)),
                         reads=[("Y", "qmask"), ktk], writes=pbk(bs, 2))
        sc = pb(bs, 2)[0:NST, :].rearrange("p (h m) -> p h m", h=4)
        P.op("dve", lambda e: e.tensor_reduce(out=smy[0:NST, 0, 0:4], in_=sc, axis=AX.X, op=ALU.max), reads=pbk(bs, 2), writes=[("Y", "mxs")])
        P.op("dve", lambda e: e.tensor_scalar(out=smy[0:NST, 1, 0:4], in0=smy[0:NST, 0, 0:4], scalar1=-1.0, scalar2=0.0, op0=ALU.mult, op1=ALU.add), reads=[("Y", "mxs")], writes=[("Y", "nmxs")])
        for h in range(4):
            P.op("act", lambda e, h=h: e.activation(out=Pes[0:NST, h, :], in_=sc[:, h, :], func=AF.Exp, bias=smy[0:NST, 1, h:h + 1], scale=1.0, accum_out=smy[0:NST, 2, h:h + 1]),
                 reads=pbk(bs, 2) + [("Y", "nmxs")], writes=[("Y", "Pes", h), ("Y", "rss", h)])
        P.op("dve", lambda e: e.reciprocal(out=smy[0:NST, 3, 0:4], in_=smy[0:NST, 2, 0:4]), reads=[("Y", "rss", h) for h in range(4)], writes=[("Y", "rrs")])
        for h in range(4):
            P.op("dve", lambda e, h=h: e.tensor_scalar(out=Pns[0:NST, h, :], in0=Pes[0:NST, h, :], scalar1=smy[0:NST, 3, h:h + 1], scalar2=0.0, op0=ALU.mult, op1=ALU.add),
                 reads=[("Y", "Pes", h), ("Y", "rrs")], writes=[("Y", "Pns", h)])
        b2 = bank()
        reserved.difference_update((bs, bs + 1))
        pst = pb(b2).bitcast(BF16)
        for h in range(4):
            for mc in range(2):
                ch = 2 * h + mc
                P.op("pe", lambda e, h=h, mc=mc, ch=ch, pst=pst: e.transpose(pst[:, ch * NST:(ch + 1) * NST], Pns[0:NST, h, mc * 128:(mc + 1) * 128], identb[0:NST, 0:NST]),
                     reads=[("Y", "Pns", h), "identb"], writes=pbk(b2))
        copy_op("dve", PTs[:, :, :], pst[:, 0:8 * NST].rearrange("p (c t) -> p c t", c=8), pbk(b2), [("Y", "PTs")])
        for s in range(NS):
            P.op("pool", lambda e, s=s: e.tensor_copy(out=PTm[:, s, :, s * ST:(s + 1) * ST], in_=PTs[:, :, s * ST:(s + 1) * ST]), reads=[("Y", "PTs"), ("Y", "PTm")], writes=[("Y", "PTm")])
        bo = bank()
        reserved.add(bo)
        for s in range(NS):
            slot, sk = WS.get([(lambda sl: sl[:, 0:2048].rearrange("p (m c) -> p m c", m=2), T["cv"][l, s0 + s].rearrange("(m p) c -> p m c", p=128))])
            vv = slot[:, 0:2048].rearrange("p (m c) -> p m c", m=2)
            for c in range(8):
                for mc in range(2):
                    P.op("pe", lambda e, c=c, mc=mc, s=s, vv=vv: e.matmul(pb(bo)[:, c * NST:(c + 1) * NST], lhsT=vv[:, mc, c * 128:(c + 1) * 128], rhs=PTm[:, s, (c // 2) * 2 + mc, :],
                                                                     start=(s == 0 and mc == 0 and c == 0), stop=(s == NS - 1 and mc == 1)),
                         reads=sk + [("Y", "PTm")], writes=pbk(bo))
        reserved.discard(bo)
        tcs = Y.alloc([8, NST], F32)
        P.op("dve", lambda e: e.tensor_tensor(out=tcs[:, :, :], in0=pb(bo)[:, 0:8 * NST].rearrange("p (c t) -> p c t", c=8), in1=sigCs[:, :, :], op=ALU.mult),
             reads=pbk(bo) + [("Y", "sigCs")], writes=[("Y", "tcs")])
        P.op("pool", lambda e: e.tensor_tensor(out=merged[:, :, TP:NTOK], in0=merged[:, :, TP:NTOK], in1=tcs[:, :, :], op=ALU.add),
             reads=[("Y", "tcs")] + [("BIG", "m", c, TP) for c in range(8)], writes=[("BIG", "m", c, TP) for c in range(8)])

        P.cur = "wout"
        for k in range(8):
            for (t0, n) in TBLK:
                copy_op(evac_eng(), hT[:, k, t0:t0 + n], merged[:, k, t0:t0 + n], mk(k, t0), hk(k, t0))
        for q in range(2):
            slot, sk = WS.get([(kx_view(512, 0, 512), T["w_out"][l][:, q * 512:(q + 1) * 512].rearrange("(k p) c -> p k c", p=128))])
            for cc in range(4):
                c = 4 * q + cc
                for (t0, n) in TBLK:
                    b = proj_kx(slot, sk, cc * 128, (t0, n), h_fn, hk)
                    P.op("dve", lambda e, b=b, c=c, t0=t0, n=n: e.tensor_tensor(out=xT[:, c, t0:t0 + n], in0=pb(b)[:, 0:n], in1=xT[:, c, t0:t0 + n], op=ALU.add),
                         reads=pbk(b) + xk(c, t0), writes=xk(c, t0))

    for p in range(DBG_PASSES):
        P.cur = "xload"
        srcs = [(T["xp"][p * TP + i * 128:p * TP + (i + 1) * 128, :], 128, i * 128) for i in range(TP // 128)]
        srcs.append((T["xs"][p * NST:(p + 1) * NST, :], NST, TP))
        for (src, nr, c0) in srcs:
            t_io, kio = ioslot()
            P.dma("sp", lambda e, t_io=t_io, src=src, nr=nr: e.dma_start(out=t_io[0:nr, :], in_=src), kio, writes=[kio])
            for hlf in range(2):
                b = bank()
                for kk in range(4):
                    k = hlf * 4 + kk
                    P.op("pe", lambda e, k=k, kk=kk, b=b, t_io=t_io, nr=nr: e.transpose(pb(b)[:, kk * nr:(kk + 1) * nr], t_io[0:nr, k * 128:(k + 1) * 128], identf[0:nr, 0:nr]),
                         reads=[kio, "identf"], writes=pbk(b))
                t0b = min(c0 // 512 * 512, TP)
                copy_op(evac_eng(), xT[:, hlf * 4:(hlf + 1) * 4, c0:c0 + nr], pb(b)[:, 0:4 * nr].rearrange("p (k f) -> p k f", k=4),
                        pbk(b) + [("x", k, t0b) for k in range(hlf * 4, hlf * 4 + 4)], [("x", k, t0b) for k in range(hlf * 4, hlf * 4 + 4)])
        for l in range(DBG_LAYERS):
            ffn(l, 1)
            if DBG_STAGE >= 2:
                mixer(l, p)
            if DBG_STAGE >= 3:
                ffn(l, 2)
        P.cur = "final"
        Y.reset()
        tmp = norm_tmp()
        BIG.reset()
        yT = BIG.alloc([8, NTOK], F32)
        rmsnorm(x_fn, xk, TBLK, "final_norm", 0, lambda k, t0, n: yT[:, k, t0:t0 + n], lambda k, t0: [("BIG", "y", k, t0)], tmp)
        for i in range(TP // 128):
            t0b = i * 128 // 512 * 512
            transpose_out(lambda k, i=i: yT[:, k, i * 128:(i + 1) * 128], [("BIG", "y", k, t0b) for k in range(8)], 128,
                          [(0, 128, T["o_yp"][p * TP + i * 128:p * TP + (i + 1) * 128, :], ("oy", i % 2))])
        transpose_out(lambda k: yT[:, k, TP:NTOK], [("BIG", "y", k, TP) for k in range(8)], NST,
                      [(0, NST, T["o_ys"][p * NST:(p + 1) * NST, :], ("oys", 0))])
    return P, WS, finals


def build_nc():
    nc = bass.Bass("TRN2", target_bir_lowering=False)
    T = {}

    def din(name, shape):
        T[name] = nc.dram_tensor(name, list(shape), F32, kind="ExternalInput").ap()

    def dout(name, shape):
        T[name] = nc.dram_tensor(name, list(shape), F32, kind="ExternalOutput").ap()

    din("xp", (SEQ, D)); din("xs", (NSEQ * ST, D)); din("mem", (NMEM, D))
    din("spool", (DEPTH, NSEQ, PB, D)); din("sconv", (DEPTH, NSEQ, CB, D))
    din("ck", (DEPTH, NSEQ, NMEM, D)); din("cv", (DEPTH, NSEQ, NMEM, D))
    for nm in VROWS:
        din(nm, (DEPTH, D))
    din("final_norm", (1, D))
    din("ffn1_w_gu", (DEPTH, D, 2 * DFF)); din("ffn1_w_down", (DEPTH, DFF, D))
    din("ffn2_w_gu", (DEPTH, D, 2 * DFF)); din("ffn2_w_down", (DEPTH, DFF, D))
    din("w_in", (DEPTH, D, INC)); din("w_mem_kv", (DEPTH, D, 2 * D)); din("w_pool", (DEPTH, 4, 256, 256))
    din("w_dw", (DEPTH, 31, D)); din("w_pw", (DEPTH, D, D)); din("w_out", (DEPTH, D, D))
    dout("o_yp", (SEQ, D)); dout("o_ys", (NSEQ * ST, D))
    dout("o_pool_p", (DEPTH, PB, D)); dout("o_conv_p", (DEPTH, CB, D))
    dout("o_mk", (DEPTH, NMEM, D)); dout("o_mv", (DEPTH, NMEM, D))
    dout("o_pool_s", (DEPTH, NSEQ, PB, D)); dout("o_conv_s", (DEPTH, NSEQ, CB, D))

    A = {}
    A["xT"] = nc.alloc_sbuf_tensor("xT", [128, 8, NTOK], F32)
    A["hT"] = nc.alloc_sbuf_tensor("hT", [128, 8, NTOK], BF16)
    A["slots"] = [nc.alloc_sbuf_tensor("ws%d" % i, [128, SLOTE], BF16) for i in range(NSLOT)]
    A["BIG_bytes"] = 8 * NTOK * 4
    A["BIG"] = nc.alloc_sbuf_tensor("BIG", [128, A["BIG_bytes"] // 2], BF16)
    A["Y_bytes"] = int(os.environ.get("DBG_YKB", 54)) * 1024
    A["Y"] = nc.alloc_sbuf_tensor("Yar", [128, A["Y_bytes"] // 2], BF16)
    A["identf"] = nc.alloc_sbuf_tensor("identf", [128, 128], F32)
    A["identb"] = nc.alloc_sbuf_tensor("identb", [128, 128], BF16)
    A["onesm"] = nc.alloc_sbuf_tensor("onesm", [128, 128], BF16)
    A["epsT"] = nc.alloc_sbuf_tensor("epsT", [128, 1], F32)
    A["vecT"] = nc.alloc_sbuf_tensor("vecT", [128, 8, 33], F32)
    A["wdwT"] = nc.alloc_sbuf_tensor("wdwT", [128, 8, 124], F32)
    A["rcnt"] = nc.alloc_sbuf_tensor("rcnt", [128, 4, 16], F32)
    A["memT"] = nc.alloc_sbuf_tensor("memT", [128, 8, NMEM], F32)
    A["poolst"] = nc.alloc_sbuf_tensor("poolst", [128, DEPTH, 8, PB], F32)
    A["convst"] = nc.alloc_sbuf_tensor("convst", [128, DEPTH, 8, CB], F32)
    A["io"] = [nc.alloc_sbuf_tensor("io%d" % i, [128, D], F32) for i in range(2)]
    A["vstage"] = A["io"][0]
    A["wstage"] = A["io"][1]
    A["psall"] = nc.alloc_psum_tensor("psall", [128, 4096], F32)
    T["alloc"] = A
    Pd, WSd, _ = build_program(nc, T, None)
    plan = WSd.rec
    P, WS, finals = build_program(nc, T, plan)
    assert len(WS.rec) == len(plan)
    P.emit(nc, finals)
    nc._dbgP = P
    return nc


_NC_CACHE = {}


def kernel(**inp):
    f = lambda a: np.ascontiguousarray(np.asarray(a, dtype=np.float32))
    if "nc" not in _NC_CACHE:
        _NC_CACHE["nc"] = build_nc()
    nc = _NC_CACHE["nc"]
    shared = {nm: f(inp[nm]) for nm in VROWS}
    shared["final_norm"] = f(inp["final_norm"]).reshape(1, D)
    for nm in ("ffn1_w_gu", "ffn1_w_down", "ffn2_w_gu", "ffn2_w_down", "w_in", "w_mem_kv", "w_pool", "w_dw", "w_pw", "w_out"):
        shared[nm] = f(inp[nm])
    xp, xs, mem = f(inp["x_prompt"]), f(inp["x_sample"]), f(inp["mem_prompt"])
    sp, sc = f(inp["state_pool"]), f(inp["state_conv"])
    ck = f(inp["cache_mem_k"]).reshape(DEPTH, 128, NMEM, D)
    cv = f(inp["cache_mem_v"]).reshape(DEPTH, 128, NMEM, D)
    in_maps = []
    for c in range(NCORES):
        m = dict(shared)
        sl = slice(c * NSEQ, (c + 1) * NSEQ)
        m["xp"] = xp[c]
        m["xs"] = np.ascontiguousarray(xs[sl].reshape(NSEQ * ST, D))
        m["mem"] = mem[c]
        m["spool"] = np.ascontiguousarray(sp[:, sl])
        m["sconv"] = np.ascontiguousarray(sc[:, sl])
        m["ck"] = np.ascontiguousarray(ck[:, sl])
        m["cv"] = np.ascontiguousarray(cv[:, sl])
        in_maps.append(m)
    res = run_bass_kernel_spmd(nc, in_maps[:DBG_CORES], core_ids=list(range(DBG_CORES)))
    R = list(res.results)
    while len(R) < NCORES:
        R.append(R[0])
    y_p = np.stack([R[c]["o_yp"] for c in range(NCORES)], 0)
    y_s = np.concatenate([R[c]["o_ys"].reshape(NSEQ, ST, D) for c in range(NCORES)], 0)
    pool_p = np.stack([R[c]["o_pool_p"] for c in range(NCORES)], 1)
    conv_p = np.stack([R[c]["o_conv_p"] for c in range(NCORES)], 1)
    mk = np.stack([R[c]["o_mk"] for c in range(NCORES)], 1).reshape(DEPTH, NCORES, NMEM, 4, 256)
    mv = np.stack([R[c]["o_mv"] for c in range(NCORES)], 1).reshape(DEPTH, NCORES, NMEM, 4, 256)
    pool_s = np.concatenate([R[c]["o_pool_s"] for c in range(NCORES)], 1)
    conv_s = np.concatenate([R[c]["o_conv_s"] for c in range(NCORES)], 1)
    return (y_p.astype(np.float32), y_s.astype(np.float32), pool_p.astype(np.float32), conv_p.astype(np.float32),
            mk.astype(np.float32), mv.astype(np.float32), pool_s.astype(np.float32), conv_s.astype(np.float32))
```

```python
import contextlib
import numpy as np
import concourse.bass as bass
import concourse.mybir as mybir
from concourse.bass_utils import run_bass_kernel_spmd

F32 = mybir.dt.float32
BF16 = mybir.dt.bfloat16
AF = mybir.ActivationFunctionType
ALU = mybir.AluOpType
AX = mybir.AxisListType

DEPTH = 4
NCORES = 8
D = 1024
DFF = 2816
NFF = 22
SEQ = 2048
NPASS = 2
TP = SEQ // NPASS
NSEQ = 16
NS = NSEQ // NPASS
ST = 8
NST = NS * ST
NTOK = TP + NST
NMEM = 256
PB, CB = 15, 30
INC = 7168
GL = 4096
EPS = 1e-6
NSLOT = 4
SLOTE = 4096
TBLK = [(0, 512), (512, 512), (1024, NST)]
PBLK = [(0, 512), (512, 512)]
PL = PB + TP
PW = PB + ST
CL = CB + TP
CW = CB + ST
ENGS = ("pe", "act", "dve", "pool", "sp")
import os
DBG_LAYERS = int(os.environ.get("DBG_LAYERS", DEPTH))
DBG_STAGE = int(os.environ.get("DBG_STAGE", 3))
DBG_PASSES = int(os.environ.get("DBG_PASSES", NPASS))
DBG_CORES = int(os.environ.get("DBG_CORES", NCORES))
VROWS = ["ffn1_norm", "mix_norm", "pool_scale", "b_dw", "conv_ln_g", "conv_ln_b", "ffn2_norm", "mem_norm"]


class Op:
    __slots__ = ("eng", "fn", "deps", "is_dma", "sem", "semval", "signal", "cnt", "idx", "lab")

    def __init__(self, eng, fn, is_dma=False):
        self.eng, self.fn, self.is_dma = eng, fn, is_dma
        self.deps = []
        self.sem = None
        self.semval = 0
        self.signal = False
        self.cnt = 0
        self.idx = 0


class Prog:
    def __init__(self, arenas=()):
        self.ops = {e: [] for e in ENGS}
        self.last_w = {}
        self.readers = {}
        self.dma_cnt = {}
        self.dma_last = {}
        self.arenas = set(arenas)
        self.arena_ops = {a: {} for a in arenas}
        self.fence = {a: {} for a in arenas}
        self.nops = 0
        self.cur = ""

    @staticmethod
    def _slot(op):
        return ("d", op.sem) if op.is_dma else ("e", op.eng)

    def _add(self, table, op):
        k = self._slot(op)
        o = table.get(k)
        if o is None or o.idx < op.idx:
            table[k] = op

    def _track(self, op, reads, writes):
        deps = {}

        def add(d):
            if d is op:
                return
            if d.eng == "pe" and op.eng == "pe" and not d.is_dma and not op.is_dma:
                return
            k = self._slot(d)
            o = deps.get(k)
            if o is None or o.idx < d.idx:
                deps[k] = d

        for k in list(reads) + list(writes):
            if isinstance(k, tuple) and k[0] in self.arenas:
                for d in self.fence[k[0]].values():
                    add(d)
        for k in reads:
            w = self.last_w.get(k)
            if w is not None:
                add(w)
            if isinstance(k, tuple) and k[0] == "pb":
                for r in self.readers.get(k, {}).values():
                    if r.eng != op.eng:
                        add(r)
        for k in writes:
            w = self.last_w.get(k)
            if w is not None:
                add(w)
            for r in self.readers.get(k, {}).values():
                add(r)
        op.deps = list(deps.values())
        for k in reads:
            self._add(self.readers.setdefault(k, {}), op)
        for k in writes:
            self.last_w[k] = op
            self.readers[k] = {}
        for k in list(reads) + list(writes):
            if isinstance(k, tuple) and k[0] in self.arenas:
                self._add(self.arena_ops[k[0]], op)

    def op(self, eng, fn, reads=(), writes=()):
        o = Op(eng, fn)
        o.lab = self.cur
        self.nops += 1
        o.idx = self.nops
        self._track(o, reads, writes)
        self.ops[eng].append(o)
        return o

    def dma(self, eng, fn, semkey, reads=(), writes=()):
        o = Op(eng, fn, True)
        o.lab = self.cur
        self.nops += 1
        o.idx = self.nops
        o.sem = semkey
        self._track(o, reads, writes)
        prev = self.dma_last.get(semkey)
        if prev is not None and all(d is not prev for d in o.deps):
            o.deps.append(prev)
        self.dma_cnt[semkey] = self.dma_cnt.get(semkey, 0) + 16
        o.semval = self.dma_cnt[semkey]
        self.dma_last[semkey] = o
        self.ops[eng].append(o)
        return o

    def handoff(self, arena):
        f = dict(self.fence[arena])
        for op in self.arena_ops[arena].values():
            self._add(f, op)
        self.fence[arena] = f
        self.arena_ops[arena] = {}
        for tab in (self.last_w, self.readers):
            for k in [k for k in tab if isinstance(k, tuple) and k[0] == arena]:
                del tab[k]

    def emit(self, nc, final_ops):
        for e in ENGS:
            for o in self.ops[e]:
                for d in o.deps:
                    if not d.is_dma:
                        d.signal = True
        for o in final_ops:
            if not o.is_dma:
                o.signal = True
        for e in ENGS:
            c = 0
            for o in self.ops[e]:
                if not o.is_dma and o.signal:
                    c += 1
                    o.cnt = c
        keys = sorted(self.dma_cnt.keys(), key=str)
        with contextlib.ExitStack() as st:
            esem = {e: st.enter_context(nc.semaphore("s_" + e)) for e in ENGS}
            dsem = {k: st.enter_context(nc.semaphore("d%d" % i)) for i, k in enumerate(keys)}
            block = st.enter_context(nc.Block())

            def mk(e):
                def body(engine):
                    waited = {}

                    def wait(d):
                        if d.is_dma:
                            key, val, sem = ("d", d.sem), d.semval, dsem[d.sem]
                        else:
                            key, val, sem = ("e", d.eng), d.cnt, esem[d.eng]
                        if waited.get(key, 0) >= val:
                            return
                        engine.wait_ge(sem, val)
                        waited[key] = val

                    for o in self.ops[e]:
                        for d in o.deps:
                            wait(d)
                        ins = o.fn(engine)
                        if o.is_dma:
                            ins.then_inc(dsem[o.sem], 16)
                        elif o.signal:
                            ins.then_inc(esem[e], 1)
                    if e == "sp":
                        for o in final_ops:
                            wait(o)
                return body

            block.tensor(mk("pe"))
            block.scalar(mk("act"))
            block.vector(mk("dve"))
            block.gpsimd(mk("pool"))
            block.sync(mk("sp"))


class WStream:
    def __init__(self, P, slots, plan):
        self.P, self.slots, self.plan = P, slots, plan
        self.rec = []
        self.nreg = 0
        self.i = 0

    def _register(self, j):
        s = j % NSLOT
        for di, (dstfn, src) in enumerate(self.plan[j]):
            dst = dstfn(self.slots[s])
            self.P.dma("pool", lambda e, dst=dst, src=src: e.dma_start(out=dst, in_=src),
                       ("w", s, di), writes=[("ws", s, di)])

    def get(self, dmas):
        j = self.i
        self.i += 1
        self.rec.append(dmas)
        if self.plan is not None:
            lim = min(j + NSLOT - 1, len(self.plan) - 1)
            while self.nreg <= lim:
                self._register(self.nreg)
                self.nreg += 1
        else:
            s = j % NSLOT
            for di, (dstfn, src) in enumerate(dmas):
                dst = dstfn(self.slots[s])
                self.P.dma("pool", lambda e, dst=dst, src=src: e.dma_start(out=dst, in_=src),
                           ("w", s, di), writes=[("ws", s, di)])
        s = j % NSLOT
        return self.slots[s], [("ws", s, di) for di in range(len(dmas))]


def kx_view(cols, off=0, width=None):
    def f(slot, cols=cols, off=off, width=width):
        W = width
        v = slot[:, 0:8 * W].rearrange("p (k c) -> p k c", k=8)
        return v[:, :, off:off + cols]
    return f


def build_program(nc, T, plan):
    P = Prog(arenas=("BIG", "Y"))
    A = T["alloc"]
    xT, hT, slots, BIGt, Yt = A["xT"], A["hT"], A["slots"], A["BIG"], A["Y"]
    identf, identb, onesm, epsT, vecT, wdwT, rcnt = (A[k] for k in ("identf", "identb", "onesm", "epsT", "vecT", "wdwT", "rcnt"))
    memT, poolst, convst, io, psall, vstage, wstage = (A[k] for k in ("memT", "poolst", "convst", "io", "psall", "vstage", "wstage"))
    WS = WStream(P, slots, plan)
    finals = []

    st = {"b": 0, "io": 0, "act": 0}

    reserved = set()

    def bank(n=1):
        b = st["b"]
        while True:
            if n == 2 and b % 2:
                b += 1
            if b + n > 8:
                b = 0
            if all((b + i) not in reserved for i in range(n)):
                break
            b += 1
        st["b"] = (b + n) % 8
        return b

    def pb(b, n=1):
        return psall[:, b * 512:(b + n) * 512]

    def pbk(b, n=1):
        return [("pb", b + i) for i in range(n)]

    def ioslot():
        s = st["io"]
        st["io"] = 1 - s
        return io[s], ("io", s)

    class Arena:
        def __init__(self, name, tile, nbytes):
            self.name, self.tile, self.nbytes, self.off = name, tile, nbytes, 0

        def reset(self, keep=0):
            P.handoff(self.name)
            self.off = keep

        def alloc(self, shape, dt):
            esz = 4 if dt == F32 else 2
            n = int(np.prod(shape))
            nb = (n * esz + 31) // 32 * 32
            assert self.off + nb <= self.nbytes, (self.name, self.off, nb, self.nbytes)
            e0 = self.off // 2
            v = self.tile[:, e0:e0 + nb // 2]
            self.off += nb
            if dt == F32:
                v = v.bitcast(F32)
            v = v[:, 0:n]
            if len(shape) == 2:
                v = v.rearrange("p (a b) -> p a b", a=shape[0])
            elif len(shape) == 3:
                v = v.rearrange("p (a b c) -> p a b c", a=shape[0], b=shape[1])
            return v

    BIG = Arena("BIG", BIGt, A["BIG_bytes"])
    Y = Arena("Y", Yt, A["Y_bytes"])

    def pull(gen, k):
        if gen is None:
            return
        for _ in range(k):
            try:
                next(gen)
            except StopIteration:
                return

    def evac_eng():
        st["act"] ^= 1
        return "act" if st["act"] else "dve"

    def copy_op(eng, out, in_, reads, writes):
        if eng == "act":
            return P.op("act", lambda e: e.copy(out=out, in_=in_), reads=reads, writes=writes)
        return P.op(eng, lambda e: e.tensor_copy(out=out, in_=in_), reads=reads, writes=writes)

    P.op("pool", lambda e: e.memset(identf[:], 0.0), writes=["identf"])
    P.op("pool", lambda e: e.affine_select(out=identf[:], in_=identf[:], pattern=[[-1, 128]], compare_op=ALU.not_equal,
                                           fill=1.0, base=0, channel_multiplier=1), reads=["identf"], writes=["identf"])
    P.op("dve", lambda e: e.tensor_copy(out=identb[:], in_=identf[:]), reads=["identf"], writes=["identb"])
    P.op("pool", lambda e: e.memset(onesm[:], 1.0 / 1024.0), writes=["onesm"])
    P.op("pool", lambda e: e.memset(epsT[:], EPS), writes=["epsT"])
    P.op("pool", lambda e: e.memset(poolst[:], 0.0), writes=["poolst"])
    P.op("pool", lambda e: e.memset(convst[:], 0.0), writes=["convst"])
    for g in range(4):
        w = 2 << g
        P.op("pool", lambda e, g=g, w=w: e.memset(rcnt[:, g, :], 1.0 / w), reads=["rcnt"], writes=["rcnt"])
        for t in range(w - 1):
            P.op("pool", lambda e, g=g, t=t: e.memset(rcnt[:, g, t:t + 1], 1.0 / (t + 1)), reads=["rcnt"], writes=["rcnt"])
    for i, nm in enumerate(VROWS):
        P.dma("sp", lambda e, i=i, nm=nm: e.dma_start(out=vstage[i * 4:(i + 1) * 4, :], in_=T[nm]), ("vs", i), writes=[("io", 0)])
    P.dma("sp", lambda e: e.dma_start(out=vstage[32:33, :], in_=T["final_norm"]), ("vs", 8), writes=[("io", 0)])
    b = bank()
    for k in range(8):
        P.op("pe", lambda e, k=k, b=b: e.transpose(pb(b)[:, k * 64:k * 64 + 33], vstage[0:33, k * 128:(k + 1) * 128], identf[0:33, 0:33]),
             reads=[("io", 0), "identf"], writes=pbk(b))
    P.op("dve", lambda e, b=b: e.tensor_copy(out=vecT[:], in_=pb(b).rearrange("p (k f) -> p k f", k=8)[:, :, 0:33]), reads=pbk(b), writes=["vecT"])
    P.dma("sp", lambda e: e.dma_start(out=wstage[0:124, :], in_=T["w_dw"].rearrange("l j c -> (l j) c")), ("vs", 9), writes=[("io", 1)])
    for hlf in range(2):
        b = bank()
        for kk in range(4):
            k = hlf * 4 + kk
            P.op("pe", lambda e, k=k, kk=kk, b=b: e.transpose(pb(b)[:, kk * 124:(kk + 1) * 124], wstage[0:124, k * 128:(k + 1) * 128], identf[0:124, 0:124]),
                 reads=[("io", 1), "identf"], writes=pbk(b))
        P.op("dve", lambda e, b=b, hlf=hlf: e.tensor_copy(out=wdwT[:, hlf * 4:(hlf + 1) * 4, :], in_=pb(b)[:, 0:496].rearrange("p (k f) -> p k f", k=4)),
             reads=pbk(b), writes=["wdwT"])
    for mt in range(2):
        t_io, kio = ioslot()
        P.dma("sp", lambda e, mt=mt, t_io=t_io: e.dma_start(out=t_io[:, :], in_=T["mem"][mt * 128:(mt + 1) * 128, :]), kio, writes=[kio])
        for hlf in range(2):
            b = bank()
            for kk in range(4):
                k = hlf * 4 + kk
                P.op("pe", lambda e, k=k, kk=kk, b=b, t_io=t_io: e.transpose(pb(b)[:, kk * 128:(kk + 1) * 128], t_io[:, k * 128:(k + 1) * 128], identf[:]),
                     reads=[kio, "identf"], writes=pbk(b))
            P.op("dve", lambda e, b=b, hlf=hlf, mt=mt: e.tensor_copy(out=memT[:, hlf * 4:(hlf + 1) * 4, mt * 128:(mt + 1) * 128],
                                                                    in_=pb(b).rearrange("p (k f) -> p k f", k=4)), reads=pbk(b), writes=["memT"])

    def vcol(name, l, k):
        r = 32 if name == "final_norm" else VROWS.index(name) * 4 + l
        return vecT[:, k, r:r + 1]

    def rmsnorm(src_fn, src_keys_fn, blks, gname, l, dst_fn, dst_keys_fn, tmp):
        for (t0, n) in blks:
            b = bank()
            for k in range(8):
                sq = tmp["sq"][k % 2]
                sk = ("Y", "sq", k % 2)
                P.op("act", lambda e, k=k, sq=sq, t0=t0, n=n: e.activation(out=sq[:, 0:n], in_=src_fn(k, t0, n), func=AF.Square),
                     reads=src_keys_fn(k, t0), writes=[sk])
                P.op("pe", lambda e, k=k, sq=sq, n=n, b=b: e.matmul(pb(b)[:, 0:n], lhsT=onesm[:], rhs=sq[:, 0:n], start=(k == 0), stop=(k == 7)),
                     reads=[sk, "onesm"], writes=pbk(b))
            rt, rstd = tmp["rt"], tmp["rstd"]
            P.op("act", lambda e, n=n, b=b, rt=rt: e.activation(out=rt[:, 0:n], in_=pb(b)[:, 0:n], func=AF.Sqrt, bias=epsT[:, 0:1], scale=1.0),
                 reads=pbk(b) + ["epsT"], writes=[("Y", "rt")])
            P.op("dve", lambda e, n=n, rt=rt, rstd=rstd: e.reciprocal(out=rstd[:, 0:n], in_=rt[:, 0:n]), reads=[("Y", "rt")], writes=[("Y", "rstd")])
            for k in range(8):
                P.op("dve", lambda e, k=k, t0=t0, n=n, rstd=rstd: e.scalar_tensor_tensor(
                    out=dst_fn(k, t0, n), in0=src_fn(k, t0, n), scalar=vcol(gname, l, k), in1=rstd[:, 0:n], op0=ALU.mult, op1=ALU.mult),
                    reads=src_keys_fn(k, t0) + [("Y", "rstd"), "vecT"], writes=dst_keys_fn(k, t0))

    def norm_tmp():
        return {"sq": [Y.alloc([512], BF16), Y.alloc([512], BF16)], "rt": Y.alloc([512], F32), "rstd": Y.alloc([512], F32)}

    xk = lambda k, t0: [("x", k, t0)]
    hk = lambda k, t0: [("h", k, t0)]
    x_fn = lambda k, t0, n: xT[:, k, t0:t0 + n]
    h_fn = lambda k, t0, n: hT[:, k, t0:t0 + n]

    def proj_kx(slot, skeys, off, blk, rhs_fn, rhs_keys_fn, nk=8, W=512):
        t0, n = blk
        b = bank()
        v = slot[:, 0:nk * W].rearrange("p (k c) -> p k c", k=nk)
        for k in range(nk):
            P.op("pe", lambda e, k=k, b=b, v=v: e.matmul(pb(b)[:, 0:n], lhsT=v[:, k, off:off + 128], rhs=rhs_fn(k, t0, n),
                                                        start=(k == 0), stop=(k == nk - 1)),
                 reads=skeys + rhs_keys_fn(k, t0), writes=pbk(b))
        return b

    def ffn(l, which):
        wgu, wdn = T["ffn%d_w_gu" % which][l], T["ffn%d_w_down" % which][l]
        P.cur = "ffn%d" % which
        Y.reset()
        tmp = norm_tmp()
        rmsnorm(x_fn, xk, TBLK, "ffn%d_norm" % which, l, h_fn, hk, tmp)
        sg = [Y.alloc([512], BF16), Y.alloc([512], BF16)]
        BIG.reset()
        hid = BIG.alloc([12, NTOK], BF16)
        for (f0, nf) in ((0, 12), (12, 10)):
            for pi in range(nf // 2):
                fa = f0 + 2 * pi
                slot, sk = WS.get([(kx_view(256, 0, 512), wgu[:, fa * 128:fa * 128 + 256].rearrange("(k p) c -> p k c", p=128)),
                                   (kx_view(256, 256, 512), wgu[:, DFF + fa * 128:DFF + fa * 128 + 256].rearrange("(k p) c -> p k c", p=128))])
                for j in range(2):
                    fi = 2 * pi + j
                    for bi, blk in enumerate(TBLK):
                        t0, n = blk
                        bg = proj_kx(slot, sk, j * 128, blk, h_fn, hk)
                        bu = proj_kx(slot, sk, 256 + j * 128, blk, h_fn, hk)
                        s = sg[(fi * 3 + bi) % 2]
                        skk = ("Y", "sg", (fi * 3 + bi) % 2)
                        P.op("act", lambda e, bg=bg, n=n, s=s: e.activation(out=s[:, 0:n], in_=pb(bg)[:, 0:n], func=AF.Silu), reads=pbk(bg), writes=[skk])
                        P.op("dve", lambda e, bu=bu, n=n, s=s, fi=fi, t0=t0: e.tensor_tensor(out=hid[:, fi, t0:t0 + n], in0=pb(bu)[:, 0:n], in1=s[:, 0:n], op=ALU.mult),
                             reads=pbk(bu) + [skk], writes=[("BIG", "hid", fi, t0)])
            for q in range(4):
                slot, sk = WS.get([(lambda sl, nf=nf: sl[:, 0:nf * 256].rearrange("p (f c) -> p f c", f=nf),
                                    wdn[f0 * 128:(f0 + nf) * 128, q * 256:(q + 1) * 256].rearrange("(f p) c -> p f c", p=128))])
                v = slot[:, 0:nf * 256].rearrange("p (f c) -> p f c", f=nf)
                for cc in range(2):
                    c = 2 * q + cc
                    for (t0, n) in TBLK:
                        b = bank()
                        for fi in range(nf):
                            P.op("pe", lambda e, fi=fi, b=b, v=v, cc=cc, t0=t0, n=n: e.matmul(pb(b)[:, 0:n], lhsT=v[:, fi, cc * 128:(cc + 1) * 128], rhs=hid[:, fi, t0:t0 + n],
                                                                                            start=(fi == 0), stop=(fi == nf - 1)),
                                 reads=sk + [("BIG", "hid", fi, t0)], writes=pbk(b))
                        P.op("dve", lambda e, b=b, c=c, t0=t0, n=n: e.scalar_tensor_tensor(out=xT[:, c, t0:t0 + n], in0=pb(b)[:, 0:n], scalar=0.5, in1=xT[:, c, t0:t0 + n],
                                                                                           op0=ALU.mult, op1=ALU.add),
                             reads=pbk(b) + xk(c, t0), writes=xk(c, t0))

    def transpose_out(src_fn, src_keys, nrow, dst_fn_list):
        b = bank(2)
        for k in range(8):
            P.op("pe", lambda e, k=k, b=b: e.transpose(pb(b, 2)[0:nrow, k * 128:(k + 1) * 128], src_fn(k), identf[:]),
                 reads=src_keys + ["identf"], writes=pbk(b, 2))
        t_io, kio = ioslot()
        copy_op(evac_eng(), t_io[0:nrow, :], pb(b, 2)[0:nrow, :], pbk(b, 2), [kio])
        outs = []
        for (r0, nr, dst, semk) in dst_fn_list:
            outs.append(P.dma("sp", lambda e, r0=r0, nr=nr, dst=dst, t_io=t_io: e.dma_start(out=dst, in_=t_io[r0:r0 + nr, :]), semk, reads=[kio]))
        finals.extend(outs)

    def load_rows_T(src_ap, nrow, dst_fn, dst_keys):
        t_io, kio = ioslot()
        P.dma("sp", lambda e, t_io=t_io: e.dma_start(out=t_io[0:nrow, :], in_=src_ap), kio, writes=[kio])
        for hlf in range(2):
            b = bank()
            for kk in range(4):
                k = hlf * 4 + kk
                P.op("pe", lambda e, k=k, kk=kk, b=b, t_io=t_io: e.transpose(pb(b)[:, kk * nrow:(kk + 1) * nrow], t_io[0:nrow, k * 128:(k + 1) * 128], identf[0:nrow, 0:nrow]),
                     reads=[kio, "identf"], writes=pbk(b))
            copy_op(evac_eng(), dst_fn(hlf), pb(b)[:, 0:4 * nrow].rearrange("p (k f) -> p k f", k=4), pbk(b), dst_keys)

    def mixer(l, p):
        last = (p == NPASS - 1)
        win = T["w_in"][l]
        s0 = p * NS
        P.cur = "mixnorm"
        Y.reset()
        tmp = norm_tmp()
        rmsnorm(x_fn, xk, TBLK, "mix_norm", l, h_fn, hk, tmp)
        BIG.reset()
        merged = BIG.alloc([8, NTOK], F32)
        mk = lambda c, t0: [("BIG", "m", c, t0)]

        def win_piece(c0a, c0b):
            return WS.get([(kx_view(256, 0, 512), win[:, c0a:c0a + 256].rearrange("(k p) c -> p k c", p=128)),
                           (kx_view(256, 256, 512), win[:, c0b:c0b + 256].rearrange("(k p) c -> p k c", p=128))])

        P.cur = "A"
        Y.reset()
        spoolT = Y.alloc([8, NS * PB], F32)
        ustage = Y.alloc([8, NST], F32)
        LA = PL + NS * PW
        uext = Y.alloc([LA], F32)
        sAB = [Y.alloc([LA], F32), Y.alloc([LA], F32)]
        pooled2 = [Y.alloc([2, NTOK], BF16), Y.alloc([2, NTOK], BF16)]
        sigA2 = [[Y.alloc([NTOK], BF16), Y.alloc([NTOK], BF16)] for _ in range(2)]
        t15 = Y.alloc([16], F32)
        load_rows_T(T["spool"][l, s0:s0 + NS].rearrange("s r c -> (s r) c"), NS * PB,
                    lambda hlf: spoolT[:, hlf * 4:(hlf + 1) * 4, :], [("Y", "spoolT")])
        finals.append(P.dma("sp", lambda e: e.dma_start(out=T["o_pool_s"][l, s0:s0 + NS, 0:PB - ST, :], in_=T["spool"][l, s0:s0 + NS, ST:PB, :]), ("dd", 0)))
        def stagePA(g):
            w = 2 << g
            pooled, sigA = pooled2[g % 2], sigA2[g % 2]
            slot, sk = win_piece(g * 256, GL + g * 256)
            for j in range(2):
                c = 2 * g + j
                for (t0, n) in TBLK:
                    b = proj_kx(slot, sk, j * 128, (t0, n), h_fn, hk)
                    if t0 < TP:
                        copy_op("act", uext[:, PB + t0:PB + t0 + n], pb(b)[:, 0:n], pbk(b), [("Y", "uext")])
                    else:
                        copy_op("act", uext[:, PL:LA].rearrange("p (s w) -> p s w", w=PW)[:, :, PB:PW],
                                pb(b)[:, 0:NST].rearrange("p (s t) -> p s t", t=ST), pbk(b), [("Y", "uext")])
                P.op("pool", lambda e, c=c: e.tensor_copy(out=uext[:, 0:PB], in_=poolst[:, l, c, :]), reads=[("pst", l, c)], writes=[("Y", "uext")])
                P.op("pool", lambda e, c=c: e.tensor_copy(out=uext[:, PL:LA].rearrange("p (s w) -> p s w", w=PW)[:, :, 0:PB],
                                                          in_=spoolT[:, c, :].rearrange("p (s r) -> p s r", r=PB)), reads=[("Y", "spoolT")], writes=[("Y", "uext")])
                P.op("pool", lambda e, c=c: e.tensor_copy(out=poolst[:, l, c, :], in_=uext[:, TP:TP + PB]), reads=[("Y", "uext")], writes=[("pst", l, c)])
                P.op("pool", lambda e, c=c: e.tensor_copy(out=ustage[:, c, :].rearrange("p (s t) -> p s t", t=ST),
                                                          in_=uext[:, PL:LA].rearrange("p (s w) -> p s w", w=PW)[:, :, PB:PW]), reads=[("Y", "uext")], writes=[("Y", "ustage")])
                src, srck = uext, ("Y", "uext")
                for stp in range(g + 1):
                    sh = 1 << stp
                    dstt, dk = sAB[stp % 2], ("Y", "sAB", stp % 2)
                    P.op("dve", lambda e, sh=sh, src=src, dstt=dstt: e.tensor_tensor(out=dstt[:, sh:LA], in0=src[:, sh:LA], in1=src[:, 0:LA - sh], op=ALU.add),
                         reads=[srck], writes=[dk])
                    src, srck = dstt, dk
                pk = ("Y", "pooled", g % 2, j)
                P.op("dve", lambda e, src=src, j=j, w=w: e.scalar_tensor_tensor(out=pooled[:, j, 0:TP], in0=src[:, PB:PL], scalar=1.0 / w, in1=uext[:, PB:PL],
                                                                               op0=ALU.mult, op1=ALU.subtract), reads=[srck, ("Y", "uext")], writes=[pk])
                if p == 0:
                    P.op("dve", lambda e, src=src, g=g: e.tensor_tensor(out=t15[:, 0:PB], in0=src[:, PB:2 * PB], in1=rcnt[:, g, 0:PB], op=ALU.mult),
                         reads=[srck, "rcnt"], writes=[("Y", "t15")])
                    P.op("dve", lambda e, j=j: e.tensor_tensor(out=pooled[:, j, 0:PB], in0=t15[:, 0:PB], in1=uext[:, PB:2 * PB], op=ALU.subtract),
                         reads=[("Y", "t15"), ("Y", "uext"), pk], writes=[pk])
                P.op("dve", lambda e, src=src, j=j, w=w: e.scalar_tensor_tensor(
                    out=pooled[:, j, TP:NTOK].rearrange("p (s t) -> p s t", t=ST),
                    in0=src[:, PL:LA].rearrange("p (s w) -> p s w", w=PW)[:, :, PB:PW], scalar=1.0 / w,
                    in1=uext[:, PL:LA].rearrange("p (s w) -> p s w", w=PW)[:, :, PB:PW], op0=ALU.mult, op1=ALU.subtract),
                    reads=[srck, ("Y", "uext"), pk], writes=[pk])
            for j in range(2):
                for (t0, n) in TBLK:
                    b = proj_kx(slot, sk, 256 + j * 128, (t0, n), h_fn, hk)
                    P.op("act", lambda e, b=b, j=j, t0=t0, n=n: e.activation(out=sigA[j][:, t0:t0 + n], in_=pb(b)[:, 0:n], func=AF.Sigmoid),
                         reads=pbk(b), writes=[("Y", "sigA", g % 2, j, t0)])
        def stageVA(g):
            pooled, sigA = pooled2[g % 2], sigA2[g % 2]
            wps, wpk = WS.get([(lambda sl: sl[:, 0:512].rearrange("p (k c) -> p k c", k=2),
                                T["w_pool"][l, g].rearrange("(k p) c -> p k c", p=128))])
            wpv = wps[:, 0:512].rearrange("p (k c) -> p k c", k=2)
            for j2 in range(2):
                c = 2 * g + j2
                for (t0, n) in TBLK:
                    b = bank()
                    for kk in range(2):
                        P.op("pe", lambda e, kk=kk, b=b, wpv=wpv, j2=j2, t0=t0, n=n: e.matmul(pb(b)[:, 0:n], lhsT=wpv[:, kk, j2 * 128:(j2 + 1) * 128],
                                                                                        rhs=pooled[:, kk, t0:t0 + n], start=(kk == 0), stop=(kk == 1)),
                             reads=wpk + [("Y", "pooled", g % 2, kk)], writes=pbk(b))
                    P.op("dve", lambda e, b=b, c=c, j2=j2, t0=t0, n=n: e.scalar_tensor_tensor(out=merged[:, c, t0:t0 + n], in0=pb(b)[:, 0:n], scalar=vcol("pool_scale", l, c),
                                                                                             in1=sigA[j2][:, t0:t0 + n], op0=ALU.mult, op1=ALU.mult),
                         reads=pbk(b) + [("Y", "sigA", g % 2, j2, t0), "vecT"], writes=mk(c, t0))
        stagePA(0)
        for g in range(4):
            if g + 1 < 4:
                stagePA(g + 1)
            stageVA(g)
        transpose_out(lambda k: ustage[:, k, :], [("Y", "ustage")], NST,
                      [(s * ST, ST, T["o_pool_s"][l, s0 + s, PB - ST:PB, :], ("ops", s)) for s in range(NS)])
        if last:
            transpose_out(lambda k: poolst[:, l, k, :], [("pst", l, k) for k in range(8)], PB, [(0, PB, T["o_pool_p"][l], ("opp", 0))])

        P.cur = "B"
        Y.reset()
        yb = Y.alloc([8, NTOK], BF16)
        keepB = Y.off
        zstage = Y.alloc([8, NST], F32)
        sconvT = Y.alloc([8, NS * CB], F32)
        LB = CL + NS * CW
        zc = [Y.alloc([LB], BF16), Y.alloc([LB], BF16)]
        diag2 = [Y.alloc([31, 128], BF16), Y.alloc([31, 128], BF16)]
        sz = Y.alloc([NTOK], BF16)
        for hf in range(2):
            load_rows_T(T["sconv"][l, s0 + hf * 4:s0 + hf * 4 + 4].rearrange("s r c -> (s r) c"), 4 * CB,
                        lambda hlf, hf=hf: sconvT[:, hlf * 4:(hlf + 1) * 4, hf * 4 * CB:(hf + 1) * 4 * CB], [("Y", "sconvT")])
        finals.append(P.dma("sp", lambda e: e.dma_start(out=T["o_conv_s"][l, s0:s0 + NS, 0:CB - ST, :], in_=T["sconv"][l, s0:s0 + NS, ST:CB, :]), ("dd", 1)))
        pieceB = {}

        def stagePB(c):
            q, j = c // 2, c % 2
            if j == 0:
                pieceB["s"] = win_piece(1024 + q * 256, 2048 + q * 256)
            slot, sk = pieceB["s"]
            zt, zk = zc[c % 2], ("Y", "zc", c % 2)
            diag, dgk = diag2[c % 2], ("Y", "diag", c % 2)
            for (t0, n) in TBLK:
                bgt = proj_kx(slot, sk, 256 + j * 128, (t0, n), h_fn, hk)
                P.op("act", lambda e, bgt=bgt, t0=t0, n=n: e.activation(out=sz[:, t0:t0 + n], in_=pb(bgt)[:, 0:n], func=AF.Sigmoid), reads=pbk(bgt), writes=[("Y", "sz", t0)])
                ba = proj_kx(slot, sk, j * 128, (t0, n), h_fn, hk)
                if t0 < TP:
                    P.op("dve", lambda e, ba=ba, zt=zt, t0=t0, n=n: e.tensor_tensor(out=zt[:, CB + t0:CB + t0 + n], in0=pb(ba)[:, 0:n], in1=sz[:, t0:t0 + n], op=ALU.mult),
                         reads=pbk(ba) + [("Y", "sz", t0)], writes=[zk])
                    if t0 + n == TP:
                        P.op("pool", lambda e, zt=zt, c=c: e.tensor_copy(out=zt[:, 0:CB], in_=convst[:, l, c, :]), reads=[("cst", l, c)], writes=[zk])
                        P.op("dve", lambda e, ba=ba, c=c, n=n: e.tensor_tensor(out=convst[:, l, c, :], in0=pb(ba)[:, n - CB:n], in1=sz[:, TP - CB:TP], op=ALU.mult),
                             reads=pbk(ba) + [("Y", "sz", t0)], writes=[("cst", l, c)])
                else:
                    P.op("dve", lambda e, ba=ba, zt=zt: e.tensor_tensor(out=zt[:, CL:LB].rearrange("p (s w) -> p s w", w=CW)[:, :, CB:CW],
                                                                      in0=pb(ba)[:, 0:NST].rearrange("p (s t) -> p s t", t=ST),
                                                                      in1=sz[:, TP:NTOK].rearrange("p (s t) -> p s t", t=ST), op=ALU.mult),
                         reads=pbk(ba) + [("Y", "sz", t0)], writes=[zk])
                    P.op("dve", lambda e, ba=ba, c=c: e.tensor_tensor(out=zstage[:, c, :], in0=pb(ba)[:, 0:NST], in1=sz[:, TP:NTOK], op=ALU.mult),
                         reads=pbk(ba) + [("Y", "sz", t0)], writes=[("Y", "zstage")])
            P.op("pool", lambda e, zt=zt, c=c: e.tensor_copy(out=zt[:, CL:LB].rearrange("p (s w) -> p s w", w=CW)[:, :, 0:CB],
                                                             in_=sconvT[:, c, :].rearrange("p (s r) -> p s r", r=CB)), reads=[("Y", "sconvT")], writes=[zk])
            P.op("pool", lambda e, c=c: e.tensor_tensor(out=diag[:, :, :], in0=bass.AP(identb, 0, [[128, 128], [0, 31], [1, 128]]),
                                                        in1=bass.AP(wdwT, c * 124 + l * 31, [[992, 128], [1, 31], [0, 128]]), op=ALU.mult),
                 reads=["identb", "wdwT"], writes=[dgk])
        def stageVB(c):
            zt, zk = zc[c % 2], ("Y", "zc", c % 2)
            diag, dgk = diag2[c % 2], ("Y", "diag", c % 2)
            for (t0, n) in TBLK:
                b = bank()
                for jj in range(31):
                    if t0 < TP:
                        rhs_fn = lambda jj=jj, zt=zt, t0=t0, n=n: zt[:, t0 + jj:t0 + jj + n]
                    else:
                        rhs_fn = lambda jj=jj, zt=zt: zt[:, CL:LB].rearrange("p (s w) -> p s w", w=CW)[:, :, jj:jj + ST]
                    P.op("pe", lambda e, jj=jj, b=b, n=n, rhs_fn=rhs_fn: e.matmul(pb(b)[:, 0:n], lhsT=diag[:, jj, :], rhs=rhs_fn(), start=(jj == 0), stop=(jj == 30)),
                         reads=[zk, dgk], writes=pbk(b))
                P.op("act", lambda e, b=b, c=c, t0=t0, n=n: e.activation(out=yb[:, c, t0:t0 + n], in_=pb(b)[:, 0:n], func=AF.Identity, bias=vcol("b_dw", l, c), scale=1.0),
                     reads=pbk(b) + ["vecT"], writes=[("Y", "yb", c, t0)])
        stagePB(0)
        for c in range(8):
            if c + 1 < 8:
                stagePB(c + 1)
            stageVB(c)
        transpose_out(lambda k: zstage[:, k, :], [("Y", "zstage")], NST,
                      [(s * ST, ST, T["o_conv_s"][l, s0 + s, CB - ST:CB, :], ("ocs", s)) for s in range(NS)])
        if last:
            transpose_out(lambda k: convst[:, l, k, :], [("cst", l, k) for k in range(8)], CB, [(0, CB, T["o_conv_p"][l], ("ocp", 0))])
        P.cur = "B_ln"
        Y.reset(keepB)
        sq2 = [Y.alloc([512], BF16), Y.alloc([512], BF16)]
        meanb, m2, rt, rstd, nmr = (Y.alloc([512], F32) for _ in range(5))
        t1 = [Y.alloc([512], F32), Y.alloc([512], F32), Y.alloc([512], F32), Y.alloc([512], F32)]
        sigB8 = Y.alloc([8, NTOK], BF16)
        tmpB = t1[0:2]
        yb_fn = lambda k, t0, n: yb[:, k, t0:t0 + n]
        ybk = lambda k, t0: [("Y", "yb", k, t0)]
        def gate_b(q):
            sg_, sgk = WS.get([(kx_view(512, 0, 512), win[:, GL + 1024 + q * 512:GL + 1024 + (q + 1) * 512].rearrange("(k p) c -> p k c", p=128))])
            for cc in range(4):
                c = 4 * q + cc
                for (t0, n) in TBLK:
                    b = proj_kx(sg_, sgk, cc * 128, (t0, n), h_fn, hk)
                    P.op("act", lambda e, b=b, c=c, t0=t0, n=n: e.activation(out=sigB8[:, c, t0:t0 + n], in_=pb(b)[:, 0:n], func=AF.Sigmoid), reads=pbk(b), writes=[("Y", "sigB", c, t0)])
                    yield

        statb = []
        gen0 = gate_b(0)
        step = 0
        for (t0, n) in TBLK:
            bm, be = bank(), bank()
            reserved.update((bm, be))
            statb.append((bm, be))
            for k in range(8):
                sq = sq2[k % 2]
                sqk = ("Y", "sq2", k % 2)
                eng = "act" if k % 2 == 0 else "dve"
                if eng == "act":
                    P.op("act", lambda e, k=k, sq=sq, t0=t0, n=n: e.activation(out=sq[:, 0:n], in_=yb[:, k, t0:t0 + n], func=AF.Square), reads=[("Y", "yb", k, t0)], writes=[sqk])
                else:
                    P.op("dve", lambda e, k=k, sq=sq, t0=t0, n=n: e.tensor_tensor(out=sq[:, 0:n], in0=yb[:, k, t0:t0 + n], in1=yb[:, k, t0:t0 + n], op=ALU.mult), reads=[("Y", "yb", k, t0)], writes=[sqk])
                P.op("pe", lambda e, k=k, bm=bm, t0=t0, n=n: e.matmul(pb(bm)[:, 0:n], lhsT=onesm[:], rhs=yb[:, k, t0:t0 + n], start=(k == 0), stop=(k == 7)),
                     reads=[("Y", "yb", k, t0), "onesm"], writes=pbk(bm))
                P.op("pe", lambda e, k=k, be=be, sq=sq, n=n: e.matmul(pb(be)[:, 0:n], lhsT=onesm[:], rhs=sq[:, 0:n], start=(k == 0), stop=(k == 7)),
                     reads=[sqk, "onesm"], writes=pbk(be))
                step += 1
                if step % 2 == 0:
                    pull(gen0, 1)
        pull(gen0, 100)
        for bi, (t0, n) in enumerate(TBLK):
            bm, be = statb[bi]
            P.op("act", lambda e, bm=bm, n=n: e.copy(out=meanb[:, 0:n], in_=pb(bm)[:, 0:n]), reads=pbk(bm), writes=[("Y", "meanb")])
            P.op("dve", lambda e, n=n: e.tensor_tensor(out=m2[:, 0:n], in0=meanb[:, 0:n], in1=meanb[:, 0:n], op=ALU.mult), reads=[("Y", "meanb")], writes=[("Y", "m2")])
            P.op("dve", lambda e, be=be, n=n: e.tensor_tensor(out=m2[:, 0:n], in0=pb(be)[:, 0:n], in1=m2[:, 0:n], op=ALU.subtract), reads=pbk(be) + [("Y", "m2")], writes=[("Y", "m2")])
            reserved.difference_update((bm, be))
            P.op("dve", lambda e, n=n: e.tensor_scalar(out=m2[:, 0:n], in0=m2[:, 0:n], scalar1=0.0, scalar2=0.0, op0=ALU.max, op1=ALU.add), reads=[("Y", "m2")], writes=[("Y", "m2")])
            P.op("act", lambda e, n=n: e.activation(out=rt[:, 0:n], in_=m2[:, 0:n], func=AF.Sqrt, bias=epsT[:, 0:1], scale=1.0), reads=[("Y", "m2"), "epsT"], writes=[("Y", "rt2")])
            P.op("dve", lambda e, n=n: e.reciprocal(out=rstd[:, 0:n], in_=rt[:, 0:n]), reads=[("Y", "rt2")], writes=[("Y", "rstd2")])
            P.op("dve", lambda e, n=n: e.scalar_tensor_tensor(out=nmr[:, 0:n], in0=meanb[:, 0:n], scalar=-1.0, in1=rstd[:, 0:n], op0=ALU.mult, op1=ALU.mult),
                 reads=[("Y", "meanb"), ("Y", "rstd2")], writes=[("Y", "nmr")])
            for k0 in range(0, 8, 2):
                eng = "dve"
                for k in (k0, k0 + 1):
                    tt, tk = t1[k % 4], ("Y", "t1", k % 4)
                    P.op(eng, lambda e, k=k, tt=tt, t0=t0, n=n: e.tensor_tensor(out=tt[:, 0:n], in0=yb[:, k, t0:t0 + n], in1=rstd[:, 0:n], op=ALU.mult),
                         reads=[("Y", "yb", k, t0), ("Y", "rstd2")], writes=[tk])
                for k in (k0, k0 + 1):
                    tt, tk = t1[k % 4], ("Y", "t1", k % 4)
                    P.op(eng, lambda e, tt=tt, n=n: e.tensor_tensor(out=tt[:, 0:n], in0=tt[:, 0:n], in1=nmr[:, 0:n], op=ALU.add), reads=[tk, ("Y", "nmr")], writes=[tk])
                    P.op("act", lambda e, k=k, tt=tt, t0=t0, n=n: e.activation(out=yb[:, k, t0:t0 + n], in_=tt[:, 0:n], func=AF.Silu, bias=vcol("conv_ln_b", l, k), scale=vcol("conv_ln_g", l, k)),
                         reads=[tk, "vecT"], writes=[("Y", "yb", k, t0)])
        pull(gate_b(1), 100)
        P.cur = "B_pw"
        ti = 0
        for q in range(2):
            sp_, spk = WS.get([(kx_view(512, 0, 512), T["w_pw"][l][:, q * 512:(q + 1) * 512].rearrange("(k p) c -> p k c", p=128))])
            for cc in range(4):
                c = 4 * q + cc
                for (t0, n) in TBLK:
                    b2 = proj_kx(sp_, spk, cc * 128, (t0, n), yb_fn, ybk)
                    tb, tbk = tmpB[ti % 2], ("Y", "t1", ti % 2)
                    ti += 1
                    P.op("dve", lambda e, b2=b2, tb=tb, c=c, t0=t0, n=n: e.tensor_tensor(out=tb[:, 0:n], in0=pb(b2)[:, 0:n], in1=sigB8[:, c, t0:t0 + n], op=ALU.mult),
                         reads=pbk(b2) + [("Y", "sigB", c, t0)], writes=[tbk])
                    P.op("pool", lambda e, tb=tb, c=c, t0=t0, n=n: e.tensor_tensor(out=merged[:, c, t0:t0 + n], in0=merged[:, c, t0:t0 + n], in1=tb[:, 0:n], op=ALU.add),
                         reads=[tbk] + mk(c, t0), writes=mk(c, t0))

        P.cur = "C_kv"
        Y.reset()
        KTp = Y.alloc([8, NMEM], BF16)
        Vp = Y.alloc([2, D], BF16)
        keepC = Y.off
        memn = Y.alloc([8, NMEM], BF16)
        kvst = [Y.alloc([512], F32), Y.alloc([512], F32)]
        tmpn = norm_tmp()
        rmsnorm(lambda k, t0, n: memT[:, k, t0:t0 + n], lambda k, t0: ["memT"], [(0, NMEM)], "mem_norm", l,
                lambda k, t0, n: memn[:, k, t0:t0 + n], lambda k, t0: [("Y", "memn", k)], tmpn)
        mn_fn = lambda k, t0, n: memn[:, k, t0:t0 + n]
        mnk = lambda k, t0: [("Y", "memn", k)]
        wkv = T["w_mem_kv"][l]
        kvi = 0
        for q in range(4):
            slot, sk = WS.get([(kx_view(512, 0, 512), wkv[:, q * 512:(q + 1) * 512].rearrange("(k p) c -> p k c", p=128))])
            v = slot[:, 0:4096].rearrange("p (k c) -> p k c", k=8)
            if q < 2:
                for cc in range(4):
                    c = 4 * q + cc
                    b = proj_kx(slot, sk, cc * 128, (0, NMEM), mn_fn, mnk)
                    copy_op(evac_eng(), KTp[:, c, :], pb(b)[:, 0:NMEM], pbk(b), [("Y", "KTp", c)])
            if q >= 2 or p == 0:
                for mt in range(2):
                    b = bank()
                    for k in range(8):
                        P.op("pe", lambda e, k=k, b=b, mt=mt, v=v: e.matmul(pb(b)[:, :], lhsT=memn[:, k, mt * 128:(mt + 1) * 128], rhs=v[:, k, :], start=(k == 0), stop=(k == 7)),
                             reads=sk + [("Y", "memn", k)], writes=pbk(b))
                    if q >= 2 and p != 0:
                        copy_op("dve", Vp[:, mt, (q - 2) * 512:(q - 1) * 512], pb(b)[:, :], pbk(b), [("Y", "Vp", mt, q - 2)])
                    if p == 0:
                        ks, kk_ = kvst[kvi % 2], ("Y", "kvst", kvi % 2)
                        kvi += 1
                        copy_op("act", ks[:, :], pb(b)[:, :], pbk(b), [kk_])
                        if q >= 2:
                            copy_op("dve", Vp[:, mt, (q - 2) * 512:(q - 1) * 512], ks[:, :], [kk_], [("Y", "Vp", mt, q - 2)])
                        dst = (T["o_mk"] if q < 2 else T["o_mv"])[l, mt * 128:(mt + 1) * 128, (q % 2) * 512:(q % 2 + 1) * 512]
                        finals.append(P.dma("sp", lambda e, dst=dst, ks=ks: e.dma_start(out=dst, in_=ks[:, :]), ("okv", kvi % 2), reads=[kk_]))
        P.cur = "C_prompt"
        Y.reset(keepC)
        qTs = Y.alloc([8, NST], BF16)
        sigCs = Y.alloc([8, NST], BF16)
        qmask = Y.alloc([NS, 8, NST], BF16)
        PTm = Y.alloc([NS, 8, NST], BF16)
        keepC2 = Y.off
        qT2 = [Y.alloc([2, TP], BF16), Y.alloc([2, TP], BF16)]
        sigC2 = [[Y.alloc([TP], BF16), Y.alloc([TP], BF16)] for _ in range(2)]
        PT = Y.alloc([2, TP], BF16)
        Pe = [Y.alloc([NMEM], BF16), Y.alloc([NMEM], BF16)]
        Pn = [Y.alloc([NMEM], BF16), Y.alloc([NMEM], BF16)]
        smx = Y.alloc([4, 8], F32)
        tmpC = [Y.alloc([512], F32), Y.alloc([512], F32)]
        P.op("pool", lambda e: e.memset(qmask[:, :, :, :], 0.0), writes=[("Y", "qmask")])
        P.op("pool", lambda e: e.memset(PTm[:, :, :, :], 0.0), writes=[("Y", "PTm")])
        ci = [0]

        def proj_gen(h):
            slot, sk = win_piece(3072 + h * 256, GL + 2048 + h * 256)
            qT, sigC = qT2[h % 2], sigC2[h % 2]
            for j in range(2):
                for (t0, n) in TBLK:
                    b = proj_kx(slot, sk, j * 128, (t0, n), h_fn, hk)
                    if t0 < TP:
                        P.op("act", lambda e, b=b, j=j, t0=t0, n=n, qT=qT: e.mul(out=qT[:, j, t0:t0 + n], in_=pb(b)[:, 0:n], mul=0.0625), reads=pbk(b), writes=[("Y", "qT", h % 2, j, t0)])
                    else:
                        P.op("act", lambda e, b=b, j=j, h=h: e.mul(out=qTs[:, 2 * h + j, :], in_=pb(b)[:, 0:NST], mul=0.0625), reads=pbk(b), writes=[("Y", "qTs")])
                    yield
                    b = proj_kx(slot, sk, 256 + j * 128, (t0, n), h_fn, hk)
                    if t0 < TP:
                        P.op("act", lambda e, b=b, j=j, t0=t0, n=n, sigC=sigC: e.activation(out=sigC[j][:, t0:t0 + n], in_=pb(b)[:, 0:n], func=AF.Sigmoid), reads=pbk(b), writes=[("Y", "sigC", h % 2, j, t0)])
                    else:
                        P.op("act", lambda e, b=b, j=j, h=h: e.activation(out=sigCs[:, 2 * h + j, :], in_=pb(b)[:, 0:NST], func=AF.Sigmoid), reads=pbk(b), writes=[("Y", "sigCs")])
                    yield

        def scores(h, i):
            qT = qT2[h % 2]
            t0b = (i * 128) // 512 * 512
            b = bank()
            reserved.add(b)
            for dc in range(2):
                P.op("pe", lambda e, dc=dc, b=b, i=i, h=h, qT=qT: e.matmul(pb(b)[:, 0:NMEM], lhsT=qT[:, dc, i * 128:(i + 1) * 128], rhs=KTp[:, 2 * h + dc, :], start=(dc == 0), stop=(dc == 1)),
                     reads=[("Y", "qT", h % 2, dc, t0b), ("Y", "KTp", 2 * h + dc)], writes=pbk(b))
            return b

        for _ in proj_gen(0):
            pass
        for h in range(4):
            sigC = sigC2[h % 2]
            gen = proj_gen(h + 1) if h + 1 < 4 else None
            if h == 3:
                for s in range(NS):
                    P.op("pool", lambda e, s=s: e.tensor_copy(out=qmask[:, s, :, s * ST:(s + 1) * ST], in_=qTs[:, :, s * ST:(s + 1) * ST]), reads=[("Y", "qTs"), ("Y", "qmask")], writes=[("Y", "qmask")])
            nt = TP // 128
            sb = {0: scores(h, 0)}
            for i in range(nt):
                b = sb.pop(i)
                pe_, pek = Pe[i % 2], ("Y", "Pe", i % 2)
                pn_, pnk = Pn[i % 2], ("Y", "Pn", i % 2)
                mi = i % 2
                P.op("dve", lambda e, b=b, mi=mi: e.tensor_reduce(out=smx[:, 0, mi:mi + 1], in_=pb(b)[:, 0:NMEM], axis=AX.X, op=ALU.max), reads=pbk(b), writes=[("Y", "mx", mi)])
                P.op("dve", lambda e, mi=mi: e.tensor_scalar(out=smx[:, 1, mi:mi + 1], in0=smx[:, 0, mi:mi + 1], scalar1=-1.0, scalar2=0.0, op0=ALU.mult, op1=ALU.add), reads=[("Y", "mx", mi)], writes=[("Y", "nmx", mi)])
                P.op("act", lambda e, b=b, pe_=pe_, mi=mi: e.activation(out=pe_[:, :], in_=pb(b)[:, 0:NMEM], func=AF.Exp, bias=smx[:, 1, mi:mi + 1], scale=1.0, accum_out=smx[:, 2, mi:mi + 1]),
                     reads=pbk(b) + [("Y", "nmx", mi)], writes=[pek, ("Y", "rs", mi)])
                reserved.discard(b)
                P.op("dve", lambda e, mi=mi: e.reciprocal(out=smx[:, 3, mi:mi + 1], in_=smx[:, 2, mi:mi + 1]), reads=[("Y", "rs", mi)], writes=[("Y", "rr", mi)])
                P.op("dve", lambda e, pe_=pe_, pn_=pn_, mi=mi: e.tensor_scalar(out=pn_[:, :], in0=pe_[:, :], scalar1=smx[:, 3, mi:mi + 1], scalar2=0.0, op0=ALU.mult, op1=ALU.add),
                     reads=[pek, ("Y", "rr", mi)], writes=[pnk])
                if i + 1 < nt:
                    sb[i + 1] = scores(h, i + 1)
                pull(gen, 2 if i % 2 == 0 else 1)
                b2 = bank()
                pst = pb(b2).bitcast(BF16)
                for mc in range(2):
                    P.op("pe", lambda e, mc=mc, pst=pst, pn_=pn_: e.transpose(pst[:, mc * 128:(mc + 1) * 128], pn_[:, mc * 128:(mc + 1) * 128], identb[:]),
                         reads=[pnk, "identb"], writes=pbk(b2))
                copy_op("act", PT[:, :, i * 128:(i + 1) * 128], pst[:, 0:256].rearrange("p (m t) -> p m t", m=2), pbk(b2), [("Y", "PT", i)])
            pull(gen, 100)
            for j in range(2):
                c = 2 * h + j
                for bi, (t0, n) in enumerate(PBLK):
                    b = bank()
                    for mc in range(2):
                        P.op("pe", lambda e, mc=mc, b=b, c=c, t0=t0, n=n: e.matmul(pb(b)[:, 0:n], lhsT=Vp[:, mc, c * 128:(c + 1) * 128], rhs=PT[:, mc, t0:t0 + n], start=(mc == 0), stop=(mc == 1)),
                             reads=[("Y", "Vp", mc, c // 4)] + [("Y", "PT", i) for i in range(t0 // 128, (t0 + n) // 128)], writes=pbk(b))
                    tb, tbk = tmpC[ci[0] % 2], ("Y", "tmpC", ci[0] % 2)
                    ci[0] += 1
                    P.op("dve", lambda e, b=b, tb=tb, j=j, t0=t0, n=n, sigC=sigC: e.tensor_tensor(out=tb[:, 0:n], in0=pb(b)[:, 0:n], in1=sigC[j][:, t0:t0 + n], op=ALU.mult),
                         reads=pbk(b) + [("Y", "sigC", h % 2, j, t0)], writes=[tbk])
                    P.op("pool", lambda e, tb=tb, c=c, t0=t0, n=n: e.tensor_tensor(out=merged[:, c, t0:t0 + n], in0=merged[:, c, t0:t0 + n], in1=tb[:, 0:n], op=ALU.add),
                         reads=[tbk] + mk(c, t0), writes=mk(c, t0))
        P.cur = "C_sample"
        Y.reset(keepC2)
        KTs = [Y.alloc([8, NMEM], BF16), Y.alloc([8, NMEM], BF16)]
        Pes = Y.alloc([4, NMEM], BF16)
        Pns = Y.alloc([4, NMEM], BF16)
        PTs = Y.alloc([8, NST], BF16)
        smy = Y.alloc([4, 8], F32)
        bs = bank(2)
        reserved.update((bs, bs + 1))
        for s in range(NS):
            slot, sk = WS.get([(lambda sl: sl[:, 0:2048].rearrange("p (m c) -> p m c", m=2), T["ck"][l, s0 + s].rearrange("(m p) c -> p m c", p=128))])
            kv = slot[:, 0:2048].rearrange("p (m c) -> p m c", m=2)
            kt, ktk = KTs[s % 2], ("Y", "KTs", s % 2)
            bt = bank(2)
            pst = pb(bt, 2).bitcast(BF16)
            for ch in range(8):
                for mc in range(2):
                    P.op("pe", lambda e, ch=ch, mc=mc, pst=pst, kv=kv: e.transpose(pst[:, ch * 256 + mc * 128:ch * 256 + (mc + 1) * 128], kv[:, mc, ch * 128:(ch + 1) * 128], identb[:]),
                         reads=sk + ["identb"], writes=pbk(bt, 2))
            copy_op(evac_eng(), kt[:, :, :], pst[:, :].rearrange("p (c m) -> p c m", c=8), pbk(bt, 2), [ktk])
            for h in range(4):
                for dc in range(2):
                    P.op("pe", lambda e, h=h, dc=dc, s=s, kt=kt: e.matmul(pb(bs, 2)[0:NST, h * 256:(h + 1) * 256], lhsT=qmask[:, s, 2 * h + dc, :], rhs=kt[:, 2 * h + dc, :],
                                                                     start=(s == 0 and dc == 0 and h % 2 == 0), stop=(s == NS - 1 and dc == 1)),
                         reads=[("Y", "qmask"), ktk], writes=pbk(bs, 2))
        sc = pb(bs, 2)[0:NST, :].rearrange("p (h m) -> p h m", h=4)
        P.op("dve", lambda e: e.tensor_reduce(out=smy[0:NST, 0, 0:4], in_=sc, axis=AX.X, op=ALU.max), reads=pbk(bs, 2), writes=[("Y", "mxs")])
        P.op("dve", lambda e: e.tensor_scalar(out=smy[0:NST, 1, 0:4], in0=smy[0:NST, 0, 0:4], scalar1=-1.0, scalar2=0.0, op0=ALU.mult, op1=ALU.add), reads=[("Y", "mxs")], writes=[("Y", "nmxs")])
        for h in range(4):
            P.op("act", lambda e, h=h: e.activation(out=Pes[0:NST, h, :], in_=sc[:, h, :], func=AF.Exp, bias=smy[0:NST, 1, h:h + 1], scale=1.0, accum_out=smy[0:NST, 2, h:h + 1]),
                 reads=pbk(bs, 2) + [("Y", "nmxs")], writes=[("Y", "Pes", h), ("Y", "rss", h)])
        P.op("dve", lambda e: e.reciprocal(out=smy[0:NST, 3, 0:4], in_=smy[0:NST, 2, 0:4]), reads=[("Y", "rss", h) for h in range(4)], writes=[("Y", "rrs")])
        for h in range(4):
            P.op("dve", lambda e, h=h: e.tensor_scalar(out=Pns[0:NST, h, :], in0=Pes[0:NST, h, :], scalar1=smy[0:NST, 3, h:h + 1], scalar2=0.0, op0=ALU.mult, op1=ALU.add),
                 reads=[("Y", "Pes", h), ("Y", "rrs")], writes=[("Y", "Pns", h)])
        b2 = bank()
        reserved.difference_update((bs, bs + 1))
        pst = pb(b2).bitcast(BF16)
        for h in range(4):
            for mc in range(2):
                ch = 2 * h + mc
                P.op("pe", lambda e, h=h, mc=mc, ch=ch, pst=pst: e.transpose(pst[:, ch * NST:(ch + 1) * NST], Pns[0:NST, h, mc * 128:(mc + 1) * 128], identb[0:NST, 0:NST]),
                     reads=[("Y", "Pns", h), "identb"], writes=pbk(b2))
        copy_op("dve", PTs[:, :, :], pst[:, 0:8 * NST].rearrange("p (c t) -> p c t", c=8), pbk(b2), [("Y", "PTs")])
        for s in range(NS):
            P.op("pool", lambda e, s=s: e.tensor_copy(out=PTm[:, s, :, s * ST:(s + 1) * ST], in_=PTs[:, :, s * ST:(s + 1) * ST]), reads=[("Y", "PTs"), ("Y", "PTm")], writes=[("Y", "PTm")])
        bo = bank()
        reserved.add(bo)
        for s in range(NS):
            slot, sk = WS.get([(lambda sl: sl[:, 0:2048].rearrange("p (m c) -> p m c", m=2), T["cv"][l, s0 + s].rearrange("(m p) c -> p m c", p=128))])
            vv = slot[:, 0:2048].rearrange("p (m c) -> p m c", m=2)
            for c in range(8):
                for mc in range(2):
                    P.op("pe", lambda e, c=c, mc=mc, s=s, vv=vv: e.matmul(pb(bo)[:, c * NST:(c + 1) * NST], lhsT=vv[:, mc, c * 128:(c + 1) * 128], rhs=PTm[:, s, (c // 2) * 2 + mc, :],
                                                                     start=(s == 0 and mc == 0 and c == 0), stop=(s == NS - 1 and mc == 1)),
                         reads=sk + [("Y", "PTm")], writes=pbk(bo))
        reserved.discard(bo)
        tcs = Y.alloc([8, NST], F32)
        P.op("dve", lambda e: e.tensor_tensor(out=tcs[:, :, :], in0=pb(bo)[:, 0:8 * NST].rearrange("p (c t) -> p c t", c=8), in1=sigCs[:, :, :], op=ALU.mult),
             reads=pbk(bo) + [("Y", "sigCs")], writes=[("Y", "tcs")])
        P.op("pool", lambda e: e.tensor_tensor(out=merged[:, :, TP:NTOK], in0=merged[:, :, TP:NTOK], in1=tcs[:, :, :], op=ALU.add),
             reads=[("Y", "tcs")] + [("BIG", "m", c, TP) for c in range(8)], writes=[("BIG", "m", c, TP) for c in range(8)])

        P.cur = "wout"
        for k in range(8):
            for (t0, n) in TBLK:
                copy_op(evac_eng(), hT[:, k, t0:t0 + n], merged[:, k, t0:t0 + n], mk(k, t0), hk(k, t0))
        for q in range(2):
            slot, sk = WS.get([(kx_view(512, 0, 512), T["w_out"][l][:, q * 512:(q + 1) * 512].rearrange("(k p) c -> p k c", p=128))])
            for cc in range(4):
                c = 4 * q + cc
                for (t0, n) in TBLK:
                    b = proj_kx(slot, sk, cc * 128, (t0, n), h_fn, hk)
                    P.op("dve", lambda e, b=b, c=c, t0=t0, n=n: e.tensor_tensor(out=xT[:, c, t0:t0 + n], in0=pb(b)[:, 0:n], in1=xT[:, c, t0:t0 + n], op=ALU.add),
                         reads=pbk(b) + xk(c, t0), writes=xk(c, t0))

    for p in range(DBG_PASSES):
        P.cur = "xload"
        srcs = [(T["xp"][p * TP + i * 128:p * TP + (i + 1) * 128, :], 128, i * 128) for i in range(TP // 128)]
        srcs.append((T["xs"][p * NST:(p + 1) * NST, :], NST, TP))
        for (src, nr, c0) in srcs:
            t_io, kio = ioslot()
            P.dma("sp", lambda e, t_io=t_io, src=src, nr=nr: e.dma_start(out=t_io[0:nr, :], in_=src), kio, writes=[kio])
            for hlf in range(2):
                b = bank()
                for kk in range(4):
                    k = hlf * 4 + kk
                    P.op("pe", lambda e, k=k, kk=kk, b=b, t_io=t_io, nr=nr: e.transpose(pb(b)[:, kk * nr:(kk + 1) * nr], t_io[0:nr, k * 128:(k + 1) * 128], identf[0:nr, 0:nr]),
                         reads=[kio, "identf"], writes=pbk(b))
                t0b = min(c0 // 512 * 512, TP)
                copy_op(evac_eng(), xT[:, hlf * 4:(hlf + 1) * 4, c0:c0 + nr], pb(b)[:, 0:4 * nr].rearrange("p (k f) -> p k f", k=4),
                        pbk(b) + [("x", k, t0b) for k in range(hlf * 4, hlf * 4 + 4)], [("x", k, t0b) for k in range(hlf * 4, hlf * 4 + 4)])
        for l in range(DBG_LAYERS):
            ffn(l, 1)
            if DBG_STAGE >= 2:
                mixer(l, p)
            if DBG_STAGE >= 3:
                ffn(l, 2)
        P.cur = "final"
        Y.reset()
        tmp = norm_tmp()
        BIG.reset()
        yT = BIG.alloc([8, NTOK], F32)
        rmsnorm(x_fn, xk, TBLK, "final_norm", 0, lambda k, t0, n: yT[:, k, t0:t0 + n], lambda k, t0: [("BIG", "y", k, t0)], tmp)
        for i in range(TP // 128):
            t0b = i * 128 // 512 * 512
            transpose_out(lambda k, i=i: yT[:, k, i * 128:(i + 1) * 128], [("BIG", "y", k, t0b) for k in range(8)], 128,
                          [(0, 128, T["o_yp"][p * TP + i * 128:p * TP + (i + 1) * 128, :], ("oy", i % 2))])
        transpose_out(lambda k: yT[:, k, TP:NTOK], [("BIG", "y", k, TP) for k in range(8)], NST,
                      [(0, NST, T["o_ys"][p * NST:(p + 1) * NST, :], ("oys", 0))])
    return P, WS, finals


def build_nc():
    nc = bass.Bass("TRN2", target_bir_lowering=False)
    T = {}

    def din(name, shape):
        T[name] = nc.dram_tensor(name, list(shape), F32, kind="ExternalInput").ap()

    def dout(name, shape):
        T[name] = nc.dram_tensor(name, list(shape), F32, kind="ExternalOutput").ap()

    din("xp", (SEQ, D)); din("xs", (NSEQ * ST, D)); din("mem", (NMEM, D))
    din("spool", (DEPTH, NSEQ, PB, D)); din("sconv", (DEPTH, NSEQ, CB, D))
    din("ck", (DEPTH, NSEQ, NMEM, D)); din("cv", (DEPTH, NSEQ, NMEM, D))
    for nm in VROWS:
        din(nm, (DEPTH, D))
    din("final_norm", (1, D))
    din("ffn1_w_gu", (DEPTH, D, 2 * DFF)); din("ffn1_w_down", (DEPTH, DFF, D))
    din("ffn2_w_gu", (DEPTH, D, 2 * DFF)); din("ffn2_w_down", (DEPTH, DFF, D))
    din("w_in", (DEPTH, D, INC)); din("w_mem_kv", (DEPTH, D, 2 * D)); din("w_pool", (DEPTH, 4, 256, 256))
    din("w_dw", (DEPTH, 31, D)); din("w_pw", (DEPTH, D, D)); din("w_out", (DEPTH, D, D))
    dout("o_yp", (SEQ, D)); dout("o_ys", (NSEQ * ST, D))
    dout("o_pool_p", (DEPTH, PB, D)); dout("o_conv_p", (DEPTH, CB, D))
    dout("o_mk", (DEPTH, NMEM, D)); dout("o_mv", (DEPTH, NMEM, D))
    dout("o_pool_s", (DEPTH, NSEQ, PB, D)); dout("o_conv_s", (DEPTH, NSEQ, CB, D))

    A = {}
    A["xT"] = nc.alloc_sbuf_tensor("xT", [128, 8, NTOK], F32)
    A["hT"] = nc.alloc_sbuf_tensor("hT", [128, 8, NTOK], BF16)
    A["slots"] = [nc.alloc_sbuf_tensor("ws%d" % i, [128, SLOTE], BF16) for i in range(NSLOT)]
    A["BIG_bytes"] = 8 * NTOK * 4
    A["BIG"] = nc.alloc_sbuf_tensor("BIG", [128, A["BIG_bytes"] // 2], BF16)
    A["Y_bytes"] = int(os.environ.get("DBG_YKB", 54)) * 1024
    A["Y"] = nc.alloc_sbuf_tensor("Yar", [128, A["Y_bytes"] // 2], BF16)
    A["identf"] = nc.alloc_sbuf_tensor("identf", [128, 128], F32)
    A["identb"] = nc.alloc_sbuf_tensor("identb", [128, 128], BF16)
    A["onesm"] = nc.alloc_sbuf_tensor("onesm", [128, 128], BF16)
    A["epsT"] = nc.alloc_sbuf_tensor("epsT", [128, 1], F32)
    A["vecT"] = nc.alloc_sbuf_tensor("vecT", [128, 8, 33], F32)
    A["wdwT"] = nc.alloc_sbuf_tensor("wdwT", [128, 8, 124], F32)
    A["rcnt"] = nc.alloc_sbuf_tensor("rcnt", [128, 4, 16], F32)
    A["memT"] = nc.alloc_sbuf_tensor("memT", [128, 8, NMEM], F32)
    A["poolst"] = nc.alloc_sbuf_tensor("poolst", [128, DEPTH, 8, PB], F32)
    A["convst"] = nc.alloc_sbuf_tensor("convst", [128, DEPTH, 8, CB], F32)
    A["io"] = [nc.alloc_sbuf_tensor("io%d" % i, [128, D], F32) for i in range(2)]
    A["vstage"] = A["io"][0]
    A["wstage"] = A["io"][1]
    A["psall"] = nc.alloc_psum_tensor("psall", [128, 4096], F32)
    T["alloc"] = A
    Pd, WSd, _ = build_program(nc, T, None)
    plan = WSd.rec
    P, WS, finals = build_program(nc, T, plan)
    assert len(WS.rec) == len(plan)
    P.emit(nc, finals)
    nc._dbgP = P
    return nc


_NC_CACHE = {}


def kernel(**inp):
    f = lambda a: np.ascontiguousarray(np.asarray(a, dtype=np.float32))
    if "nc" not in _NC_CACHE:
        _NC_CACHE["nc"] = build_nc()
    nc = _NC_CACHE["nc"]
    shared = {nm: f(inp[nm]) for nm in VROWS}
    shared["final_norm"] = f(inp["final_norm"]).reshape(1, D)
    for nm in ("ffn1_w_gu", "ffn1_w_down", "ffn2_w_gu", "ffn2_w_down", "w_in", "w_mem_kv", "w_pool", "w_dw", "w_pw", "w_out"):
        shared[nm] = f(inp[nm])
    xp, xs, mem = f(inp["x_prompt"]), f(inp["x_sample"]), f(inp["mem_prompt"])
    sp, sc = f(inp["state_pool"]), f(inp["state_conv"])
    ck = f(inp["cache_mem_k"]).reshape(DEPTH, 128, NMEM, D)
    cv = f(inp["cache_mem_v"]).reshape(DEPTH, 128, NMEM, D)
    in_maps = []
    for c in range(NCORES):
        m = dict(shared)
        sl = slice(c * NSEQ, (c + 1) * NSEQ)
        m["xp"] = xp[c]
        m["xs"] = np.ascontiguousarray(xs[sl].reshape(NSEQ * ST, D))
        m["mem"] = mem[c]
        m["spool"] = np.ascontiguousarray(sp[:, sl])
        m["sconv"] = np.ascontiguousarray(sc[:, sl])
        m["ck"] = np.ascontiguousarray(ck[:, sl])
        m["cv"] = np.ascontiguousarray(cv[:, sl])
        in_maps.append(m)
    res = run_bass_kernel_spmd(nc, in_maps[:DBG_CORES], core_ids=list(range(DBG_CORES)))
    R = list(res.results)
    while len(R) < NCORES:
        R.append(R[0])
    y_p = np.stack([R[c]["o_yp"] for c in range(NCORES)], 0)
    y_s = np.concatenate([R[c]["o_ys"].reshape(NSEQ, ST, D) for c in range(NCORES)], 0)
    pool_p = np.stack([R[c]["o_pool_p"] for c in range(NCORES)], 1)
    conv_p = np.stack([R[c]["o_conv_p"] for c in range(NCORES)], 1)
    mk = np.stack([R[c]["o_mk"] for c in range(NCORES)], 1).reshape(DEPTH, NCORES, NMEM, 4, 256)
    mv = np.stack([R[c]["o_mv"] for c in range(NCORES)], 1).reshape(DEPTH, NCORES, NMEM, 4, 256)
    pool_s = np.concatenate([R[c]["o_pool_s"] for c in range(NCORES)], 1)
    conv_s = np.concatenate([R[c]["o_conv_s"] for c in range(NCORES)], 1)
    return (y_p.astype(np.float32), y_s.astype(np.float32), pool_p.astype(np.float32), conv_p.astype(np.float32),
            mk.astype(np.float32), mv.astype(np.float32), pool_s.astype(np.float32), conv_s.astype(np.float32))
```

```python
import contextlib
import numpy as np
import concourse.bass as bass
import concourse.mybir as mybir
from concourse.bass_utils import run_bass_kernel_spmd

F32 = mybir.dt.float32
BF16 = mybir.dt.bfloat16
AF = mybir.ActivationFunctionType
ALU = mybir.AluOpType
AX = mybir.AxisListType

DEPTH = 4
NCORES = 8
D = 1024
DFF = 2816
NFF = 22
SEQ = 2048
NPASS = 2
TP = SEQ // NPASS
NSEQ = 16
NS = NSEQ // NPASS
ST = 8
NST = NS * ST
NTOK = TP + NST
NMEM = 256
PB, CB = 15, 30
INC = 7168
GL = 4096
EPS = 1e-6
NSLOT = 4
SLOTE = 4096
TBLK = [(0, 512), (512, 512), (1024, NST)]
PBLK = [(0, 512), (512, 512)]
PL = PB + TP
PW = PB + ST
CL = CB + TP
CW = CB + ST
ENGS = ("pe", "act", "dve", "pool", "sp")
import os
DBG_LAYERS = int(os.environ.get("DBG_LAYERS", DEPTH))
DBG_STAGE = int(os.environ.get("DBG_STAGE", 3))
DBG_PASSES = int(os.environ.get("DBG_PASSES", NPASS))
DBG_CORES = int(os.environ.get("DBG_CORES", NCORES))
VROWS = ["ffn1_norm", "mix_norm", "pool_scale", "b_dw", "conv_ln_g", "conv_ln_b", "ffn2_norm", "mem_norm"]


class Op:
    __slots__ = ("eng", "fn", "deps", "is_dma", "sem", "semval", "signal", "cnt", "idx", "lab")

    def __init__(self, eng, fn, is_dma=False):
        self.eng, self.fn, self.is_dma = eng, fn, is_dma
        self.deps = []
        self.sem = None
        self.semval = 0
        self.signal = False
        self.cnt = 0
        self.idx = 0


class Prog:
    def __init__(self, arenas=()):
        self.ops = {e: [] for e in ENGS}
        self.last_w = {}
        self.readers = {}
        self.dma_cnt = {}
        self.dma_last = {}
        self.arenas = set(arenas)
        self.arena_ops = {a: {} for a in arenas}
        self.fence = {a: {} for a in arenas}
        self.nops = 0
        self.cur = ""

    @staticmethod
    def _slot(op):
        return ("d", op.sem) if op.is_dma else ("e", op.eng)

    def _add(self, table, op):
        k = self._slot(op)
        o = table.get(k)
        if o is None or o.idx < op.idx:
            table[k] = op

    def _track(self, op, reads, writes):
        deps = {}

        def add(d):
            if d is op:
                return
            if d.eng == "pe" and op.eng == "pe" and not d.is_dma and not op.is_dma:
                return
            k = self._slot(d)
            o = deps.get(k)
            if o is None or o.idx < d.idx:
                deps[k] = d

        for k in list(reads) + list(writes):
            if isinstance(k, tuple) and k[0] in self.arenas:
                for d in self.fence[k[0]].values():
                    add(d)
        for k in reads:
            w = self.last_w.get(k)
            if w is not None:
                add(w)
            if isinstance(k, tuple) and k[0] == "pb":
                for r in self.readers.get(k, {}).values():
                    if r.eng != op.eng:
                        add(r)
        for k in writes:
            w = self.last_w.get(k)
            if w is not None:
                add(w)
            for r in self.readers.get(k, {}).values():
                add(r)
        op.deps = list(deps.values())
        for k in reads:
            self._add(self.readers.setdefault(k, {}), op)
        for k in writes:
            self.last_w[k] = op
            self.readers[k] = {}
        for k in list(reads) + list(writes):
            if isinstance(k, tuple) and k[0] in self.arenas:
                self._add(self.arena_ops[k[0]], op)

    def op(self, eng, fn, reads=(), writes=()):
        o = Op(eng, fn)
        o.lab = self.cur
        self.nops += 1
        o.idx = self.nops
        self._track(o, reads, writes)
        self.ops[eng].append(o)
        return o

    def dma(self, eng, fn, semkey, reads=(), writes=()):
        o = Op(eng, fn, True)
        o.lab = self.cur
        self.nops += 1
        o.idx = self.nops
        o.sem = semkey
        self._track(o, reads, writes)
        prev = self.dma_last.get(semkey)
        if prev is not None and all(d is not prev for d in o.deps):
            o.deps.append(prev)
        self.dma_cnt[semkey] = self.dma_cnt.get(semkey, 0) + 16
        o.semval = self.dma_cnt[semkey]
        self.dma_last[semkey] = o
        self.ops[eng].append(o)
        return o

    def handoff(self, arena):
        f = dict(self.fence[arena])
        for op in self.arena_ops[arena].values():
            self._add(f, op)
        self.fence[arena] = f
        self.arena_ops[arena] = {}
        for tab in (self.last_w, self.readers):
            for k in [k for k in tab if isinstance(k, tuple) and k[0] == arena]:
                del tab[k]

    def emit(self, nc, final_ops):
        for e in ENGS:
            for o in self.ops[e]:
                for d in o.deps:
                    if not d.is_dma:
                        d.signal = True
        for o in final_ops:
            if not o.is_dma:
                o.signal = True
        for e in ENGS:
            c = 0
            for o in self.ops[e]:
                if not o.is_dma and o.signal:
                    c += 1
                    o.cnt = c
        keys = sorted(self.dma_cnt.keys(), key=str)
        with contextlib.ExitStack() as st:
            esem = {e: st.enter_context(nc.semaphore("s_" + e)) for e in ENGS}
            dsem = {k: st.enter_context(nc.semaphore("d%d" % i)) for i, k in enumerate(keys)}
            block = st.enter_context(nc.Block())

            def mk(e):
                def body(engine):
                    waited = {}

                    def wait(d):
                        if d.is_dma:
                            key, val, sem = ("d", d.sem), d.semval, dsem[d.sem]
                        else:
                            key, val, sem = ("e", d.eng), d.cnt, esem[d.eng]
                        if waited.get(key, 0) >= val:
                            return
                        engine.wait_ge(sem, val)
                        waited[key] = val

                    for o in self.ops[e]:
                        for d in o.deps:
                            wait(d)
                        ins = o.fn(engine)
                        if o.is_dma:
                            ins.then_inc(dsem[o.sem], 16)
                        elif o.signal:
                            ins.then_inc(esem[e], 1)
                    if e == "sp":
                        for o in final_ops:
                            wait(o)
                return body

            block.tensor(mk("pe"))
            block.scalar(mk("act"))
            block.vector(mk("dve"))
            block.gpsimd(mk("pool"))
            block.sync(mk("sp"))


class WStream:
    def __init__(self, P, slots, plan):
        self.P, self.slots, self.plan = P, slots, plan
        self.rec = []
        self.nreg = 0
        self.i = 0

    def _register(self, j):
        s = j % NSLOT
        for di, (dstfn, src) in enumerate(self.plan[j]):
            dst = dstfn(self.slots[s])
            self.P.dma("pool", lambda e, dst=dst, src=src: e.dma_start(out=dst, in_=src),
                       ("w", s, di), writes=[("ws", s, di)])

    def get(self, dmas):
        j = self.i
        self.i += 1
        self.rec.append(dmas)
        if self.plan is not None:
            lim = min(j + NSLOT - 1, len(self.plan) - 1)
            while self.nreg <= lim:
                self._register(self.nreg)
                self.nreg += 1
        else:
            s = j % NSLOT
            for di, (dstfn, src) in enumerate(dmas):
                dst = dstfn(self.slots[s])
                self.P.dma("pool", lambda e, dst=dst, src=src: e.dma_start(out=dst, in_=src),
                           ("w", s, di), writes=[("ws", s, di)])
        s = j % NSLOT
        return self.slots[s], [("ws", s, di) for di in range(len(dmas))]


def kx_view(cols, off=0, width=None):
    def f(slot, cols=cols, off=off, width=width):
        W = width
        v = slot[:, 0:8 * W].rearrange("p (k c) -> p k c", k=8)
        return v[:, :, off:off + cols]
    return f


def build_program(nc, T, plan):
    P = Prog(arenas=("BIG", "Y"))
    A = T["alloc"]
    xT, hT, slots, BIGt, Yt = A["xT"], A["hT"], A["slots"], A["BIG"], A["Y"]
    identf, identb, onesm, epsT, vecT, wdwT, rcnt = (A[k] for k in ("identf", "identb", "onesm", "epsT", "vecT", "wdwT", "rcnt"))
    memT, poolst, convst, io, psall, vstage, wstage = (A[k] for k in ("memT", "poolst", "convst", "io", "psall", "vstage", "wstage"))
    WS = WStream(P, slots, plan)
    finals = []

    st = {"b": 0, "io": 0, "act": 0}

    reserved = set()

    def bank(n=1):
        b = st["b"]
        while True:
            if n == 2 and b % 2:
                b += 1
            if b + n > 8:
                b = 0
            if all((b + i) not in reserved for i in range(n)):
                break
            b += 1
        st["b"] = (b + n) % 8
        return b

    def pb(b, n=1):
        return psall[:, b * 512:(b + n) * 512]

    def pbk(b, n=1):
        return [("pb", b + i) for i in range(n)]

    def ioslot():
        s = st["io"]
        st["io"] = 1 - s
        return io[s], ("io", s)

    class Arena:
        def __init__(self, name, tile, nbytes):
            self.name, self.tile, self.nbytes, self.off = name, tile, nbytes, 0

        def reset(self, keep=0):
            P.handoff(self.name)
            self.off = keep

        def alloc(self, shape, dt):
            esz = 4 if dt == F32 else 2
            n = int(np.prod(shape))
            nb = (n * esz + 31) // 32 * 32
            assert self.off + nb <= self.nbytes, (self.name, self.off, nb, self.nbytes)
            e0 = self.off // 2
            v = self.tile[:, e0:e0 + nb // 2]
            self.off += nb
            if dt == F32:
                v = v.bitcast(F32)
            v = v[:, 0:n]
            if len(shape) == 2:
                v = v.rearrange("p (a b) -> p a b", a=shape[0])
            elif len(shape) == 3:
                v = v.rearrange("p (a b c) -> p a b c", a=shape[0], b=shape[1])
            return v

    BIG = Arena("BIG", BIGt, A["BIG_bytes"])
    Y = Arena("Y", Yt, A["Y_bytes"])

    def pull(gen, k):
        if gen is None:
            return
        for _ in range(k):
            try:
                next(gen)
            except StopIteration:
                return

    def evac_eng():
        st["act"] ^= 1
        return "act" if st["act"] else "dve"

    def copy_op(eng, out, in_, reads, writes):
        if eng == "act":
            return P.op("act", lambda e: e.copy(out=out, in_=in_), reads=reads, writes=writes)
        return P.op(eng, lambda e: e.tensor_copy(out=out, in_=in_), reads=reads, writes=writes)

    P.op("pool", lambda e: e.memset(identf[:], 0.0), writes=["identf"])
    P.op("pool", lambda e: e.affine_select(out=identf[:], in_=identf[:], pattern=[[-1, 128]], compare_op=ALU.not_equal,
                                           fill=1.0, base=0, channel_multiplier=1), reads=["identf"], writes=["identf"])
    P.op("dve", lambda e: e.tensor_copy(out=identb[:], in_=identf[:]), reads=["identf"], writes=["identb"])
    P.op("pool", lambda e: e.memset(onesm[:], 1.0 / 1024.0), writes=["onesm"])
    P.op("pool", lambda e: e.memset(epsT[:], EPS), writes=["epsT"])
    P.op("pool", lambda e: e.memset(poolst[:], 0.0), writes=["poolst"])
    P.op("pool", lambda e: e.memset(convst[:], 0.0), writes=["convst"])
    for g in range(4):
        w = 2 << g
        P.op("pool", lambda e, g=g, w=w: e.memset(rcnt[:, g, :], 1.0 / w), reads=["rcnt"], writes=["rcnt"])
        for t in range(w - 1):
            P.op("pool", lambda e, g=g, t=t: e.memset(rcnt[:, g, t:t + 1], 1.0 / (t + 1)), reads=["rcnt"], writes=["rcnt"])
    for i, nm in enumerate(VROWS):
        P.dma("sp", lambda e, i=i, nm=nm: e.dma_start(out=vstage[i * 4:(i + 1) * 4, :], in_=T[nm]), ("vs", i), writes=[("io", 0)])
    P.dma("sp", lambda e: e.dma_start(out=vstage[32:33, :], in_=T["final_norm"]), ("vs", 8), writes=[("io", 0)])
    b = bank()
    for k in range(8):
        P.op("pe", lambda e, k=k, b=b: e.transpose(pb(b)[:, k * 64:k * 64 + 33], vstage[0:33, k * 128:(k + 1) * 128], identf[0:33, 0:33]),
             reads=[("io", 0), "identf"], writes=pbk(b))
    P.op("dve", lambda e, b=b: e.tensor_copy(out=vecT[:], in_=pb(b).rearrange("p (k f) -> p k f", k=8)[:, :, 0:33]), reads=pbk(b), writes=["vecT"])
    P.dma("sp", lambda e: e.dma_start(out=wstage[0:124, :], in_=T["w_dw"].rearrange("l j c -> (l j) c")), ("vs", 9), writes=[("io", 1)])
    for hlf in range(2):
        b = bank()
        for kk in range(4):
            k = hlf * 4 + kk
            P.op("pe", lambda e, k=k, kk=kk, b=b: e.transpose(pb(b)[:, kk * 124:(kk + 1) * 124], wstage[0:124, k * 128:(k + 1) * 128], identf[0:124, 0:124]),
                 reads=[("io", 1), "identf"], writes=pbk(b))
        P.op("dve", lambda e, b=b, hlf=hlf: e.tensor_copy(out=wdwT[:, hlf * 4:(hlf + 1) * 4, :], in_=pb(b)[:, 0:496].rearrange("p (k f) -> p k f", k=4)),
             reads=pbk(b), writes=["wdwT"])
    for mt in range(2):
        t_io, kio = ioslot()
        P.dma("sp", lambda e, mt=mt, t_io=t_io: e.dma_start(out=t_io[:, :], in_=T["mem"][mt * 128:(mt + 1) * 128, :]), kio, writes=[kio])
        for hlf in range(2):
            b = bank()
            for kk in range(4):
                k = hlf * 4 + kk
                P.op("pe", lambda e, k=k, kk=kk, b=b, t_io=t_io: e.transpose(pb(b)[:, kk * 128:(kk + 1) * 128], t_io[:, k * 128:(k + 1) * 128], identf[:]),
                     reads=[kio, "identf"], writes=pbk(b))
            P.op("dve", lambda e, b=b, hlf=hlf, mt=mt: e.tensor_copy(out=memT[:, hlf * 4:(hlf + 1) * 4, mt * 128:(mt + 1) * 128],
                                                                    in_=pb(b).rearrange("p (k f) -> p k f", k=4)), reads=pbk(b), writes=["memT"])

    def vcol(name, l, k):
        r = 32 if name == "final_norm" else VROWS.index(name) * 4 + l
        return vecT[:, k, r:r + 1]

    def rmsnorm(src_fn, src_keys_fn, blks, gname, l, dst_fn, dst_keys_fn, tmp):
        for (t0, n) in blks:
            b = bank()
            for k in range(8):
                sq = tmp["sq"][k % 2]
                sk = ("Y", "sq", k % 2)
                P.op("act", lambda e, k=k, sq=sq, t0=t0, n=n: e.activation(out=sq[:, 0:n], in_=src_fn(k, t0, n), func=AF.Square),
                     reads=src_keys_fn(k, t0), writes=[sk])
                P.op("pe", lambda e, k=k, sq=sq, n=n, b=b: e.matmul(pb(b)[:, 0:n], lhsT=onesm[:], rhs=sq[:, 0:n], start=(k == 0), stop=(k == 7)),
                     reads=[sk, "onesm"], writes=pbk(b))
            rt, rstd = tmp["rt"], tmp["rstd"]
            P.op("act", lambda e, n=n, b=b, rt=rt: e.activation(out=rt[:, 0:n], in_=pb(b)[:, 0:n], func=AF.Sqrt, bias=epsT[:, 0:1], scale=1.0),
                 reads=pbk(b) + ["epsT"], writes=[("Y", "rt")])
            P.op("dve", lambda e, n=n, rt=rt, rstd=rstd: e.reciprocal(out=rstd[:, 0:n], in_=rt[:, 0:n]), reads=[("Y", "rt")], writes=[("Y", "rstd")])
            for k in range(8):
                P.op("dve", lambda e, k=k, t0=t0, n=n, rstd=rstd: e.scalar_tensor_tensor(
                    out=dst_fn(k, t0, n), in0=src_fn(k, t0, n), scalar=vcol(gname, l, k), in1=rstd[:, 0:n], op0=ALU.mult, op1=ALU.mult),
                    reads=src_keys_fn(k, t0) + [("Y", "rstd"), "vecT"], writes=dst_keys_fn(k, t0))

    def norm_tmp():
        return {"sq": [Y.alloc([512], BF16), Y.alloc([512], BF16)], "rt": Y.alloc([512], F32), "rstd": Y.alloc([512], F32)}

    xk = lambda k, t0: [("x", k, t0)]
    hk = lambda k, t0: [("h", k, t0)]
    x_fn = lambda k, t0, n: xT[:, k, t0:t0 + n]
    h_fn = lambda k, t0, n: hT[:, k, t0:t0 + n]

    def proj_kx(slot, skeys, off, blk, rhs_fn, rhs_keys_fn, nk=8, W=512):
        t0, n = blk
        b = bank()
        v = slot[:, 0:nk * W].rearrange("p (k c) -> p k c", k=nk)
        for k in range(nk):
            P.op("pe", lambda e, k=k, b=b, v=v: e.matmul(pb(b)[:, 0:n], lhsT=v[:, k, off:off + 128], rhs=rhs_fn(k, t0, n),
                                                        start=(k == 0), stop=(k == nk - 1)),
                 reads=skeys + rhs_keys_fn(k, t0), writes=pbk(b))
        return b

    def ffn(l, which):
        wgu, wdn = T["ffn%d_w_gu" % which][l], T["ffn%d_w_down" % which][l]
        P.cur = "ffn%d" % which
        Y.reset()
        tmp = norm_tmp()
        rmsnorm(x_fn, xk, TBLK, "ffn%d_norm" % which, l, h_fn, hk, tmp)
        sg = [Y.alloc([512], BF16), Y.alloc([512], BF16)]
        BIG.reset()
        hid = BIG.alloc([12, NTOK], BF16)
        for (f0, nf) in ((0, 12), (12, 10)):
            for pi in range(nf // 2):
                fa = f0 + 2 * pi
                slot, sk = WS.get([(kx_view(256, 0, 512), wgu[:, fa * 128:fa * 128 + 256].rearrange("(k p) c -> p k c", p=128)),
                                   (kx_view(256, 256, 512), wgu[:, DFF + fa * 128:DFF + fa * 128 + 256].rearrange("(k p) c -> p k c", p=128))])
                for j in range(2):
                    fi = 2 * pi + j
                    for bi, blk in enumerate(TBLK):
                        t0, n = blk
                        bg = proj_kx(slot, sk, j * 128, blk, h_fn, hk)
                        bu = proj_kx(slot, sk, 256 + j * 128, blk, h_fn, hk)
                        s = sg[(fi * 3 + bi) % 2]
                        skk = ("Y", "sg", (fi * 3 + bi) % 2)
                        P.op("act", lambda e, bg=bg, n=n, s=s: e.activation(out=s[:, 0:n], in_=pb(bg)[:, 0:n], func=AF.Silu), reads=pbk(bg), writes=[skk])
                        P.op("dve", lambda e, bu=bu, n=n, s=s, fi=fi, t0=t0: e.tensor_tensor(out=hid[:, fi, t0:t0 + n], in0=pb(bu)[:, 0:n], in1=s[:, 0:n], op=ALU.mult),
                             reads=pbk(bu) + [skk], writes=[("BIG", "hid", fi, t0)])
            for q in range(4):
                slot, sk = WS.get([(lambda sl, nf=nf: sl[:, 0:nf * 256].rearrange("p (f c) -> p f c", f=nf),
                                    wdn[f0 * 128:(f0 + nf) * 128, q * 256:(q + 1) * 256].rearrange("(f p) c -> p f c", p=128))])
                v = slot[:, 0:nf * 256].rearrange("p (f c) -> p f c", f=nf)
                for cc in range(2):
                    c = 2 * q + cc
                    for (t0, n) in TBLK:
                        b = bank()
                        for fi in range(nf):
                            P.op("pe", lambda e, fi=fi, b=b, v=v, cc=cc, t0=t0, n=n: e.matmul(pb(b)[:, 0:n], lhsT=v[:, fi, cc * 128:(cc + 1) * 128], rhs=hid[:, fi, t0:t0 + n],
                                                                                            start=(fi == 0), stop=(fi == nf - 1)),
                                 reads=sk + [("BIG", "hid", fi, t0)], writes=pbk(b))
                        P.op("dve", lambda e, b=b, c=c, t0=t0, n=n: e.scalar_tensor_tensor(out=xT[:, c, t0:t0 + n], in0=pb(b)[:, 0:n], scalar=0.5, in1=xT[:, c, t0:t0 + n],
                                                                                           op0=ALU.mult, op1=ALU.add),
                             reads=pbk(b) + xk(c, t0), writes=xk(c, t0))

    def transpose_out(src_fn, src_keys, nrow, dst_fn_list):
        b = bank(2)
        for k in range(8):
            P.op("pe", lambda e, k=k, b=b: e.transpose(pb(b, 2)[0:nrow, k * 128:(k + 1) * 128], src_fn(k), identf[:]),
                 reads=src_keys + ["identf"], writes=pbk(b, 2))
        t_io, kio = ioslot()
        copy_op(evac_eng(), t_io[0:nrow, :], pb(b, 2)[0:nrow, :], pbk(b, 2), [kio])
        outs = []
        for (r0, nr, dst, semk) in dst_fn_list:
            outs.append(P.dma("sp", lambda e, r0=r0, nr=nr, dst=dst, t_io=t_io: e.dma_start(out=dst, in_=t_io[r0:r0 + nr, :]), semk, reads=[kio]))
        finals.extend(outs)

    def load_rows_T(src_ap, nrow, dst_fn, dst_keys):
        t_io, kio = ioslot()
        P.dma("sp", lambda e, t_io=t_io: e.dma_start(out=t_io[0:nrow, :], in_=src_ap), kio, writes=[kio])
        for hlf in range(2):
            b = bank()
            for kk in range(4):
                k = hlf * 4 + kk
                P.op("pe", lambda e, k=k, kk=kk, b=b, t_io=t_io: e.transpose(pb(b)[:, kk * nrow:(kk + 1) * nrow], t_io[0:nrow, k * 128:(k + 1) * 128], identf[0:nrow, 0:nrow]),
                     reads=[kio, "identf"], writes=pbk(b))
            copy_op(evac_eng(), dst_fn(hlf), pb(b)[:, 0:4 * nrow].rearrange("p (k f) -> p k f", k=4), pbk(b), dst_keys)

    def mixer(l, p):
        last = (p == NPASS - 1)
        win = T["w_in"][l]
        s0 = p * NS
        P.cur = "mixnorm"
        Y.reset()
        tmp = norm_tmp()
        rmsnorm(x_fn, xk, TBLK, "mix_norm", l, h_fn, hk, tmp)
        BIG.reset()
        merged = BIG.alloc([8, NTOK], F32)
        mk = lambda c, t0: [("BIG", "m", c, t0)]

        def win_piece(c0a, c0b):
            return WS.get([(kx_view(256, 0, 512), win[:, c0a:c0a + 256].rearrange("(k p) c -> p k c", p=128)),
                           (kx_view(256, 256, 512), win[:, c0b:c0b + 256].rearrange("(k p) c -> p k c", p=128))])

        P.cur = "A"
        Y.reset()
        spoolT = Y.alloc([8, NS * PB], F32)
        ustage = Y.alloc([8, NST], F32)
        LA = PL + NS * PW
        uext = Y.alloc([LA], F32)
        sAB = [Y.alloc([LA], F32), Y.alloc([LA], F32)]
        pooled2 = [Y.alloc([2, NTOK], BF16), Y.alloc([2, NTOK], BF16)]
        sigA2 = [[Y.alloc([NTOK], BF16), Y.alloc([NTOK], BF16)] for _ in range(2)]
        t15 = Y.alloc([16], F32)
        load_rows_T(T["spool"][l, s0:s0 + NS].rearrange("s r c -> (s r) c"), NS * PB,
                    lambda hlf: spoolT[:, hlf * 4:(hlf + 1) * 4, :], [("Y", "spoolT")])
        finals.append(P.dma("sp", lambda e: e.dma_start(out=T["o_pool_s"][l, s0:s0 + NS, 0:PB - ST, :], in_=T["spool"][l, s0:s0 + NS, ST:PB, :]), ("dd", 0)))
        def stagePA(g):
            w = 2 << g
            pooled, sigA = pooled2[g % 2], sigA2[g % 2]
            slot, sk = win_piece(g * 256, GL + g * 256)
            for j in range(2):
                c = 2 * g + j
                for (t0, n) in TBLK:
                    b = proj_kx(slot, sk, j * 128, (t0, n), h_fn, hk)
                    if t0 < TP:
                        copy_op("act", uext[:, PB + t0:PB + t0 + n], pb(b)[:, 0:n], pbk(b), [("Y", "uext")])
                    else:
                        copy_op("act", uext[:, PL:LA].rearrange("p (s w) -> p s w", w=PW)[:, :, PB:PW],
                                pb(b)[:, 0:NST].rearrange("p (s t) -> p s t", t=ST), pbk(b), [("Y", "uext")])
                P.op("pool", lambda e, c=c: e.tensor_copy(out=uext[:, 0:PB], in_=poolst[:, l, c, :]), reads=[("pst", l, c)], writes=[("Y", "uext")])
                P.op("pool", lambda e, c=c: e.tensor_copy(out=uext[:, PL:LA].rearrange("p (s w) -> p s w", w=PW)[:, :, 0:PB],
                                                          in_=spoolT[:, c, :].rearrange("p (s r) -> p s r", r=PB)), reads=[("Y", "spoolT")], writes=[("Y", "uext")])
                P.op("pool", lambda e, c=c: e.tensor_copy(out=poolst[:, l, c, :], in_=uext[:, TP:TP + PB]), reads=[("Y", "uext")], writes=[("pst", l, c)])
                P.op("pool", lambda e, c=c: e.tensor_copy(out=ustage[:, c, :].rearrange("p (s t) -> p s t", t=ST),
                                                          in_=uext[:, PL:LA].rearrange("p (s w) -> p s w", w=PW)[:, :, PB:PW]), reads=[("Y", "uext")], writes=[("Y", "ustage")])
                src, srck = uext, ("Y", "uext")
                for stp in range(g + 1):
                    sh = 1 << stp
                    dstt, dk = sAB[stp % 2], ("Y", "sAB", stp % 2)
                    P.op("dve", lambda e, sh=sh, src=src, dstt=dstt: e.tensor_tensor(out=dstt[:, sh:LA], in0=src[:, sh:LA], in1=src[:, 0:LA - sh], op=ALU.add),
                         reads=[srck], writes=[dk])
                    src, srck = dstt, dk
                pk = ("Y", "pooled", g % 2, j)
                P.op("dve", lambda e, src=src, j=j, w=w: e.scalar_tensor_tensor(out=pooled[:, j, 0:TP], in0=src[:, PB:PL], scalar=1.0 / w, in1=uext[:, PB:PL],
                                                                               op0=ALU.mult, op1=ALU.subtract), reads=[srck, ("Y", "uext")], writes=[pk])
                if p == 0:
                    P.op("dve", lambda e, src=src, g=g: e.tensor_tensor(out=t15[:, 0:PB], in0=src[:, PB:2 * PB], in1=rcnt[:, g, 0:PB], op=ALU.mult),
                         reads=[srck, "rcnt"], writes=[("Y", "t15")])
                    P.op("dve", lambda e, j=j: e.tensor_tensor(out=pooled[:, j, 0:PB], in0=t15[:, 0:PB], in1=uext[:, PB:2 * PB], op=ALU.subtract),
                         reads=[("Y", "t15"), ("Y", "uext"), pk], writes=[pk])
                P.op("dve", lambda e, src=src, j=j, w=w: e.scalar_tensor_tensor(
                    out=pooled[:, j, TP:NTOK].rearrange("p (s t) -> p s t", t=ST),
                    in0=src[:, PL:LA].rearrange("p (s w) -> p s w", w=PW)[:, :, PB:PW], scalar=1.0 / w,
                    in1=uext[:, PL:LA].rearrange("p (s w) -> p s w", w=PW)[:, :, PB:PW], op0=ALU.mult, op1=ALU.subtract),
                    reads=[srck, ("Y", "uext"), pk], writes=[pk])
            for j in range(2):
                for (t0, n) in TBLK:
                    b = proj_kx(slot, sk, 256 + j * 128, (t0, n), h_fn, hk)
                    P.op("act", lambda e, b=b, j=j, t0=t0, n=n: e.activation(out=sigA[j][:, t0:t0 + n], in_=pb(b)[:, 0:n], func=AF.Sigmoid),
                         reads=pbk(b), writes=[("Y", "sigA", g % 2, j, t0)])
        def stageVA(g):
            pooled, sigA = pooled2[g % 2], sigA2[g % 2]
            wps, wpk = WS.get([(lambda sl: sl[:, 0:512].rearrange("p (k c) -> p k c", k=2),
                                T["w_pool"][l, g].rearrange("(k p) c -> p k c", p=128))])
            wpv = wps[:, 0:512].rearrange("p (k c) -> p k c", k=2)
            for j2 in range(2):
                c = 2 * g + j2
                for (t0, n) in TBLK:
                    b = bank()
                    for kk in range(2):
                        P.op("pe", lambda e, kk=kk, b=b, wpv=wpv, j2=j2, t0=t0, n=n: e.matmul(pb(b)[:, 0:n], lhsT=wpv[:, kk, j2 * 128:(j2 + 1) * 128],
                                                                                        rhs=pooled[:, kk, t0:t0 + n], start=(kk == 0), stop=(kk == 1)),
                             reads=wpk + [("Y", "pooled", g % 2, kk)], writes=pbk(b))
                    P.op("dve", lambda e, b=b, c=c, j2=j2, t0=t0, n=n: e.scalar_tensor_tensor(out=merged[:, c, t0:t0 + n], in0=pb(b)[:, 0:n], scalar=vcol("pool_scale", l, c),
                                                                                             in1=sigA[j2][:, t0:t0 + n], op0=ALU.mult, op1=ALU.mult),
                         reads=pbk(b) + [("Y", "sigA", g % 2, j2, t0), "vecT"], writes=mk(c, t0))
        stagePA(0)
        for g in range(4):
            if g + 1 < 4:
                stagePA(g + 1)
            stageVA(g)
        transpose_out(lambda k: ustage[:, k, :], [("Y", "ustage")], NST,
                      [(s * ST, ST, T["o_pool_s"][l, s0 + s, PB - ST:PB, :], ("ops", s)) for s in range(NS)])
        if last:
            transpose_out(lambda k: poolst[:, l, k, :], [("pst", l, k) for k in range(8)], PB, [(0, PB, T["o_pool_p"][l], ("opp", 0))])

        P.cur = "B"
        Y.reset()
        yb = Y.alloc([8, NTOK], BF16)
        keepB = Y.off
        zstage = Y.alloc([8, NST], F32)
        sconvT = Y.alloc([8, NS * CB], F32)
        LB = CL + NS * CW
        zc = [Y.alloc([LB], BF16), Y.alloc([LB], BF16)]
        diag2 = [Y.alloc([31, 128], BF16), Y.alloc([31, 128], BF16)]
        sz = Y.alloc([NTOK], BF16)
        acc = Y.alloc([NTOK], F32)
        for hf in range(2):
            load_rows_T(T["sconv"][l, s0 + hf * 4:s0 + hf * 4 + 4].rearrange("s r c -> (s r) c"), 4 * CB,
                        lambda hlf, hf=hf: sconvT[:, hlf * 4:(hlf + 1) * 4, hf * 4 * CB:(hf + 1) * 4 * CB], [("Y", "sconvT")])
        finals.append(P.dma("sp", lambda e: e.dma_start(out=T["o_conv_s"][l, s0:s0 + NS, 0:CB - ST, :], in_=T["sconv"][l, s0:s0 + NS, ST:CB, :]), ("dd", 1)))
        pieceB = {}

        def stagePB(c):
            q, j = c // 2, c % 2
            if j == 0:
                pieceB["s"] = win_piece(1024 + q * 256, 2048 + q * 256)
            slot, sk = pieceB["s"]
            zt, zk = zc[c % 2], ("Y", "zc", c % 2)
            diag, dgk = diag2[c % 2], ("Y", "diag", c % 2)
            for (t0, n) in TBLK:
                bgt = proj_kx(slot, sk, 256 + j * 128, (t0, n), h_fn, hk)
                P.op("act", lambda e, bgt=bgt, t0=t0, n=n: e.activation(out=sz[:, t0:t0 + n], in_=pb(bgt)[:, 0:n], func=AF.Sigmoid), reads=pbk(bgt), writes=[("Y", "sz", t0)])
                ba = proj_kx(slot, sk, j * 128, (t0, n), h_fn, hk)
                if t0 < TP:
                    P.op("dve", lambda e, ba=ba, zt=zt, t0=t0, n=n: e.tensor_tensor(out=zt[:, CB + t0:CB + t0 + n], in0=pb(ba)[:, 0:n], in1=sz[:, t0:t0 + n], op=ALU.mult),
                         reads=pbk(ba) + [("Y", "sz", t0)], writes=[zk])
                    if t0 + n == TP:
                        P.op("pool", lambda e, zt=zt, c=c: e.tensor_copy(out=zt[:, 0:CB], in_=convst[:, l, c, :]), reads=[("cst", l, c)], writes=[zk])
                        P.op("dve", lambda e, ba=ba, c=c, n=n: e.tensor_tensor(out=convst[:, l, c, :], in0=pb(ba)[:, n - CB:n], in1=sz[:, TP - CB:TP], op=ALU.mult),
                             reads=pbk(ba) + [("Y", "sz", t0)], writes=[("cst", l, c)])
                else:
                    P.op("dve", lambda e, ba=ba, zt=zt: e.tensor_tensor(out=zt[:, CL:LB].rearrange("p (s w) -> p s w", w=CW)[:, :, CB:CW],
                                                                      in0=pb(ba)[:, 0:NST].rearrange("p (s t) -> p s t", t=ST),
                                                                      in1=sz[:, TP:NTOK].rearrange("p (s t) -> p s t", t=ST), op=ALU.mult),
                         reads=pbk(ba) + [("Y", "sz", t0)], writes=[zk])
                    P.op("dve", lambda e, ba=ba, c=c: e.tensor_tensor(out=zstage[:, c, :], in0=pb(ba)[:, 0:NST], in1=sz[:, TP:NTOK], op=ALU.mult),
                         reads=pbk(ba) + [("Y", "sz", t0)], writes=[("Y", "zstage")])
            P.op("pool", lambda e, zt=zt, c=c: e.tensor_copy(out=zt[:, CL:LB].rearrange("p (s w) -> p s w", w=CW)[:, :, 0:CB],
                                                             in_=sconvT[:, c, :].rearrange("p (s r) -> p s r", r=CB)), reads=[("Y", "sconvT")], writes=[zk])
            P.op("pool", lambda e, c=c: e.tensor_tensor(out=diag[:, :, :], in0=bass.AP(identb, 0, [[128, 128], [0, 31], [1, 128]]),
                                                        in1=bass.AP(wdwT, c * 124 + l * 31, [[992, 128], [1, 31], [0, 128]]), op=ALU.mult),
                 reads=["identb", "wdwT"], writes=[dgk])
        NDVE = 5

        def stageVB(c):
            zt, zk = zc[c % 2], ("Y", "zc", c % 2)
            diag, dgk = diag2[c % 2], ("Y", "diag", c % 2)
            zs = zt[:, CL:LB].rearrange("p (s w) -> p s w", w=CW)
            accs = acc[:, TP:NTOK].rearrange("p (s t) -> p s t", t=ST)
            for jj in range(NDVE):
                wc = wdwT[:, c, l * 31 + jj:l * 31 + jj + 1]
                if jj == 0:
                    P.op("dve", lambda e, wc=wc, zt=zt: e.tensor_scalar(out=acc[:, 0:TP], in0=zt[:, 0:TP], scalar1=wc, scalar2=0.0, op0=ALU.mult, op1=ALU.add),
                         reads=[zk, "wdwT"], writes=[("Y", "acc", 0)])
                    P.op("dve", lambda e, wc=wc, zs=zs, accs=accs: e.tensor_scalar(out=accs, in0=zs[:, :, 0:ST], scalar1=wc, scalar2=0.0, op0=ALU.mult, op1=ALU.add),
                         reads=[zk, "wdwT"], writes=[("Y", "acc", 1)])
                else:
                    P.op("dve", lambda e, wc=wc, zt=zt, jj=jj: e.scalar_tensor_tensor(out=acc[:, 0:TP], in0=zt[:, jj:jj + TP], scalar=wc, in1=acc[:, 0:TP], op0=ALU.mult, op1=ALU.add),
                         reads=[zk, "wdwT", ("Y", "acc", 0)], writes=[("Y", "acc", 0)])
                    P.op("dve", lambda e, wc=wc, zs=zs, accs=accs, jj=jj: e.scalar_tensor_tensor(out=accs, in0=zs[:, :, jj:jj + ST], scalar=wc, in1=accs, op0=ALU.mult, op1=ALU.add),
                         reads=[zk, "wdwT", ("Y", "acc", 1)], writes=[("Y", "acc", 1)])
            for (t0, n) in TBLK:
                b = bank()
                for jj in range(NDVE, 31):
                    if t0 < TP:
                        rhs_fn = lambda jj=jj, zt=zt, t0=t0, n=n: zt[:, t0 + jj:t0 + jj + n]
                    else:
                        rhs_fn = lambda jj=jj, zs=zs: zs[:, :, jj:jj + ST]
                    P.op("pe", lambda e, jj=jj, b=b, n=n, rhs_fn=rhs_fn, diag=diag: e.matmul(pb(b)[:, 0:n], lhsT=diag[:, jj, :], rhs=rhs_fn(), start=(jj == NDVE), stop=(jj == 30)),
                         reads=[zk, dgk], writes=pbk(b))
                ak = ("Y", "acc", 0 if t0 < TP else 1)
                P.op("dve", lambda e, b=b, c=c, t0=t0, n=n: e.scalar_tensor_tensor(out=yb[:, c, t0:t0 + n], in0=pb(b)[:, 0:n], scalar=vcol("b_dw", l, c), in1=acc[:, t0:t0 + n],
                                                                                   op0=ALU.add, op1=ALU.add),
                     reads=pbk(b) + ["vecT", ak], writes=[("Y", "yb", c, t0)])
        stagePB(0)
        for c in range(8):
            if c + 1 < 8:
                stagePB(c + 1)
            stageVB(c)
        transpose_out(lambda k: zstage[:, k, :], [("Y", "zstage")], NST,
                      [(s * ST, ST, T["o_conv_s"][l, s0 + s, CB - ST:CB, :], ("ocs", s)) for s in range(NS)])
        if last:
            transpose_out(lambda k: convst[:, l, k, :], [("cst", l, k) for k in range(8)], CB, [(0, CB, T["o_conv_p"][l], ("ocp", 0))])
        P.cur = "B_ln"
        Y.reset(keepB)
        sq2 = [Y.alloc([512], BF16), Y.alloc([512], BF16)]
        meanb, m2, rt, rstd, nmr = (Y.alloc([512], F32) for _ in range(5))
        t1 = [Y.alloc([512], F32), Y.alloc([512], F32), Y.alloc([512], F32), Y.alloc([512], F32)]
        sigB8 = Y.alloc([8, NTOK], BF16)
        tmpB = t1[0:2]
        yb_fn = lambda k, t0, n: yb[:, k, t0:t0 + n]
        ybk = lambda k, t0: [("Y", "yb", k, t0)]
        def gate_b(q):
            sg_, sgk = WS.get([(kx_view(512, 0, 512), win[:, GL + 1024 + q * 512:GL + 1024 + (q + 1) * 512].rearrange("(k p) c -> p k c", p=128))])
            for cc in range(4):
                c = 4 * q + cc
                for (t0, n) in TBLK:
                    b = proj_kx(sg_, sgk, cc * 128, (t0, n), h_fn, hk)
                    P.op("act", lambda e, b=b, c=c, t0=t0, n=n: e.activation(out=sigB8[:, c, t0:t0 + n], in_=pb(b)[:, 0:n], func=AF.Sigmoid), reads=pbk(b), writes=[("Y", "sigB", c, t0)])
                    yield

        statb = []
        gen0 = gate_b(0)
        step = 0
        for (t0, n) in TBLK:
            bm, be = bank(), bank()
            reserved.update((bm, be))
            statb.append((bm, be))
            for k in range(8):
                sq = sq2[k % 2]
                sqk = ("Y", "sq2", k % 2)
                eng = "act" if k % 2 == 0 else "dve"
                if eng == "act":
                    P.op("act", lambda e, k=k, sq=sq, t0=t0, n=n: e.activation(out=sq[:, 0:n], in_=yb[:, k, t0:t0 + n], func=AF.Square), reads=[("Y", "yb", k, t0)], writes=[sqk])
                else:
                    P.op("dve", lambda e, k=k, sq=sq, t0=t0, n=n: e.tensor_tensor(out=sq[:, 0:n], in0=yb[:, k, t0:t0 + n], in1=yb[:, k, t0:t0 + n], op=ALU.mult), reads=[("Y", "yb", k, t0)], writes=[sqk])
                P.op("pe", lambda e, k=k, bm=bm, t0=t0, n=n: e.matmul(pb(bm)[:, 0:n], lhsT=onesm[:], rhs=yb[:, k, t0:t0 + n], start=(k == 0), stop=(k == 7)),
                     reads=[("Y", "yb", k, t0), "onesm"], writes=pbk(bm))
                P.op("pe", lambda e, k=k, be=be, sq=sq, n=n: e.matmul(pb(be)[:, 0:n], lhsT=onesm[:], rhs=sq[:, 0:n], start=(k == 0), stop=(k == 7)),
                     reads=[sqk, "onesm"], writes=pbk(be))
                step += 1
                if step % 2 == 0:
                    pull(gen0, 1)
        pull(gen0, 100)
        for bi, (t0, n) in enumerate(TBLK):
            bm, be = statb[bi]
            P.op("act", lambda e, bm=bm, n=n: e.copy(out=meanb[:, 0:n], in_=pb(bm)[:, 0:n]), reads=pbk(bm), writes=[("Y", "meanb")])
            P.op("dve", lambda e, n=n: e.tensor_tensor(out=m2[:, 0:n], in0=meanb[:, 0:n], in1=meanb[:, 0:n], op=ALU.mult), reads=[("Y", "meanb")], writes=[("Y", "m2")])
            P.op("dve", lambda e, be=be, n=n: e.tensor_tensor(out=m2[:, 0:n], in0=pb(be)[:, 0:n], in1=m2[:, 0:n], op=ALU.subtract), reads=pbk(be) + [("Y", "m2")], writes=[("Y", "m2")])
            reserved.difference_update((bm, be))
            P.op("dve", lambda e, n=n: e.tensor_scalar(out=m2[:, 0:n], in0=m2[:, 0:n], scalar1=0.0, scalar2=0.0, op0=ALU.max, op1=ALU.add), reads=[("Y", "m2")], writes=[("Y", "m2")])
            P.op("act", lambda e, n=n: e.activation(out=rt[:, 0:n], in_=m2[:, 0:n], func=AF.Sqrt, bias=epsT[:, 0:1], scale=1.0), reads=[("Y", "m2"), "epsT"], writes=[("Y", "rt2")])
            P.op("dve", lambda e, n=n: e.reciprocal(out=rstd[:, 0:n], in_=rt[:, 0:n]), reads=[("Y", "rt2")], writes=[("Y", "rstd2")])
            P.op("dve", lambda e, n=n: e.scalar_tensor_tensor(out=nmr[:, 0:n], in0=meanb[:, 0:n], scalar=-1.0, in1=rstd[:, 0:n], op0=ALU.mult, op1=ALU.mult),
                 reads=[("Y", "meanb"), ("Y", "rstd2")], writes=[("Y", "nmr")])
            for k0 in range(0, 8, 2):
                eng = "dve"
                for k in (k0, k0 + 1):
                    tt, tk = t1[k % 4], ("Y", "t1", k % 4)
                    P.op(eng, lambda e, k=k, tt=tt, t0=t0, n=n: e.tensor_tensor(out=tt[:, 0:n], in0=yb[:, k, t0:t0 + n], in1=rstd[:, 0:n], op=ALU.mult),
                         reads=[("Y", "yb", k, t0), ("Y", "rstd2")], writes=[tk])
                for k in (k0, k0 + 1):
                    tt, tk = t1[k % 4], ("Y", "t1", k % 4)
                    P.op(eng, lambda e, tt=tt, n=n: e.tensor_tensor(out=tt[:, 0:n], in0=tt[:, 0:n], in1=nmr[:, 0:n], op=ALU.add), reads=[tk, ("Y", "nmr")], writes=[tk])
                    P.op("act", lambda e, k=k, tt=tt, t0=t0, n=n: e.activation(out=yb[:, k, t0:t0 + n], in_=tt[:, 0:n], func=AF.Silu, bias=vcol("conv_ln_b", l, k), scale=vcol("conv_ln_g", l, k)),
                         reads=[tk, "vecT"], writes=[("Y", "yb", k, t0)])
        pull(gate_b(1), 100)
        P.cur = "B_pw"
        ti = 0
        for q in range(2):
            sp_, spk = WS.get([(kx_view(512, 0, 512), T["w_pw"][l][:, q * 512:(q + 1) * 512].rearrange("(k p) c -> p k c", p=128))])
            for cc in range(4):
                c = 4 * q + cc
                for (t0, n) in TBLK:
                    b2 = proj_kx(sp_, spk, cc * 128, (t0, n), yb_fn, ybk)
                    tb, tbk = tmpB[ti % 2], ("Y", "t1", ti % 2)
                    ti += 1
                    P.op("dve", lambda e, b2=b2, tb=tb, c=c, t0=t0, n=n: e.tensor_tensor(out=tb[:, 0:n], in0=pb(b2)[:, 0:n], in1=sigB8[:, c, t0:t0 + n], op=ALU.mult),
                         reads=pbk(b2) + [("Y", "sigB", c, t0)], writes=[tbk])
                    P.op("pool", lambda e, tb=tb, c=c, t0=t0, n=n: e.tensor_tensor(out=merged[:, c, t0:t0 + n], in0=merged[:, c, t0:t0 + n], in1=tb[:, 0:n], op=ALU.add),
                         reads=[tbk] + mk(c, t0), writes=mk(c, t0))

        P.cur = "C_kv"
        Y.reset()
        KTp = Y.alloc([8, NMEM], BF16)
        Vp = Y.alloc([2, D], BF16)
        keepC = Y.off
        memn = Y.alloc([8, NMEM], BF16)
        kvst = [Y.alloc([512], F32), Y.alloc([512], F32)]
        tmpn = norm_tmp()
        rmsnorm(lambda k, t0, n: memT[:, k, t0:t0 + n], lambda k, t0: ["memT"], [(0, NMEM)], "mem_norm", l,
                lambda k, t0, n: memn[:, k, t0:t0 + n], lambda k, t0: [("Y", "memn", k)], tmpn)
        mn_fn = lambda k, t0, n: memn[:, k, t0:t0 + n]
        mnk = lambda k, t0: [("Y", "memn", k)]
        wkv = T["w_mem_kv"][l]
        kvi = 0
        for q in range(4):
            slot, sk = WS.get([(kx_view(512, 0, 512), wkv[:, q * 512:(q + 1) * 512].rearrange("(k p) c -> p k c", p=128))])
            v = slot[:, 0:4096].rearrange("p (k c) -> p k c", k=8)
            if q < 2:
                for cc in range(4):
                    c = 4 * q + cc
                    b = proj_kx(slot, sk, cc * 128, (0, NMEM), mn_fn, mnk)
                    copy_op(evac_eng(), KTp[:, c, :], pb(b)[:, 0:NMEM], pbk(b), [("Y", "KTp", c)])
            if q >= 2 or p == 0:
                for mt in range(2):
                    b = bank()
                    for k in range(8):
                        P.op("pe", lambda e, k=k, b=b, mt=mt, v=v: e.matmul(pb(b)[:, :], lhsT=memn[:, k, mt * 128:(mt + 1) * 128], rhs=v[:, k, :], start=(k == 0), stop=(k == 7)),
                             reads=sk + [("Y", "memn", k)], writes=pbk(b))
                    if q >= 2 and p != 0:
                        copy_op("dve", Vp[:, mt, (q - 2) * 512:(q - 1) * 512], pb(b)[:, :], pbk(b), [("Y", "Vp", mt, q - 2)])
                    if p == 0:
                        ks, kk_ = kvst[kvi % 2], ("Y", "kvst", kvi % 2)
                        kvi += 1
                        copy_op("act", ks[:, :], pb(b)[:, :], pbk(b), [kk_])
                        if q >= 2:
                            copy_op("dve", Vp[:, mt, (q - 2) * 512:(q - 1) * 512], ks[:, :], [kk_], [("Y", "Vp", mt, q - 2)])
                        dst = (T["o_mk"] if q < 2 else T["o_mv"])[l, mt * 128:(mt + 1) * 128, (q % 2) * 512:(q % 2 + 1) * 512]
                        finals.append(P.dma("sp", lambda e, dst=dst, ks=ks: e.dma_start(out=dst, in_=ks[:, :]), ("okv", kvi % 2), reads=[kk_]))
        P.cur = "C_prompt"
        Y.reset(keepC)
        qTs = Y.alloc([8, NST], BF16)
        sigCs = Y.alloc([8, NST], BF16)
        qmask = Y.alloc([NS, 8, NST], BF16)
        PTm = Y.alloc([NS, 8, NST], BF16)
        keepC2 = Y.off
        qT2 = [Y.alloc([2, TP], BF16), Y.alloc([2, TP], BF16)]
        sigC2 = [[Y.alloc([TP], BF16), Y.alloc([TP], BF16)] for _ in range(2)]
        PT = Y.alloc([2, TP], BF16)
        Pe = [Y.alloc([NMEM], BF16) for _ in range(4)]
        smx = Y.alloc([4, 8], F32)
        tmpC = [Y.alloc([512], F32), Y.alloc([512], F32)]
        P.op("pool", lambda e: e.memset(qmask[:, :, :, :], 0.0), writes=[("Y", "qmask")])
        P.op("pool", lambda e: e.memset(PTm[:, :, :, :], 0.0), writes=[("Y", "PTm")])
        ci = [0]

        def proj_gen(h):
            slot, sk = win_piece(3072 + h * 256, GL + 2048 + h * 256)
            qT, sigC = qT2[h % 2], sigC2[h % 2]
            for j in range(2):
                for (t0, n) in TBLK:
                    b = proj_kx(slot, sk, j * 128, (t0, n), h_fn, hk)
                    if t0 < TP:
                        P.op("act", lambda e, b=b, j=j, t0=t0, n=n, qT=qT: e.mul(out=qT[:, j, t0:t0 + n], in_=pb(b)[:, 0:n], mul=0.0625), reads=pbk(b), writes=[("Y", "qT", h % 2, j, t0)])
                    else:
                        P.op("act", lambda e, b=b, j=j, h=h: e.mul(out=qTs[:, 2 * h + j, :], in_=pb(b)[:, 0:NST], mul=0.0625), reads=pbk(b), writes=[("Y", "qTs")])
                    yield
                    b = proj_kx(slot, sk, 256 + j * 128, (t0, n), h_fn, hk)
                    if t0 < TP:
                        P.op("act", lambda e, b=b, j=j, t0=t0, n=n, sigC=sigC: e.activation(out=sigC[j][:, t0:t0 + n], in_=pb(b)[:, 0:n], func=AF.Sigmoid), reads=pbk(b), writes=[("Y", "sigC", h % 2, j, t0)])
                    else:
                        P.op("act", lambda e, b=b, j=j, h=h: e.activation(out=sigCs[:, 2 * h + j, :], in_=pb(b)[:, 0:NST], func=AF.Sigmoid), reads=pbk(b), writes=[("Y", "sigCs")])
                    yield

        def scores(h, i):
            qT = qT2[h % 2]
            t0b = (i * 128) // 512 * 512
            b = bank()
            reserved.add(b)
            for dc in range(2):
                P.op("pe", lambda e, dc=dc, b=b, i=i, h=h, qT=qT: e.matmul(pb(b)[:, 0:NMEM], lhsT=qT[:, dc, i * 128:(i + 1) * 128], rhs=KTp[:, 2 * h + dc, :], start=(dc == 0), stop=(dc == 1)),
                     reads=[("Y", "qT", h % 2, dc, t0b), ("Y", "KTp", 2 * h + dc)], writes=pbk(b))
            return b

        for _ in proj_gen(0):
            pass
        for h in range(4):
            sigC = sigC2[h % 2]
            gen = proj_gen(h + 1) if h + 1 < 4 else None
            if h == 3:
                for s in range(NS):
                    P.op("pool", lambda e, s=s: e.tensor_copy(out=qmask[:, s, :, s * ST:(s + 1) * ST], in_=qTs[:, :, s * ST:(s + 1) * ST]), reads=[("Y", "qTs"), ("Y", "qmask")], writes=[("Y", "qmask")])
            nt = TP // 128
            SLA = 2
            sb = {i0: scores(h, i0) for i0 in range(SLA)}
            for i in range(nt):
                b = sb.pop(i)
                mi = i % 4
                pe_, pek = Pe[mi], ("Y", "Pe", mi)
                pn_, pnk = pe_, pek
                P.op("dve", lambda e, b=b, mi=mi: e.tensor_reduce(out=smx[:, 0, mi:mi + 1], in_=pb(b)[:, 0:NMEM], axis=AX.X, op=ALU.max), reads=pbk(b), writes=[("Y", "mx", mi)])
                P.op("dve", lambda e, mi=mi: e.tensor_scalar(out=smx[:, 1, mi:mi + 1], in0=smx[:, 0, mi:mi + 1], scalar1=-1.0, scalar2=0.0, op0=ALU.mult, op1=ALU.add), reads=[("Y", "mx", mi)], writes=[("Y", "nmx", mi)])
                P.op("act", lambda e, b=b, pe_=pe_, mi=mi: e.activation(out=pe_[:, :], in_=pb(b)[:, 0:NMEM], func=AF.Exp, bias=smx[:, 1, mi:mi + 1], scale=1.0, accum_out=smx[:, 2, mi:mi + 1]),
                     reads=pbk(b) + [("Y", "nmx", mi)], writes=[pek, ("Y", "rs", mi)])
                reserved.discard(b)
                P.op("dve", lambda e, mi=mi: e.reciprocal(out=smx[:, 3, mi:mi + 1], in_=smx[:, 2, mi:mi + 1]), reads=[("Y", "rs", mi)], writes=[("Y", "rr", mi)])
                P.op("dve", lambda e, pe_=pe_, mi=mi: e.tensor_scalar(out=pe_[:, :], in0=pe_[:, :], scalar1=smx[:, 3, mi:mi + 1], scalar2=0.0, op0=ALU.mult, op1=ALU.add),
                     reads=[pek, ("Y", "rr", mi)], writes=[pek])
                if i + SLA < nt:
                    sb[i + SLA] = scores(h, i + SLA)
                pull(gen, 2 if i % 2 == 0 else 1)
                b2 = bank()
                pst = pb(b2).bitcast(BF16)
                for mc in range(2):
                    P.op("pe", lambda e, mc=mc, pst=pst, pn_=pn_: e.transpose(pst[:, mc * 128:(mc + 1) * 128], pn_[:, mc * 128:(mc + 1) * 128], identb[:]),
                         reads=[pnk, "identb"], writes=pbk(b2))
                copy_op("act", PT[:, :, i * 128:(i + 1) * 128], pst[:, 0:256].rearrange("p (m t) -> p m t", m=2), pbk(b2), [("Y", "PT", i)])
            pull(gen, 100)
            for j in range(2):
                c = 2 * h + j
                for bi, (t0, n) in enumerate(PBLK):
                    b = bank()
                    for mc in range(2):
                        P.op("pe", lambda e, mc=mc, b=b, c=c, t0=t0, n=n: e.matmul(pb(b)[:, 0:n], lhsT=Vp[:, mc, c * 128:(c + 1) * 128], rhs=PT[:, mc, t0:t0 + n], start=(mc == 0), stop=(mc == 1)),
                             reads=[("Y", "Vp", mc, c // 4)] + [("Y", "PT", i) for i in range(t0 // 128, (t0 + n) // 128)], writes=pbk(b))
                    tb, tbk = tmpC[ci[0] % 2], ("Y", "tmpC", ci[0] % 2)
                    ci[0] += 1
                    P.op("dve", lambda e, b=b, tb=tb, j=j, t0=t0, n=n, sigC=sigC: e.tensor_tensor(out=tb[:, 0:n], in0=pb(b)[:, 0:n], in1=sigC[j][:, t0:t0 + n], op=ALU.mult),
                         reads=pbk(b) + [("Y", "sigC", h % 2, j, t0)], writes=[tbk])
                    P.op("pool", lambda e, tb=tb, c=c, t0=t0, n=n: e.tensor_tensor(out=merged[:, c, t0:t0 + n], in0=merged[:, c, t0:t0 + n], in1=tb[:, 0:n], op=ALU.add),
                         reads=[tbk] + mk(c, t0), writes=mk(c, t0))
        P.cur = "C_sample"
        Y.reset(keepC2)
        KTs = [Y.alloc([8, NMEM], BF16), Y.alloc([8, NMEM], BF16)]
        Pes = Y.alloc([4, NMEM], BF16)
        Pns = Y.alloc([4, NMEM], BF16)
        PTs = Y.alloc([8, NST], BF16)
        smy = Y.alloc([4, 8], F32)
        bs = bank(2)
        reserved.update((bs, bs + 1))
        for s in range(NS):
            slot, sk = WS.get([(lambda sl: sl[:, 0:2048].rearrange("p (m c) -> p m c", m=2), T["ck"][l, s0 + s].rearrange("(m p) c -> p m c", p=128))])
            kv = slot[:, 0:2048].rearrange("p (m c) -> p m c", m=2)
            kt, ktk = KTs[s % 2], ("Y", "KTs", s % 2)
            bt = bank(2)
            pst = pb(bt, 2).bitcast(BF16)
            for ch in range(8):
                for mc in range(2):
                    P.op("pe", lambda e, ch=ch, mc=mc, pst=pst, kv=kv: e.transpose(pst[:, ch * 256 + mc * 128:ch * 256 + (mc + 1) * 128], kv[:, mc, ch * 128:(ch + 1) * 128], identb[:]),
                         reads=sk + ["identb"], writes=pbk(bt, 2))
            copy_op(evac_eng(), kt[:, :, :], pst[:, :].rearrange("p (c m) -> p c m", c=8), pbk(bt, 2), [ktk])
            for h in range(4):
                for dc in range(2):
                    P.op("pe", lambda e, h=h, dc=dc, s=s, kt=kt: e.matmul(pb(bs, 2)[0:NST, h * 256:(h + 1) * 256], lhsT=qmask[:, s, 2 * h + dc, :], rhs=kt[:, 2 * h + dc, :],
                                                                     start=(s == 0 and dc == 0 and h % 2 == 0), stop=(s == NS - 1 and dc == 1)),
                         reads=[("Y", "qmask"), ktk], writes=pbk(bs, 2))
        sc = pb(bs, 2)[0:NST, :].rearrange("p (h m) -> p h m", h=4)
        P.op("dve", lambda e: e.tensor_reduce(out=smy[0:NST, 0, 0:4], in_=sc, axis=AX.X, op=ALU.max), reads=pbk(bs, 2), writes=[("Y", "mxs")])
        P.op("dve", lambda e: e.tensor_scalar(out=smy[0:NST, 1, 0:4], in0=smy[0:NST, 0, 0:4], scalar1=-1.0, scalar2=0.0, op0=ALU.mult, op1=ALU.add), reads=[("Y", "mxs")], writes=[("Y", "nmxs")])
        for h in range(4):
            P.op("act", lambda e, h=h: e.activation(out=Pes[0:NST, h, :], in_=sc[:, h, :], func=AF.Exp, bias=smy[0:NST, 1, h:h + 1], scale=1.0, accum_out=smy[0:NST, 2, h:h + 1]),
                 reads=pbk(bs, 2) + [("Y", "nmxs")], writes=[("Y", "Pes", h), ("Y", "rss", h)])
        P.op("dve", lambda e: e.reciprocal(out=smy[0:NST, 3, 0:4], in_=smy[0:NST, 2, 0:4]), reads=[("Y", "rss", h) for h in range(4)], writes=[("Y", "rrs")])
        for h in range(4):
            P.op("dve", lambda e, h=h: e.tensor_scalar(out=Pns[0:NST, h, :], in0=Pes[0:NST, h, :], scalar1=smy[0:NST, 3, h:h + 1], scalar2=0.0, op0=ALU.mult, op1=ALU.add),
                 reads=[("Y", "Pes", h), ("Y", "rrs")], writes=[("Y", "Pns", h)])
        b2 = bank()
        reserved.difference_update((bs, bs + 1))
        pst = pb(b2).bitcast(BF16)
        for h in range(4):
            for mc in range(2):
                ch = 2 * h + mc
                P.op("pe", lambda e, h=h, mc=mc, ch=ch, pst=pst: e.transpose(pst[:, ch * NST:(ch + 1) * NST], Pns[0:NST, h, mc * 128:(mc + 1) * 128], identb[0:NST, 0:NST]),
                     reads=[("Y", "Pns", h), "identb"], writes=pbk(b2))
        copy_op("dve", PTs[:, :, :], pst[:, 0:8 * NST].rearrange("p (c t) -> p c t", c=8), pbk(b2), [("Y", "PTs")])
        for s in range(NS):
            P.op("pool", lambda e, s=s: e.tensor_copy(out=PTm[:, s, :, s * ST:(s + 1) * ST], in_=PTs[:, :, s * ST:(s + 1) * ST]), reads=[("Y", "PTs"), ("Y", "PTm")], writes=[("Y", "PTm")])
        bo = bank()
        reserved.add(bo)
        for s in range(NS):
            slot, sk = WS.get([(lambda sl: sl[:, 0:2048].rearrange("p (m c) -> p m c", m=2), T["cv"][l, s0 + s].rearrange("(m p) c -> p m c", p=128))])
            vv = slot[:, 0:2048].rearrange("p (m c) -> p m c", m=2)
            for c in range(8):
                for mc in range(2):
                    P.op("pe", lambda e, c=c, mc=mc, s=s, vv=vv: e.matmul(pb(bo)[:, c * NST:(c + 1) * NST], lhsT=vv[:, mc, c * 128:(c + 1) * 128], rhs=PTm[:, s, (c // 2) * 2 + mc, :],
                                                                     start=(s == 0 and mc == 0 and c == 0), stop=(s == NS - 1 and mc == 1)),
                         reads=sk + [("Y", "PTm")], writes=pbk(bo))
        reserved.discard(bo)
        tcs = Y.alloc([8, NST], F32)
        P.op("dve", lambda e: e.tensor_tensor(out=tcs[:, :, :], in0=pb(bo)[:, 0:8 * NST].rearrange("p (c t) -> p c t", c=8), in1=sigCs[:, :, :], op=ALU.mult),
             reads=pbk(bo) + [("Y", "sigCs")], writes=[("Y", "tcs")])
        P.op("pool", lambda e: e.tensor_tensor(out=merged[:, :, TP:NTOK], in0=merged[:, :, TP:NTOK], in1=tcs[:, :, :], op=ALU.add),
             reads=[("Y", "tcs")] + [("BIG", "m", c, TP) for c in range(8)], writes=[("BIG", "m", c, TP) for c in range(8)])

        P.cur = "wout"
        for k in range(8):
            for (t0, n) in TBLK:
                copy_op(evac_eng(), hT[:, k, t0:t0 + n], merged[:, k, t0:t0 + n], mk(k, t0), hk(k, t0))
        for q in range(2):
            slot, sk = WS.get([(kx_view(512, 0, 512), T["w_out"][l][:, q * 512:(q + 1) * 512].rearrange("(k p) c -> p k c", p=128))])
            for cc in range(4):
                c = 4 * q + cc
                for (t0, n) in TBLK:
                    b = proj_kx(slot, sk, cc * 128, (t0, n), h_fn, hk)
                    P.op("dve", lambda e, b=b, c=c, t0=t0, n=n: e.tensor_tensor(out=xT[:, c, t0:t0 + n], in0=pb(b)[:, 0:n], in1=xT[:, c, t0:t0 + n], op=ALU.add),
                         reads=pbk(b) + xk(c, t0), writes=xk(c, t0))

    for p in range(DBG_PASSES):
        P.cur = "xload"
        srcs = [(T["xp"][p * TP + i * 128:p * TP + (i + 1) * 128, :], 128, i * 128) for i in range(TP // 128)]
        srcs.append((T["xs"][p * NST:(p + 1) * NST, :], NST, TP))
        for (src, nr, c0) in srcs:
            t_io, kio = ioslot()
            P.dma("sp", lambda e, t_io=t_io, src=src, nr=nr: e.dma_start(out=t_io[0:nr, :], in_=src), kio, writes=[kio])
            for hlf in range(2):
                b = bank()
                for kk in range(4):
                    k = hlf * 4 + kk
                    P.op("pe", lambda e, k=k, kk=kk, b=b, t_io=t_io, nr=nr: e.transpose(pb(b)[:, kk * nr:(kk + 1) * nr], t_io[0:nr, k * 128:(k + 1) * 128], identf[0:nr, 0:nr]),
                         reads=[kio, "identf"], writes=pbk(b))
                t0b = min(c0 // 512 * 512, TP)
                copy_op(evac_eng(), xT[:, hlf * 4:(hlf + 1) * 4, c0:c0 + nr], pb(b)[:, 0:4 * nr].rearrange("p (k f) -> p k f", k=4),
                        pbk(b) + [("x", k, t0b) for k in range(hlf * 4, hlf * 4 + 4)], [("x", k, t0b) for k in range(hlf * 4, hlf * 4 + 4)])
        for l in range(DBG_LAYERS):
            ffn(l, 1)
            if DBG_STAGE >= 2:
                mixer(l, p)
            if DBG_STAGE >= 3:
                ffn(l, 2)
        P.cur = "final"
        Y.reset()
        tmp = norm_tmp()
        BIG.reset()
        yT = BIG.alloc([8, NTOK], F32)
        rmsnorm(x_fn, xk, TBLK, "final_norm", 0, lambda k, t0, n: yT[:, k, t0:t0 + n], lambda k, t0: [("BIG", "y", k, t0)], tmp)
        for i in range(TP // 128):
            t0b = i * 128 // 512 * 512
            transpose_out(lambda k, i=i: yT[:, k, i * 128:(i + 1) * 128], [("BIG", "y", k, t0b) for k in range(8)], 128,
                          [(0, 128, T["o_yp"][p * TP + i * 128:p * TP + (i + 1) * 128, :], ("oy", i % 2))])
        transpose_out(lambda k: yT[:, k, TP:NTOK], [("BIG", "y", k, TP) for k in range(8)], NST,
                      [(0, NST, T["o_ys"][p * NST:(p + 1) * NST, :], ("oys", 0))])
    return P, WS, finals


def build_nc():
    nc = bass.Bass("TRN2", target_bir_lowering=False)
    T = {}

    def din(name, shape):
        T[name] = nc.dram_tensor(name, list(shape), F32, kind="ExternalInput").ap()

    def dout(name, shape):
        T[name] = nc.dram_tensor(name, list(shape), F32, kind="ExternalOutput").ap()

    din("xp", (SEQ, D)); din("xs", (NSEQ * ST, D)); din("mem", (NMEM, D))
    din("spool", (DEPTH, NSEQ, PB, D)); din("sconv", (DEPTH, NSEQ, CB, D))
    din("ck", (DEPTH, NSEQ, NMEM, D)); din("cv", (DEPTH, NSEQ, NMEM, D))
    for nm in VROWS:
        din(nm, (DEPTH, D))
    din("final_norm", (1, D))
    din("ffn1_w_gu", (DEPTH, D, 2 * DFF)); din("ffn1_w_down", (DEPTH, DFF, D))
    din("ffn2_w_gu", (DEPTH, D, 2 * DFF)); din("ffn2_w_down", (DEPTH, DFF, D))
    din("w_in", (DEPTH, D, INC)); din("w_mem_kv", (DEPTH, D, 2 * D)); din("w_pool", (DEPTH, 4, 256, 256))
    din("w_dw", (DEPTH, 31, D)); din("w_pw", (DEPTH, D, D)); din("w_out", (DEPTH, D, D))
    dout("o_yp", (SEQ, D)); dout("o_ys", (NSEQ * ST, D))
    dout("o_pool_p", (DEPTH, PB, D)); dout("o_conv_p", (DEPTH, CB, D))
    dout("o_mk", (DEPTH, NMEM, D)); dout("o_mv", (DEPTH, NMEM, D))
    dout("o_pool_s", (DEPTH, NSEQ, PB, D)); dout("o_conv_s", (DEPTH, NSEQ, CB, D))

    A = {}
    A["xT"] = nc.alloc_sbuf_tensor("xT", [128, 8, NTOK], F32)
    A["hT"] = nc.alloc_sbuf_tensor("hT", [128, 8, NTOK], BF16)
    A["slots"] = [nc.alloc_sbuf_tensor("ws%d" % i, [128, SLOTE], BF16) for i in range(NSLOT)]
    A["BIG_bytes"] = 8 * NTOK * 4
    A["BIG"] = nc.alloc_sbuf_tensor("BIG", [128, A["BIG_bytes"] // 2], BF16)
    A["Y_bytes"] = int(os.environ.get("DBG_YKB", 54)) * 1024
    A["Y"] = nc.alloc_sbuf_tensor("Yar", [128, A["Y_bytes"] // 2], BF16)
    A["identf"] = nc.alloc_sbuf_tensor("identf", [128, 128], F32)
    A["identb"] = nc.alloc_sbuf_tensor("identb", [128, 128], BF16)
    A["onesm"] = nc.alloc_sbuf_tensor("onesm", [128, 128], BF16)
    A["epsT"] = nc.alloc_sbuf_tensor("epsT", [128, 1], F32)
    A["vecT"] = nc.alloc_sbuf_tensor("vecT", [128, 8, 33], F32)
    A["wdwT"] = nc.alloc_sbuf_tensor("wdwT", [128, 8, 124], F32)
    A["rcnt"] = nc.alloc_sbuf_tensor("rcnt", [128, 4, 16], F32)
    A["memT"] = nc.alloc_sbuf_tensor("memT", [128, 8, NMEM], F32)
    A["poolst"] = nc.alloc_sbuf_tensor("poolst", [128, DEPTH, 8, PB], F32)
    A["convst"] = nc.alloc_sbuf_tensor("convst", [128, DEPTH, 8, CB], F32)
    A["io"] = [nc.alloc_sbuf_tensor("io%d" % i, [128, D], F32) for i in range(2)]
    A["vstage"] = A["io"][0]
    A["wstage"] = A["io"][1]
    A["psall"] = nc.alloc_psum_tensor("psall", [128, 4096], F32)
    T["alloc"] = A
    Pd, WSd, _ = build_program(nc, T, None)
    plan = WSd.rec
    P, WS, finals = build_program(nc, T, plan)
    assert len(WS.rec) == len(plan)
    P.emit(nc, finals)
    nc._dbgP = P
    return nc


_NC_CACHE = {}


def kernel(**inp):
    f = lambda a: np.ascontiguousarray(np.asarray(a, dtype=np.float32))
    if "nc" not in _NC_CACHE:
        _NC_CACHE["nc"] = build_nc()
    nc = _NC_CACHE["nc"]
    shared = {nm: f(inp[nm]) for nm in VROWS}
    shared["final_norm"] = f(inp["final_norm"]).reshape(1, D)
    for nm in ("ffn1_w_gu", "ffn1_w_down", "ffn2_w_gu", "ffn2_w_down", "w_in", "w_mem_kv", "w_pool", "w_dw", "w_pw", "w_out"):
        shared[nm] = f(inp[nm])
    xp, xs, mem = f(inp["x_prompt"]), f(inp["x_sample"]), f(inp["mem_prompt"])
    sp, sc = f(inp["state_pool"]), f(inp["state_conv"])
    ck = f(inp["cache_mem_k"]).reshape(DEPTH, 128, NMEM, D)
    cv = f(inp["cache_mem_v"]).reshape(DEPTH, 128, NMEM, D)
    in_maps = []
    for c in range(NCORES):
        m = dict(shared)
        sl = slice(c * NSEQ, (c + 1) * NSEQ)
        m["xp"] = xp[c]
        m["xs"] = np.ascontiguousarray(xs[sl].reshape(NSEQ * ST, D))
        m["mem"] = mem[c]
        m["spool"] = np.ascontiguousarray(sp[:, sl])
        m["sconv"] = np.ascontiguousarray(sc[:, sl])
        m["ck"] = np.ascontiguousarray(ck[:, sl])
        m["cv"] = np.ascontiguousarray(cv[:, sl])
        in_maps.append(m)
    res = run_bass_kernel_spmd(nc, in_maps[:DBG_CORES], core_ids=list(range(DBG_CORES)))
    R = list(res.results)
    while len(R) < NCORES:
        R.append(R[0])
    y_p = np.stack([R[c]["o_yp"] for c in range(NCORES)], 0)
    y_s = np.concatenate([R[c]["o_ys"].reshape(NSEQ, ST, D) for c in range(NCORES)], 0)
    pool_p = np.stack([R[c]["o_pool_p"] for c in range(NCORES)], 1)
    conv_p = np.stack([R[c]["o_conv_p"] for c in range(NCORES)], 1)
    mk = np.stack([R[c]["o_mk"] for c in range(NCORES)], 1).reshape(DEPTH, NCORES, NMEM, 4, 256)
    mv = np.stack([R[c]["o_mv"] for c in range(NCORES)], 1).reshape(DEPTH, NCORES, NMEM, 4, 256)
    pool_s = np.concatenate([R[c]["o_pool_s"] for c in range(NCORES)], 1)
    conv_s = np.concatenate([R[c]["o_conv_s"] for c in range(NCORES)], 1)
    return (y_p.astype(np.float32), y_s.astype(np.float32), pool_p.astype(np.float32), conv_p.astype(np.float32),
            mk.astype(np.float32), mv.astype(np.float32), pool_s.astype(np.float32), conv_s.astype(np.float32))
```

```python
import contextlib
import numpy as np
import concourse.bass as bass
import concourse.mybir as mybir
from concourse.bass_utils import run_bass_kernel_spmd

F32 = mybir.dt.float32
BF16 = mybir.dt.bfloat16
AF = mybir.ActivationFunctionType
ALU = mybir.AluOpType
AX = mybir.AxisListType

DEPTH = 4
NCORES = 8
D = 1024
DFF = 2816
NFF = 22
SEQ = 2048
NPASS = 2
TP = SEQ // NPASS
NSEQ = 16
NS = NSEQ // NPASS
ST = 8
NST = NS * ST
NTOK = TP + NST
NMEM = 256
PB, CB = 15, 30
INC = 7168
GL = 4096
EPS = 1e-6
NSLOT = 4
SLOTE = 4096
TBLK = [(0, 512), (512, 512), (1024, NST)]
PBLK = [(0, 512), (512, 512)]
PL = PB + TP
PW = PB + ST
CL = CB + TP
CW = CB + ST
ENGS = ("pe", "act", "dve", "pool", "sp")
import os
DBG_LAYERS = int(os.environ.get("DBG_LAYERS", DEPTH))
DBG_STAGE = int(os.environ.get("DBG_STAGE", 3))
DBG_PASSES = int(os.environ.get("DBG_PASSES", NPASS))
DBG_CORES = int(os.environ.get("DBG_CORES", NCORES))
VROWS = ["ffn1_norm", "mix_norm", "pool_scale", "b_dw", "conv_ln_g", "conv_ln_b", "ffn2_norm", "mem_norm"]


class Op:
    __slots__ = ("eng", "fn", "deps", "is_dma", "sem", "semval", "signal", "cnt", "idx", "lab")

    def __init__(self, eng, fn, is_dma=False):
        self.eng, self.fn, self.is_dma = eng, fn, is_dma
        self.deps = []
        self.sem = None
        self.semval = 0
        self.signal = False
        self.cnt = 0
        self.idx = 0


class Prog:
    def __init__(self, arenas=()):
        self.ops = {e: [] for e in ENGS}
        self.last_w = {}
        self.readers = {}
        self.dma_cnt = {}
        self.dma_last = {}
        self.arenas = set(arenas)
        self.arena_ops = {a: {} for a in arenas}
        self.fence = {a: {} for a in arenas}
        self.nops = 0
        self.cur = ""

    @staticmethod
    def _slot(op):
        return ("d", op.sem) if op.is_dma else ("e", op.eng)

    def _add(self, table, op):
        k = self._slot(op)
        o = table.get(k)
        if o is None or o.idx < op.idx:
            table[k] = op

    def _track(self, op, reads, writes):
        deps = {}

        def add(d):
            if d is op:
                return
            if d.eng == "pe" and op.eng == "pe" and not d.is_dma and not op.is_dma:
                return
            k = self._slot(d)
            o = deps.get(k)
            if o is None or o.idx < d.idx:
                deps[k] = d

        for k in list(reads) + list(writes):
            if isinstance(k, tuple) and k[0] in self.arenas:
                for d in self.fence[k[0]].values():
                    add(d)
        for k in reads:
            w = self.last_w.get(k)
            if w is not None:
                add(w)
            if isinstance(k, tuple) and k[0] == "pb":
                for r in self.readers.get(k, {}).values():
                    if r.eng != op.eng:
                        add(r)
        for k in writes:
            w = self.last_w.get(k)
            if w is not None:
                add(w)
            for r in self.readers.get(k, {}).values():
                add(r)
        op.deps = list(deps.values())
        for k in reads:
            self._add(self.readers.setdefault(k, {}), op)
        for k in writes:
            self.last_w[k] = op
            self.readers[k] = {}
        for k in list(reads) + list(writes):
            if isinstance(k, tuple) and k[0] in self.arenas:
                self._add(self.arena_ops[k[0]], op)

    def op(self, eng, fn, reads=(), writes=()):
        o = Op(eng, fn)
        o.lab = self.cur
        self.nops += 1
        o.idx = self.nops
        self._track(o, reads, writes)
        self.ops[eng].append(o)
        return o

    def dma(self, eng, fn, semkey, reads=(), writes=()):
        o = Op(eng, fn, True)
        o.lab = self.cur
        self.nops += 1
        o.idx = self.nops
        o.sem = semkey
        self._track(o, reads, writes)
        prev = self.dma_last.get(semkey)
        if prev is not None and all(d is not prev for d in o.deps):
            o.deps.append(prev)
        self.dma_cnt[semkey] = self.dma_cnt.get(semkey, 0) + 16
        o.semval = self.dma_cnt[semkey]
        self.dma_last[semkey] = o
        self.ops[eng].append(o)
        return o

    def handoff(self, arena):
        f = dict(self.fence[arena])
        for op in self.arena_ops[arena].values():
            self._add(f, op)
        self.fence[arena] = f
        self.arena_ops[arena] = {}
        for tab in (self.last_w, self.readers):
            for k in [k for k in tab if isinstance(k, tuple) and k[0] == arena]:
                del tab[k]

    def emit(self, nc, final_ops):
        for e in ENGS:
            for o in self.ops[e]:
                for d in o.deps:
                    if not d.is_dma:
                        d.signal = True
        for o in final_ops:
            if not o.is_dma:
                o.signal = True
        for e in ENGS:
            c = 0
            for o in self.ops[e]:
                if not o.is_dma and o.signal:
                    c += 1
                    o.cnt = c
        keys = sorted(self.dma_cnt.keys(), key=str)
        with contextlib.ExitStack() as st:
            esem = {e: st.enter_context(nc.semaphore("s_" + e)) for e in ENGS}
            dsem = {k: st.enter_context(nc.semaphore("d%d" % i)) for i, k in enumerate(keys)}
            block = st.enter_context(nc.Block())

            def mk(e):
                def body(engine):
                    waited = {}

                    def wait(d):
                        if d.is_dma:
                            key, val, sem = ("d", d.sem), d.semval, dsem[d.sem]
                        else:
                            key, val, sem = ("e", d.eng), d.cnt, esem[d.eng]
                        if waited.get(key, 0) >= val:
                            return
                        engine.wait_ge(sem, val)
                        waited[key] = val

                    for o in self.ops[e]:
                        for d in o.deps:
                            wait(d)
                        ins = o.fn(engine)
                        if o.is_dma:
                            ins.then_inc(dsem[o.sem], 16)
                        elif o.signal:
                            ins.then_inc(esem[e], 1)
                    if e == "sp":
                        for o in final_ops:
                            wait(o)
                return body

            block.tensor(mk("pe"))
            block.scalar(mk("act"))
            block.vector(mk("dve"))
            block.gpsimd(mk("pool"))
            block.sync(mk("sp"))


class WStream:
    def __init__(self, P, slots, plan):
        self.P, self.slots, self.plan = P, slots, plan
        self.rec = []
        self.nreg = 0
        self.i = 0

    def _register(self, j):
        s = j % NSLOT
        for di, (dstfn, src) in enumerate(self.plan[j]):
            dst = dstfn(self.slots[s])
            self.P.dma("pool", lambda e, dst=dst, src=src: e.dma_start(out=dst, in_=src),
                       ("w", s, di), writes=[("ws", s, di)])

    def get(self, dmas):
        j = self.i
        self.i += 1
        self.rec.append(dmas)
        if self.plan is not None:
            lim = min(j + NSLOT - 1, len(self.plan) - 1)
            while self.nreg <= lim:
                self._register(self.nreg)
                self.nreg += 1
        else:
            s = j % NSLOT
            for di, (dstfn, src) in enumerate(dmas):
                dst = dstfn(self.slots[s])
                self.P.dma("pool", lambda e, dst=dst, src=src: e.dma_start(out=dst, in_=src),
                           ("w", s, di), writes=[("ws", s, di)])
        s = j % NSLOT
        return self.slots[s], [("ws", s, di) for di in range(len(dmas))]


def kx_view(cols, off=0, width=None):
    def f(slot, cols=cols, off=off, width=width):
        W = width
        v = slot[:, 0:8 * W].rearrange("p (k c) -> p k c", k=8)
        return v[:, :, off:off + cols]
    return f


def build_program(nc, T, plan):
    P = Prog(arenas=("BIG", "Y"))
    A = T["alloc"]
    xT, hT, slots, BIGt, Yt = A["xT"], A["hT"], A["slots"], A["BIG"], A["Y"]
    identf, identb, onesm, epsT, vecT, wdwT, rcnt = (A[k] for k in ("identf", "identb", "onesm", "epsT", "vecT", "wdwT", "rcnt"))
    memT, poolst, convst, io, psall, vstage, wstage = (A[k] for k in ("memT", "poolst", "convst", "io", "psall", "vstage", "wstage"))
    WS = WStream(P, slots, plan)
    finals = []

    st = {"b": 0, "io": 0, "act": 0}

    reserved = set()

    def bank(n=1):
        b = st["b"]
        while True:
            if n == 2 and b % 2:
                b += 1
            if b + n > 8:
                b = 0
            if all((b + i) not in reserved for i in range(n)):
                break
            b += 1
        st["b"] = (b + n) % 8
        return b

    def pb(b, n=1):
        return psall[:, b * 512:(b + n) * 512]

    def pbk(b, n=1):
        return [("pb", b + i) for i in range(n)]

    def ioslot():
        s = st["io"]
        st["io"] = 1 - s
        return io[s], ("io", s)

    class Arena:
        def __init__(self, name, tile, nbytes):
            self.name, self.tile, self.nbytes, self.off = name, tile, nbytes, 0

        def reset(self, keep=0):
            P.handoff(self.name)
            self.off = keep

        def alloc(self, shape, dt):
            esz = 4 if dt == F32 else 2
            n = int(np.prod(shape))
            nb = (n * esz + 31) // 32 * 32
            assert self.off + nb <= self.nbytes, (self.name, self.off, nb, self.nbytes)
            e0 = self.off // 2
            v = self.tile[:, e0:e0 + nb // 2]
            self.off += nb
            if dt == F32:
                v = v.bitcast(F32)
            v = v[:, 0:n]
            if len(shape) == 2:
                v = v.rearrange("p (a b) -> p a b", a=shape[0])
            elif len(shape) == 3:
                v = v.rearrange("p (a b c) -> p a b c", a=shape[0], b=shape[1])
            return v

    BIG = Arena("BIG", BIGt, A["BIG_bytes"])
    Y = Arena("Y", Yt, A["Y_bytes"])

    def pull(gen, k):
        if gen is None:
            return
        for _ in range(k):
            try:
                next(gen)
            except StopIteration:
                return

    def evac_eng():
        st["act"] ^= 1
        return "act" if st["act"] else "dve"

    def copy_op(eng, out, in_, reads, writes):
        if eng == "act":
            return P.op("act", lambda e: e.copy(out=out, in_=in_), reads=reads, writes=writes)
        return P.op(eng, lambda e: e.tensor_copy(out=out, in_=in_), reads=reads, writes=writes)

    P.op("pool", lambda e: e.memset(identf[:], 0.0), writes=["identf"])
    P.op("pool", lambda e: e.affine_select(out=identf[:], in_=identf[:], pattern=[[-1, 128]], compare_op=ALU.not_equal,
                                           fill=1.0, base=0, channel_multiplier=1), reads=["identf"], writes=["identf"])
    P.op("dve", lambda e: e.tensor_copy(out=identb[:], in_=identf[:]), reads=["identf"], writes=["identb"])
    P.op("pool", lambda e: e.memset(onesm[:], 1.0 / 1024.0), writes=["onesm"])
    P.op("pool", lambda e: e.memset(epsT[:], EPS), writes=["epsT"])
    P.op("pool", lambda e: e.memset(poolst[:], 0.0), writes=["poolst"])
    P.op("pool", lambda e: e.memset(convst[:], 0.0), writes=["convst"])
    for g in range(4):
        w = 2 << g
        P.op("pool", lambda e, g=g, w=w: e.memset(rcnt[:, g, :], 1.0 / w), reads=["rcnt"], writes=["rcnt"])
        for t in range(w - 1):
            P.op("pool", lambda e, g=g, t=t: e.memset(rcnt[:, g, t:t + 1], 1.0 / (t + 1)), reads=["rcnt"], writes=["rcnt"])
    for i, nm in enumerate(VROWS):
        P.dma("sp", lambda e, i=i, nm=nm: e.dma_start(out=vstage[i * 4:(i + 1) * 4, :], in_=T[nm]), ("vs", i), writes=[("io", 0)])
    P.dma("sp", lambda e: e.dma_start(out=vstage[32:33, :], in_=T["final_norm"]), ("vs", 8), writes=[("io", 0)])
    b = bank()
    for k in range(8):
        P.op("pe", lambda e, k=k, b=b: e.transpose(pb(b)[:, k * 64:k * 64 + 33], vstage[0:33, k * 128:(k + 1) * 128], identf[0:33, 0:33]),
             reads=[("io", 0), "identf"], writes=pbk(b))
    P.op("dve", lambda e, b=b: e.tensor_copy(out=vecT[:], in_=pb(b).rearrange("p (k f) -> p k f", k=8)[:, :, 0:33]), reads=pbk(b), writes=["vecT"])
    P.dma("sp", lambda e: e.dma_start(out=wstage[0:124, :], in_=T["w_dw"].rearrange("l j c -> (l j) c")), ("vs", 9), writes=[("io", 1)])
    for hlf in range(2):
        b = bank()
        for kk in range(4):
            k = hlf * 4 + kk
            P.op("pe", lambda e, k=k, kk=kk, b=b: e.transpose(pb(b)[:, kk * 124:(kk + 1) * 124], wstage[0:124, k * 128:(k + 1) * 128], identf[0:124, 0:124]),
                 reads=[("io", 1), "identf"], writes=pbk(b))
        P.op("dve", lambda e, b=b, hlf=hlf: e.tensor_copy(out=wdwT[:, hlf * 4:(hlf + 1) * 4, :], in_=pb(b)[:, 0:496].rearrange("p (k f) -> p k f", k=4)),
             reads=pbk(b), writes=["wdwT"])
    for mt in range(2):
        t_io, kio = ioslot()
        P.dma("sp", lambda e, mt=mt, t_io=t_io: e.dma_start(out=t_io[:, :], in_=T["mem"][mt * 128:(mt + 1) * 128, :]), kio, writes=[kio])
        for hlf in range(2):
            b = bank()
            for kk in range(4):
                k = hlf * 4 + kk
                P.op("pe", lambda e, k=k, kk=kk, b=b, t_io=t_io: e.transpose(pb(b)[:, kk * 128:(kk + 1) * 128], t_io[:, k * 128:(k + 1) * 128], identf[:]),
                     reads=[kio, "identf"], writes=pbk(b))
            P.op("dve", lambda e, b=b, hlf=hlf, mt=mt: e.tensor_copy(out=memT[:, hlf * 4:(hlf + 1) * 4, mt * 128:(mt + 1) * 128],
                                                                    in_=pb(b).rearrange("p (k f) -> p k f", k=4)), reads=pbk(b), writes=["memT"])

    def vcol(name, l, k):
        r = 32 if name == "final_norm" else VROWS.index(name) * 4 + l
        return vecT[:, k, r:r + 1]

    def rmsnorm(src_fn, src_keys_fn, blks, gname, l, dst_fn, dst_keys_fn, tmp):
        for (t0, n) in blks:
            b = bank()
            for k in range(8):
                sq = tmp["sq"][k % 2]
                sk = ("Y", "sq", k % 2)
                P.op("act", lambda e, k=k, sq=sq, t0=t0, n=n: e.activation(out=sq[:, 0:n], in_=src_fn(k, t0, n), func=AF.Square),
                     reads=src_keys_fn(k, t0), writes=[sk])
                P.op("pe", lambda e, k=k, sq=sq, n=n, b=b: e.matmul(pb(b)[:, 0:n], lhsT=onesm[:], rhs=sq[:, 0:n], start=(k == 0), stop=(k == 7)),
                     reads=[sk, "onesm"], writes=pbk(b))
            rt, rstd = tmp["rt"], tmp["rstd"]
            P.op("act", lambda e, n=n, b=b, rt=rt: e.activation(out=rt[:, 0:n], in_=pb(b)[:, 0:n], func=AF.Sqrt, bias=epsT[:, 0:1], scale=1.0),
                 reads=pbk(b) + ["epsT"], writes=[("Y", "rt")])
            P.op("dve", lambda e, n=n, rt=rt, rstd=rstd: e.reciprocal(out=rstd[:, 0:n], in_=rt[:, 0:n]), reads=[("Y", "rt")], writes=[("Y", "rstd")])
            for k in range(8):
                P.op("dve", lambda e, k=k, t0=t0, n=n, rstd=rstd: e.scalar_tensor_tensor(
                    out=dst_fn(k, t0, n), in0=src_fn(k, t0, n), scalar=vcol(gname, l, k), in1=rstd[:, 0:n], op0=ALU.mult, op1=ALU.mult),
                    reads=src_keys_fn(k, t0) + [("Y", "rstd"), "vecT"], writes=dst_keys_fn(k, t0))

    def norm_tmp():
        return {"sq": [Y.alloc([512], BF16), Y.alloc([512], BF16)], "rt": Y.alloc([512], F32), "rstd": Y.alloc([512], F32)}

    xk = lambda k, t0: [("x", k, t0)]
    hk = lambda k, t0: [("h", k, t0)]
    x_fn = lambda k, t0, n: xT[:, k, t0:t0 + n]
    h_fn = lambda k, t0, n: hT[:, k, t0:t0 + n]

    def proj_kx(slot, skeys, off, blk, rhs_fn, rhs_keys_fn, nk=8, W=512):
        t0, n = blk
        b = bank()
        v = slot[:, 0:nk * W].rearrange("p (k c) -> p k c", k=nk)
        for k in range(nk):
            P.op("pe", lambda e, k=k, b=b, v=v: e.matmul(pb(b)[:, 0:n], lhsT=v[:, k, off:off + 128], rhs=rhs_fn(k, t0, n),
                                                        start=(k == 0), stop=(k == nk - 1)),
                 reads=skeys + rhs_keys_fn(k, t0), writes=pbk(b))
        return b

    def ffn(l, which):
        wgu, wdn = T["ffn%d_w_gu" % which][l], T["ffn%d_w_down" % which][l]
        P.cur = "ffn%d" % which
        Y.reset()
        tmp = norm_tmp()
        rmsnorm(x_fn, xk, TBLK, "ffn%d_norm" % which, l, h_fn, hk, tmp)
        sg = [Y.alloc([512], BF16), Y.alloc([512], BF16)]
        BIG.reset()
        hid = BIG.alloc([12, NTOK], BF16)
        for (f0, nf) in ((0, 12), (12, 10)):
            for pi in range(nf // 2):
                fa = f0 + 2 * pi
                slot, sk = WS.get([(kx_view(256, 0, 512), wgu[:, fa * 128:fa * 128 + 256].rearrange("(k p) c -> p k c", p=128)),
                                   (kx_view(256, 256, 512), wgu[:, DFF + fa * 128:DFF + fa * 128 + 256].rearrange("(k p) c -> p k c", p=128))])
                for j in range(2):
                    fi = 2 * pi + j
                    for bi, blk in enumerate(TBLK):
                        t0, n = blk
                        bg = proj_kx(slot, sk, j * 128, blk, h_fn, hk)
                        bu = proj_kx(slot, sk, 256 + j * 128, blk, h_fn, hk)
                        s = sg[(fi * 3 + bi) % 2]
                        skk = ("Y", "sg", (fi * 3 + bi) % 2)
                        P.op("act", lambda e, bg=bg, n=n, s=s: e.activation(out=s[:, 0:n], in_=pb(bg)[:, 0:n], func=AF.Silu), reads=pbk(bg), writes=[skk])
                        P.op("dve", lambda e, bu=bu, n=n, s=s, fi=fi, t0=t0: e.tensor_tensor(out=hid[:, fi, t0:t0 + n], in0=pb(bu)[:, 0:n], in1=s[:, 0:n], op=ALU.mult),
                             reads=pbk(bu) + [skk], writes=[("BIG", "hid", fi, t0)])
            for q in range(4):
                slot, sk = WS.get([(lambda sl, nf=nf: sl[:, 0:nf * 256].rearrange("p (f c) -> p f c", f=nf),
                                    wdn[f0 * 128:(f0 + nf) * 128, q * 256:(q + 1) * 256].rearrange("(f p) c -> p f c", p=128))])
                v = slot[:, 0:nf * 256].rearrange("p (f c) -> p f c", f=nf)
                for cc in range(2):
                    c = 2 * q + cc
                    for (t0, n) in TBLK:
                        b = bank()
                        for fi in range(nf):
                            P.op("pe", lambda e, fi=fi, b=b, v=v, cc=cc, t0=t0, n=n: e.matmul(pb(b)[:, 0:n], lhsT=v[:, fi, cc * 128:(cc + 1) * 128], rhs=hid[:, fi, t0:t0 + n],
                                                                                            start=(fi == 0), stop=(fi == nf - 1)),
                                 reads=sk + [("BIG", "hid", fi, t0)], writes=pbk(b))
                        P.op("dve", lambda e, b=b, c=c, t0=t0, n=n: e.scalar_tensor_tensor(out=xT[:, c, t0:t0 + n], in0=pb(b)[:, 0:n], scalar=0.5, in1=xT[:, c, t0:t0 + n],
                                                                                           op0=ALU.mult, op1=ALU.add),
                             reads=pbk(b) + xk(c, t0), writes=xk(c, t0))

    def transpose_out(src_fn, src_keys, nrow, dst_fn_list):
        b = bank(2)
        for k in range(8):
            P.op("pe", lambda e, k=k, b=b: e.transpose(pb(b, 2)[0:nrow, k * 128:(k + 1) * 128], src_fn(k), identf[:]),
                 reads=src_keys + ["identf"], writes=pbk(b, 2))
        t_io, kio = ioslot()
        copy_op(evac_eng(), t_io[0:nrow, :], pb(b, 2)[0:nrow, :], pbk(b, 2), [kio])
        outs = []
        for (r0, nr, dst, semk) in dst_fn_list:
            outs.append(P.dma("sp", lambda e, r0=r0, nr=nr, dst=dst, t_io=t_io: e.dma_start(out=dst, in_=t_io[r0:r0 + nr, :]), semk, reads=[kio]))
        finals.extend(outs)

    def load_rows_T(src_ap, nrow, dst_fn, dst_keys):
        t_io, kio = ioslot()
        P.dma("sp", lambda e, t_io=t_io: e.dma_start(out=t_io[0:nrow, :], in_=src_ap), kio, writes=[kio])
        for hlf in range(2):
            b = bank()
            for kk in range(4):
                k = hlf * 4 + kk
                P.op("pe", lambda e, k=k, kk=kk, b=b, t_io=t_io: e.transpose(pb(b)[:, kk * nrow:(kk + 1) * nrow], t_io[0:nrow, k * 128:(k + 1) * 128], identf[0:nrow, 0:nrow]),
                     reads=[kio, "identf"], writes=pbk(b))
            copy_op(evac_eng(), dst_fn(hlf), pb(b)[:, 0:4 * nrow].rearrange("p (k f) -> p k f", k=4), pbk(b), dst_keys)

    def mixer(l, p):
        last = (p == NPASS - 1)
        win = T["w_in"][l]
        s0 = p * NS
        P.cur = "mixnorm"
        Y.reset()
        tmp = norm_tmp()
        rmsnorm(x_fn, xk, TBLK, "mix_norm", l, h_fn, hk, tmp)
        BIG.reset()
        merged = BIG.alloc([8, NTOK], F32)
        mk = lambda c, t0: [("BIG", "m", c, t0)]

        def win_piece(c0a, c0b):
            return WS.get([(kx_view(256, 0, 512), win[:, c0a:c0a + 256].rearrange("(k p) c -> p k c", p=128)),
                           (kx_view(256, 256, 512), win[:, c0b:c0b + 256].rearrange("(k p) c -> p k c", p=128))])

        P.cur = "A"
        Y.reset()
        spoolT = Y.alloc([8, NS * PB], F32)
        ustage = Y.alloc([8, NST], F32)
        LA = PL + NS * PW
        uext = Y.alloc([LA], F32)
        sAB = [Y.alloc([LA], F32), Y.alloc([LA], F32)]
        pooled2 = [Y.alloc([2, NTOK], BF16), Y.alloc([2, NTOK], BF16)]
        sigA2 = [[Y.alloc([NTOK], BF16), Y.alloc([NTOK], BF16)] for _ in range(2)]
        t15 = Y.alloc([16], F32)
        load_rows_T(T["spool"][l, s0:s0 + NS].rearrange("s r c -> (s r) c"), NS * PB,
                    lambda hlf: spoolT[:, hlf * 4:(hlf + 1) * 4, :], [("Y", "spoolT")])
        finals.append(P.dma("sp", lambda e: e.dma_start(out=T["o_pool_s"][l, s0:s0 + NS, 0:PB - ST, :], in_=T["spool"][l, s0:s0 + NS, ST:PB, :]), ("dd", 0)))
        def stagePA(g):
            w = 2 << g
            pooled, sigA = pooled2[g % 2], sigA2[g % 2]
            slot, sk = win_piece(g * 256, GL + g * 256)
            for j in range(2):
                c = 2 * g + j
                for (t0, n) in TBLK:
                    b = proj_kx(slot, sk, j * 128, (t0, n), h_fn, hk)
                    if t0 < TP:
                        copy_op("act", uext[:, PB + t0:PB + t0 + n], pb(b)[:, 0:n], pbk(b), [("Y", "uext")])
                    else:
                        copy_op("act", uext[:, PL:LA].rearrange("p (s w) -> p s w", w=PW)[:, :, PB:PW],
                                pb(b)[:, 0:NST].rearrange("p (s t) -> p s t", t=ST), pbk(b), [("Y", "uext")])
                P.op("pool", lambda e, c=c: e.tensor_copy(out=uext[:, 0:PB], in_=poolst[:, l, c, :]), reads=[("pst", l, c)], writes=[("Y", "uext")])
                P.op("pool", lambda e, c=c: e.tensor_copy(out=uext[:, PL:LA].rearrange("p (s w) -> p s w", w=PW)[:, :, 0:PB],
                                                          in_=spoolT[:, c, :].rearrange("p (s r) -> p s r", r=PB)), reads=[("Y", "spoolT")], writes=[("Y", "uext")])
                P.op("pool", lambda e, c=c: e.tensor_copy(out=poolst[:, l, c, :], in_=uext[:, TP:TP + PB]), reads=[("Y", "uext")], writes=[("pst", l, c)])
                P.op("pool", lambda e, c=c: e.tensor_copy(out=ustage[:, c, :].rearrange("p (s t) -> p s t", t=ST),
                                                          in_=uext[:, PL:LA].rearrange("p (s w) -> p s w", w=PW)[:, :, PB:PW]), reads=[("Y", "uext")], writes=[("Y", "ustage")])
                src, srck = uext, ("Y", "uext")
                for stp in range(g + 1):
                    sh = 1 << stp
                    dstt, dk = sAB[stp % 2], ("Y", "sAB", stp % 2)
                    P.op("dve", lambda e, sh=sh, src=src, dstt=dstt: e.tensor_tensor(out=dstt[:, sh:LA], in0=src[:, sh:LA], in1=src[:, 0:LA - sh], op=ALU.add),
                         reads=[srck], writes=[dk])
                    src, srck = dstt, dk
                pk = ("Y", "pooled", g % 2, j)
                P.op("dve", lambda e, src=src, j=j, w=w: e.scalar_tensor_tensor(out=pooled[:, j, 0:TP], in0=src[:, PB:PL], scalar=1.0 / w, in1=uext[:, PB:PL],
                                                                               op0=ALU.mult, op1=ALU.subtract), reads=[srck, ("Y", "uext")], writes=[pk])
                if p == 0:
                    P.op("dve", lambda e, src=src, g=g: e.tensor_tensor(out=t15[:, 0:PB], in0=src[:, PB:2 * PB], in1=rcnt[:, g, 0:PB], op=ALU.mult),
                         reads=[srck, "rcnt"], writes=[("Y", "t15")])
                    P.op("dve", lambda e, j=j: e.tensor_tensor(out=pooled[:, j, 0:PB], in0=t15[:, 0:PB], in1=uext[:, PB:2 * PB], op=ALU.subtract),
                         reads=[("Y", "t15"), ("Y", "uext"), pk], writes=[pk])
                P.op("dve", lambda e, src=src, j=j, w=w: e.scalar_tensor_tensor(
                    out=pooled[:, j, TP:NTOK].rearrange("p (s t) -> p s t", t=ST),
                    in0=src[:, PL:LA].rearrange("p (s w) -> p s w", w=PW)[:, :, PB:PW], scalar=1.0 / w,
                    in1=uext[:, PL:LA].rearrange("p (s w) -> p s w", w=PW)[:, :, PB:PW], op0=ALU.mult, op1=ALU.subtract),
                    reads=[srck, ("Y", "uext"), pk], writes=[pk])
            for j in range(2):
                for (t0, n) in TBLK:
                    b = proj_kx(slot, sk, 256 + j * 128, (t0, n), h_fn, hk)
                    P.op("act", lambda e, b=b, j=j, t0=t0, n=n: e.activation(out=sigA[j][:, t0:t0 + n], in_=pb(b)[:, 0:n], func=AF.Sigmoid),
                         reads=pbk(b), writes=[("Y", "sigA", g % 2, j, t0)])
        def stageVA(g):
            pooled, sigA = pooled2[g % 2], sigA2[g % 2]
            wps, wpk = WS.get([(lambda sl: sl[:, 0:512].rearrange("p (k c) -> p k c", k=2),
                                T["w_pool"][l, g].rearrange("(k p) c -> p k c", p=128))])
            wpv = wps[:, 0:512].rearrange("p (k c) -> p k c", k=2)
            for j2 in range(2):
                c = 2 * g + j2
                for (t0, n) in TBLK:
                    b = bank()
                    for kk in range(2):
                        P.op("pe", lambda e, kk=kk, b=b, wpv=wpv, j2=j2, t0=t0, n=n: e.matmul(pb(b)[:, 0:n], lhsT=wpv[:, kk, j2 * 128:(j2 + 1) * 128],
                                                                                        rhs=pooled[:, kk, t0:t0 + n], start=(kk == 0), stop=(kk == 1)),
                             reads=wpk + [("Y", "pooled", g % 2, kk)], writes=pbk(b))
                    P.op("dve", lambda e, b=b, c=c, j2=j2, t0=t0, n=n: e.scalar_tensor_tensor(out=merged[:, c, t0:t0 + n], in0=pb(b)[:, 0:n], scalar=vcol("pool_scale", l, c),
                                                                                             in1=sigA[j2][:, t0:t0 + n], op0=ALU.mult, op1=ALU.mult),
                         reads=pbk(b) + [("Y", "sigA", g % 2, j2, t0), "vecT"], writes=mk(c, t0))
        stagePA(0)
        for g in range(4):
            if g + 1 < 4:
                stagePA(g + 1)
            stageVA(g)
        transpose_out(lambda k: ustage[:, k, :], [("Y", "ustage")], NST,
                      [(s * ST, ST, T["o_pool_s"][l, s0 + s, PB - ST:PB, :], ("ops", s)) for s in range(NS)])
        if last:
            transpose_out(lambda k: poolst[:, l, k, :], [("pst", l, k) for k in range(8)], PB, [(0, PB, T["o_pool_p"][l], ("opp", 0))])

        P.cur = "B"
        Y.reset()
        yb = Y.alloc([8, NTOK], BF16)
        keepB = Y.off
        zstage = Y.alloc([8, NST], F32)
        sconvT = Y.alloc([8, NS * CB], F32)
        LB = CL + NS * CW
        zc = [Y.alloc([LB], BF16), Y.alloc([LB], BF16)]
        diag2 = [Y.alloc([31, 128], BF16), Y.alloc([31, 128], BF16)]
        sz = Y.alloc([NTOK], BF16)
        acc = Y.alloc([NTOK], F32)
        for hf in range(2):
            load_rows_T(T["sconv"][l, s0 + hf * 4:s0 + hf * 4 + 4].rearrange("s r c -> (s r) c"), 4 * CB,
                        lambda hlf, hf=hf: sconvT[:, hlf * 4:(hlf + 1) * 4, hf * 4 * CB:(hf + 1) * 4 * CB], [("Y", "sconvT")])
        finals.append(P.dma("sp", lambda e: e.dma_start(out=T["o_conv_s"][l, s0:s0 + NS, 0:CB - ST, :], in_=T["sconv"][l, s0:s0 + NS, ST:CB, :]), ("dd", 1)))
        pieceB = {}

        def stagePB(c):
            q, j = c // 2, c % 2
            if j == 0:
                pieceB["s"] = win_piece(1024 + q * 256, 2048 + q * 256)
            slot, sk = pieceB["s"]
            zt, zk = zc[c % 2], ("Y", "zc", c % 2)
            diag, dgk = diag2[c % 2], ("Y", "diag", c % 2)
            for (t0, n) in TBLK:
                bgt = proj_kx(slot, sk, 256 + j * 128, (t0, n), h_fn, hk)
                P.op("act", lambda e, bgt=bgt, t0=t0, n=n: e.activation(out=sz[:, t0:t0 + n], in_=pb(bgt)[:, 0:n], func=AF.Sigmoid), reads=pbk(bgt), writes=[("Y", "sz", t0)])
                ba = proj_kx(slot, sk, j * 128, (t0, n), h_fn, hk)
                if t0 < TP:
                    P.op("dve", lambda e, ba=ba, zt=zt, t0=t0, n=n: e.tensor_tensor(out=zt[:, CB + t0:CB + t0 + n], in0=pb(ba)[:, 0:n], in1=sz[:, t0:t0 + n], op=ALU.mult),
                         reads=pbk(ba) + [("Y", "sz", t0)], writes=[zk])
                    if t0 + n == TP:
                        P.op("pool", lambda e, zt=zt, c=c: e.tensor_copy(out=zt[:, 0:CB], in_=convst[:, l, c, :]), reads=[("cst", l, c)], writes=[zk])
                        P.op("dve", lambda e, ba=ba, c=c, n=n: e.tensor_tensor(out=convst[:, l, c, :], in0=pb(ba)[:, n - CB:n], in1=sz[:, TP - CB:TP], op=ALU.mult),
                             reads=pbk(ba) + [("Y", "sz", t0)], writes=[("cst", l, c)])
                else:
                    P.op("dve", lambda e, ba=ba, zt=zt: e.tensor_tensor(out=zt[:, CL:LB].rearrange("p (s w) -> p s w", w=CW)[:, :, CB:CW],
                                                                      in0=pb(ba)[:, 0:NST].rearrange("p (s t) -> p s t", t=ST),
                                                                      in1=sz[:, TP:NTOK].rearrange("p (s t) -> p s t", t=ST), op=ALU.mult),
                         reads=pbk(ba) + [("Y", "sz", t0)], writes=[zk])
                    P.op("dve", lambda e, ba=ba, c=c: e.tensor_tensor(out=zstage[:, c, :], in0=pb(ba)[:, 0:NST], in1=sz[:, TP:NTOK], op=ALU.mult),
                         reads=pbk(ba) + [("Y", "sz", t0)], writes=[("Y", "zstage")])
            P.op("pool", lambda e, zt=zt, c=c: e.tensor_copy(out=zt[:, CL:LB].rearrange("p (s w) -> p s w", w=CW)[:, :, 0:CB],
                                                             in_=sconvT[:, c, :].rearrange("p (s r) -> p s r", r=CB)), reads=[("Y", "sconvT")], writes=[zk])
            P.op("pool", lambda e, c=c: e.tensor_tensor(out=diag[:, :, :], in0=bass.AP(identb, 0, [[128, 128], [0, 31], [1, 128]]),
                                                        in1=bass.AP(wdwT, c * 124 + l * 31, [[992, 128], [1, 31], [0, 128]]), op=ALU.mult),
                 reads=["identb", "wdwT"], writes=[dgk])
        NDVE = 5

        def stageVB(c):
            zt, zk = zc[c % 2], ("Y", "zc", c % 2)
            diag, dgk = diag2[c % 2], ("Y", "diag", c % 2)
            zs = zt[:, CL:LB].rearrange("p (s w) -> p s w", w=CW)
            accs = acc[:, TP:NTOK].rearrange("p (s t) -> p s t", t=ST)
            for jj in range(NDVE):
                wc = wdwT[:, c, l * 31 + jj:l * 31 + jj + 1]
                if jj == 0:
                    P.op("dve", lambda e, wc=wc, zt=zt: e.tensor_scalar(out=acc[:, 0:TP], in0=zt[:, 0:TP], scalar1=wc, scalar2=0.0, op0=ALU.mult, op1=ALU.add),
                         reads=[zk, "wdwT"], writes=[("Y", "acc", 0)])
                    P.op("dve", lambda e, wc=wc, zs=zs, accs=accs: e.tensor_scalar(out=accs, in0=zs[:, :, 0:ST], scalar1=wc, scalar2=0.0, op0=ALU.mult, op1=ALU.add),
                         reads=[zk, "wdwT"], writes=[("Y", "acc", 1)])
                else:
                    P.op("dve", lambda e, wc=wc, zt=zt, jj=jj: e.scalar_tensor_tensor(out=acc[:, 0:TP], in0=zt[:, jj:jj + TP], scalar=wc, in1=acc[:, 0:TP], op0=ALU.mult, op1=ALU.add),
                         reads=[zk, "wdwT", ("Y", "acc", 0)], writes=[("Y", "acc", 0)])
                    P.op("dve", lambda e, wc=wc, zs=zs, accs=accs, jj=jj: e.scalar_tensor_tensor(out=accs, in0=zs[:, :, jj:jj + ST], scalar=wc, in1=accs, op0=ALU.mult, op1=ALU.add),
                         reads=[zk, "wdwT", ("Y", "acc", 1)], writes=[("Y", "acc", 1)])
            for (t0, n) in TBLK:
                b = bank()
                for jj in range(NDVE, 31):
                    if t0 < TP:
                        rhs_fn = lambda jj=jj, zt=zt, t0=t0, n=n: zt[:, t0 + jj:t0 + jj + n]
                    else:
                        rhs_fn = lambda jj=jj, zs=zs: zs[:, :, jj:jj + ST]
                    P.op("pe", lambda e, jj=jj, b=b, n=n, rhs_fn=rhs_fn, diag=diag: e.matmul(pb(b)[:, 0:n], lhsT=diag[:, jj, :], rhs=rhs_fn(), start=(jj == NDVE), stop=(jj == 30)),
                         reads=[zk, dgk], writes=pbk(b))
                ak = ("Y", "acc", 0 if t0 < TP else 1)
                P.op("dve", lambda e, b=b, c=c, t0=t0, n=n: e.scalar_tensor_tensor(out=yb[:, c, t0:t0 + n], in0=pb(b)[:, 0:n], scalar=vcol("b_dw", l, c), in1=acc[:, t0:t0 + n],
                                                                                   op0=ALU.add, op1=ALU.add),
                     reads=pbk(b) + ["vecT", ak], writes=[("Y", "yb", c, t0)])
        stagePB(0)
        for c in range(8):
            if c + 1 < 8:
                stagePB(c + 1)
            stageVB(c)
        transpose_out(lambda k: zstage[:, k, :], [("Y", "zstage")], NST,
                      [(s * ST, ST, T["o_conv_s"][l, s0 + s, CB - ST:CB, :], ("ocs", s)) for s in range(NS)])
        if last:
            transpose_out(lambda k: convst[:, l, k, :], [("cst", l, k) for k in range(8)], CB, [(0, CB, T["o_conv_p"][l], ("ocp", 0))])
        P.cur = "B_ln"
        Y.reset(keepB)
        sq2 = [Y.alloc([512], BF16), Y.alloc([512], BF16)]
        meanb, m2, rt, rstd, nmr = (Y.alloc([512], F32) for _ in range(5))
        t1 = [Y.alloc([512], F32), Y.alloc([512], F32), Y.alloc([512], F32), Y.alloc([512], F32)]
        sigB8 = Y.alloc([8, NTOK], BF16)
        tmpB = t1[0:2]
        yb_fn = lambda k, t0, n: yb[:, k, t0:t0 + n]
        ybk = lambda k, t0: [("Y", "yb", k, t0)]
        def gate_b(q):
            sg_, sgk = WS.get([(kx_view(512, 0, 512), win[:, GL + 1024 + q * 512:GL + 1024 + (q + 1) * 512].rearrange("(k p) c -> p k c", p=128))])
            for cc in range(4):
                c = 4 * q + cc
                for (t0, n) in TBLK:
                    b = proj_kx(sg_, sgk, cc * 128, (t0, n), h_fn, hk)
                    P.op("act", lambda e, b=b, c=c, t0=t0, n=n: e.activation(out=sigB8[:, c, t0:t0 + n], in_=pb(b)[:, 0:n], func=AF.Sigmoid), reads=pbk(b), writes=[("Y", "sigB", c, t0)])
                    yield

        statb = []
        gen0 = gate_b(0)
        step = 0
        for (t0, n) in TBLK:
            bm, be = bank(), bank()
            reserved.update((bm, be))
            statb.append((bm, be))
            for k in range(8):
                sq = sq2[k % 2]
                sqk = ("Y", "sq2", k % 2)
                eng = "act" if k % 2 == 0 else "dve"
                if eng == "act":
                    P.op("act", lambda e, k=k, sq=sq, t0=t0, n=n: e.activation(out=sq[:, 0:n], in_=yb[:, k, t0:t0 + n], func=AF.Square), reads=[("Y", "yb", k, t0)], writes=[sqk])
                else:
                    P.op("dve", lambda e, k=k, sq=sq, t0=t0, n=n: e.tensor_tensor(out=sq[:, 0:n], in0=yb[:, k, t0:t0 + n], in1=yb[:, k, t0:t0 + n], op=ALU.mult), reads=[("Y", "yb", k, t0)], writes=[sqk])
                P.op("pe", lambda e, k=k, bm=bm, t0=t0, n=n: e.matmul(pb(bm)[:, 0:n], lhsT=onesm[:], rhs=yb[:, k, t0:t0 + n], start=(k == 0), stop=(k == 7)),
                     reads=[("Y", "yb", k, t0), "onesm"], writes=pbk(bm))
                P.op("pe", lambda e, k=k, be=be, sq=sq, n=n: e.matmul(pb(be)[:, 0:n], lhsT=onesm[:], rhs=sq[:, 0:n], start=(k == 0), stop=(k == 7)),
                     reads=[sqk, "onesm"], writes=pbk(be))
                step += 1
                if step % 2 == 0:
                    pull(gen0, 1)
        pull(gen0, 100)
        for bi, (t0, n) in enumerate(TBLK):
            bm, be = statb[bi]
            P.op("act", lambda e, bm=bm, n=n: e.copy(out=meanb[:, 0:n], in_=pb(bm)[:, 0:n]), reads=pbk(bm), writes=[("Y", "meanb")])
            P.op("dve", lambda e, n=n: e.tensor_tensor(out=m2[:, 0:n], in0=meanb[:, 0:n], in1=meanb[:, 0:n], op=ALU.mult), reads=[("Y", "meanb")], writes=[("Y", "m2")])
            P.op("dve", lambda e, be=be, n=n: e.tensor_tensor(out=m2[:, 0:n], in0=pb(be)[:, 0:n], in1=m2[:, 0:n], op=ALU.subtract), reads=pbk(be) + [("Y", "m2")], writes=[("Y", "m2")])
            reserved.difference_update((bm, be))
            P.op("dve", lambda e, n=n: e.tensor_scalar(out=m2[:, 0:n], in0=m2[:, 0:n], scalar1=0.0, scalar2=0.0, op0=ALU.max, op1=ALU.add), reads=[("Y", "m2")], writes=[("Y", "m2")])
            P.op("act", lambda e, n=n: e.activation(out=rt[:, 0:n], in_=m2[:, 0:n], func=AF.Sqrt, bias=epsT[:, 0:1], scale=1.0), reads=[("Y", "m2"), "epsT"], writes=[("Y", "rt2")])
            P.op("dve", lambda e, n=n: e.reciprocal(out=rstd[:, 0:n], in_=rt[:, 0:n]), reads=[("Y", "rt2")], writes=[("Y", "rstd2")])
            P.op("dve", lambda e, n=n: e.scalar_tensor_tensor(out=nmr[:, 0:n], in0=meanb[:, 0:n], scalar=-1.0, in1=rstd[:, 0:n], op0=ALU.mult, op1=ALU.mult),
                 reads=[("Y", "meanb"), ("Y", "rstd2")], writes=[("Y", "nmr")])
            for k0 in range(0, 8, 2):
                eng = "dve"
                for k in (k0, k0 + 1):
                    tt, tk = t1[k % 4], ("Y", "t1", k % 4)
                    P.op(eng, lambda e, k=k, tt=tt, t0=t0, n=n: e.tensor_tensor(out=tt[:, 0:n], in0=yb[:, k, t0:t0 + n], in1=rstd[:, 0:n], op=ALU.mult),
                         reads=[("Y", "yb", k, t0), ("Y", "rstd2")], writes=[tk])
                for k in (k0, k0 + 1):
                    tt, tk = t1[k % 4], ("Y", "t1", k % 4)
                    P.op(eng, lambda e, tt=tt, n=n: e.tensor_tensor(out=tt[:, 0:n], in0=tt[:, 0:n], in1=nmr[:, 0:n], op=ALU.add), reads=[tk, ("Y", "nmr")], writes=[tk])
                    P.op("act", lambda e, k=k, tt=tt, t0=t0, n=n: e.activation(out=yb[:, k, t0:t0 + n], in_=tt[:, 0:n], func=AF.Silu, bias=vcol("conv_ln_b", l, k), scale=vcol("conv_ln_g", l, k)),
                         reads=[tk, "vecT"], writes=[("Y", "yb", k, t0)])
        pull(gate_b(1), 100)
        P.cur = "B_pw"
        ti = 0
        for q in range(2):
            sp_, spk = WS.get([(kx_view(512, 0, 512), T["w_pw"][l][:, q * 512:(q + 1) * 512].rearrange("(k p) c -> p k c", p=128))])
            for cc in range(4):
                c = 4 * q + cc
                for (t0, n) in TBLK:
                    b2 = proj_kx(sp_, spk, cc * 128, (t0, n), yb_fn, ybk)
                    tb, tbk = tmpB[ti % 2], ("Y", "t1", ti % 2)
                    ti += 1
                    P.op("dve", lambda e, b2=b2, tb=tb, c=c, t0=t0, n=n: e.tensor_tensor(out=tb[:, 0:n], in0=pb(b2)[:, 0:n], in1=sigB8[:, c, t0:t0 + n], op=ALU.mult),
                         reads=pbk(b2) + [("Y", "sigB", c, t0)], writes=[tbk])
                    P.op("pool", lambda e, tb=tb, c=c, t0=t0, n=n: e.tensor_tensor(out=merged[:, c, t0:t0 + n], in0=merged[:, c, t0:t0 + n], in1=tb[:, 0:n], op=ALU.add),
                         reads=[tbk] + mk(c, t0), writes=mk(c, t0))

        P.cur = "C_kv"
        Y.reset()
        KTp = Y.alloc([8, NMEM], BF16)
        Vp = Y.alloc([2, D], BF16)
        keepC = Y.off
        memn = Y.alloc([8, NMEM], BF16)
        kvst = [Y.alloc([512], F32), Y.alloc([512], F32)]
        tmpn = norm_tmp()
        rmsnorm(lambda k, t0, n: memT[:, k, t0:t0 + n], lambda k, t0: ["memT"], [(0, NMEM)], "mem_norm", l,
                lambda k, t0, n: memn[:, k, t0:t0 + n], lambda k, t0: [("Y", "memn", k)], tmpn)
        mn_fn = lambda k, t0, n: memn[:, k, t0:t0 + n]
        mnk = lambda k, t0: [("Y", "memn", k)]
        wkv = T["w_mem_kv"][l]
        kvi = 0
        for q in range(4):
            slot, sk = WS.get([(kx_view(512, 0, 512), wkv[:, q * 512:(q + 1) * 512].rearrange("(k p) c -> p k c", p=128))])
            v = slot[:, 0:4096].rearrange("p (k c) -> p k c", k=8)
            if q < 2:
                for cc in range(4):
                    c = 4 * q + cc
                    b = proj_kx(slot, sk, cc * 128, (0, NMEM), mn_fn, mnk)
                    copy_op(evac_eng(), KTp[:, c, :], pb(b)[:, 0:NMEM], pbk(b), [("Y", "KTp", c)])
            if q >= 2 or p == 0:
                for mt in range(2):
                    b = bank()
                    for k in range(8):
                        P.op("pe", lambda e, k=k, b=b, mt=mt, v=v: e.matmul(pb(b)[:, :], lhsT=memn[:, k, mt * 128:(mt + 1) * 128], rhs=v[:, k, :], start=(k == 0), stop=(k == 7)),
                             reads=sk + [("Y", "memn", k)], writes=pbk(b))
                    if q >= 2 and p != 0:
                        copy_op("dve", Vp[:, mt, (q - 2) * 512:(q - 1) * 512], pb(b)[:, :], pbk(b), [("Y", "Vp", mt, q - 2)])
                    if p == 0:
                        ks, kk_ = kvst[kvi % 2], ("Y", "kvst", kvi % 2)
                        kvi += 1
                        copy_op("act", ks[:, :], pb(b)[:, :], pbk(b), [kk_])
                        if q >= 2:
                            copy_op("dve", Vp[:, mt, (q - 2) * 512:(q - 1) * 512], ks[:, :], [kk_], [("Y", "Vp", mt, q - 2)])
                        dst = (T["o_mk"] if q < 2 else T["o_mv"])[l, mt * 128:(mt + 1) * 128, (q % 2) * 512:(q % 2 + 1) * 512]
                        finals.append(P.dma("sp", lambda e, dst=dst, ks=ks: e.dma_start(out=dst, in_=ks[:, :]), ("okv", kvi % 2), reads=[kk_]))
        P.cur = "C_prompt"
        Y.reset(keepC)
        qTs = Y.alloc([8, NST], BF16)
        sigCs = Y.alloc([8, NST], BF16)
        qmask = Y.alloc([NS, 8, NST], BF16)
        PTm = Y.alloc([NS, 8, NST], BF16)
        keepC2 = Y.off
        qT2 = [Y.alloc([2, TP], BF16), Y.alloc([2, TP], BF16)]
        sigC2 = [[Y.alloc([TP], BF16), Y.alloc([TP], BF16)] for _ in range(2)]
        PT = Y.alloc([2, TP], BF16)
        Pe = [Y.alloc([NMEM], BF16), Y.alloc([NMEM], BF16)]
        Pn = [Y.alloc([NMEM], BF16), Y.alloc([NMEM], BF16)]
        smx = Y.alloc([4, 8], F32)
        tmpC = [Y.alloc([512], F32), Y.alloc([512], F32)]
        P.op("pool", lambda e: e.memset(qmask[:, :, :, :], 0.0), writes=[("Y", "qmask")])
        P.op("pool", lambda e: e.memset(PTm[:, :, :, :], 0.0), writes=[("Y", "PTm")])
        ci = [0]

        def proj_gen(h):
            slot, sk = win_piece(3072 + h * 256, GL + 2048 + h * 256)
            qT, sigC = qT2[h % 2], sigC2[h % 2]
            for j in range(2):
                for (t0, n) in TBLK:
                    b = proj_kx(slot, sk, j * 128, (t0, n), h_fn, hk)
                    if t0 < TP:
                        P.op("act", lambda e, b=b, j=j, t0=t0, n=n, qT=qT: e.mul(out=qT[:, j, t0:t0 + n], in_=pb(b)[:, 0:n], mul=0.0625), reads=pbk(b), writes=[("Y", "qT", h % 2, j, t0)])
                    else:
                        P.op("act", lambda e, b=b, j=j, h=h: e.mul(out=qTs[:, 2 * h + j, :], in_=pb(b)[:, 0:NST], mul=0.0625), reads=pbk(b), writes=[("Y", "qTs")])
                    yield
                    b = proj_kx(slot, sk, 256 + j * 128, (t0, n), h_fn, hk)
                    if t0 < TP:
                        P.op("act", lambda e, b=b, j=j, t0=t0, n=n, sigC=sigC: e.activation(out=sigC[j][:, t0:t0 + n], in_=pb(b)[:, 0:n], func=AF.Sigmoid), reads=pbk(b), writes=[("Y", "sigC", h % 2, j, t0)])
                    else:
                        P.op("act", lambda e, b=b, j=j, h=h: e.activation(out=sigCs[:, 2 * h + j, :], in_=pb(b)[:, 0:NST], func=AF.Sigmoid), reads=pbk(b), writes=[("Y", "sigCs")])
                    yield

        def scores(h, i):
            qT = qT2[h % 2]
            t0b = (i * 128) // 512 * 512
            b = bank()
            reserved.add(b)
            for dc in range(2):
                P.op("pe", lambda e, dc=dc, b=b, i=i, h=h, qT=qT: e.matmul(pb(b)[:, 0:NMEM], lhsT=qT[:, dc, i * 128:(i + 1) * 128], rhs=KTp[:, 2 * h + dc, :], start=(dc == 0), stop=(dc == 1)),
                     reads=[("Y", "qT", h % 2, dc, t0b), ("Y", "KTp", 2 * h + dc)], writes=pbk(b))
            return b

        for _ in proj_gen(0):
            pass
        for h in range(4):
            sigC = sigC2[h % 2]
            gen = proj_gen(h + 1) if h + 1 < 4 else None
            if h == 3:
                for s in range(NS):
                    P.op("pool", lambda e, s=s: e.tensor_copy(out=qmask[:, s, :, s * ST:(s + 1) * ST], in_=qTs[:, :, s * ST:(s + 1) * ST]), reads=[("Y", "qTs"), ("Y", "qmask")], writes=[("Y", "qmask")])
            nt = TP // 128
            sb = {0: scores(h, 0)}
            for i in range(nt):
                b = sb.pop(i)
                pe_, pek = Pe[i % 2], ("Y", "Pe", i % 2)
                pn_, pnk = Pn[i % 2], ("Y", "Pn", i % 2)
                mi = i % 2
                P.op("dve", lambda e, b=b, mi=mi: e.tensor_reduce(out=smx[:, 0, mi:mi + 1], in_=pb(b)[:, 0:NMEM], axis=AX.X, op=ALU.max), reads=pbk(b), writes=[("Y", "mx", mi)])
                P.op("dve", lambda e, mi=mi: e.tensor_scalar(out=smx[:, 1, mi:mi + 1], in0=smx[:, 0, mi:mi + 1], scalar1=-1.0, scalar2=0.0, op0=ALU.mult, op1=ALU.add), reads=[("Y", "mx", mi)], writes=[("Y", "nmx", mi)])
                P.op("act", lambda e, b=b, pe_=pe_, mi=mi: e.activation(out=pe_[:, :], in_=pb(b)[:, 0:NMEM], func=AF.Exp, bias=smx[:, 1, mi:mi + 1], scale=1.0, accum_out=smx[:, 2, mi:mi + 1]),
                     reads=pbk(b) + [("Y", "nmx", mi)], writes=[pek, ("Y", "rs", mi)])
                reserved.discard(b)
                P.op("dve", lambda e, mi=mi: e.reciprocal(out=smx[:, 3, mi:mi + 1], in_=smx[:, 2, mi:mi + 1]), reads=[("Y", "rs", mi)], writes=[("Y", "rr", mi)])
                P.op("dve", lambda e, pe_=pe_, pn_=pn_, mi=mi: e.tensor_scalar(out=pn_[:, :], in0=pe_[:, :], scalar1=smx[:, 3, mi:mi + 1], scalar2=0.0, op0=ALU.mult, op1=ALU.add),
                     reads=[pek, ("Y", "rr", mi)], writes=[pnk])
                if i + 1 < nt:
                    sb[i + 1] = scores(h, i + 1)
                pull(gen, 2 if i % 2 == 0 else 1)
                b2 = bank()
                pst = pb(b2).bitcast(BF16)
                for mc in range(2):
                    P.op("pe", lambda e, mc=mc, pst=pst, pn_=pn_: e.transpose(pst[:, mc * 128:(mc + 1) * 128], pn_[:, mc * 128:(mc + 1) * 128], identb[:]),
                         reads=[pnk, "identb"], writes=pbk(b2))
                copy_op("act", PT[:, :, i * 128:(i + 1) * 128], pst[:, 0:256].rearrange("p (m t) -> p m t", m=2), pbk(b2), [("Y", "PT", i)])
            pull(gen, 100)
            for j in range(2):
                c = 2 * h + j
                for bi, (t0, n) in enumerate(PBLK):
                    b = bank()
                    for mc in range(2):
                        P.op("pe", lambda e, mc=mc, b=b, c=c, t0=t0, n=n: e.matmul(pb(b)[:, 0:n], lhsT=Vp[:, mc, c * 128:(c + 1) * 128], rhs=PT[:, mc, t0:t0 + n], start=(mc == 0), stop=(mc == 1)),
                             reads=[("Y", "Vp", mc, c // 4)] + [("Y", "PT", i) for i in range(t0 // 128, (t0 + n) // 128)], writes=pbk(b))
                    tb, tbk = tmpC[ci[0] % 2], ("Y", "tmpC", ci[0] % 2)
                    ci[0] += 1
                    P.op("dve", lambda e, b=b, tb=tb, j=j, t0=t0, n=n, sigC=sigC: e.tensor_tensor(out=tb[:, 0:n], in0=pb(b)[:, 0:n], in1=sigC[j][:, t0:t0 + n], op=ALU.mult),
                         reads=pbk(b) + [("Y", "sigC", h % 2, j, t0)], writes=[tbk])
                    P.op("pool", lambda e, tb=tb, c=c, t0=t0, n=n: e.tensor_tensor(out=merged[:, c, t0:t0 + n], in0=merged[:, c, t0:t0 + n], in1=tb[:, 0:n], op=ALU.add),
                         reads=[tbk] + mk(c, t0), writes=mk(c, t0))
        P.cur = "C_sample"
        Y.reset(keepC2)
        KTs = [Y.alloc([8, NMEM], BF16), Y.alloc([8, NMEM], BF16)]
        Pes = Y.alloc([4, NMEM], BF16)
        Pns = Y.alloc([4, NMEM], BF16)
        PTs = Y.alloc([8, NST], BF16)
        smy = Y.alloc([4, 8], F32)
        bs = bank(2)
        reserved.update((bs, bs + 1))
        for s in range(NS):
            slot, sk = WS.get([(lambda sl: sl[:, 0:2048].rearrange("p (m c) -> p m c", m=2), T["ck"][l, s0 + s].rearrange("(m p) c -> p m c", p=128))])
            kv = slot[:, 0:2048].rearrange("p (m c) -> p m c", m=2)
            kt, ktk = KTs[s % 2], ("Y", "KTs", s % 2)
            bt = bank(2)
            pst = pb(bt, 2).bitcast(BF16)
            for ch in range(8):
                for mc in range(2):
                    P.op("pe", lambda e, ch=ch, mc=mc, pst=pst, kv=kv: e.transpose(pst[:, ch * 256 + mc * 128:ch * 256 + (mc + 1) * 128], kv[:, mc, ch * 128:(ch + 1) * 128], identb[:]),
                         reads=sk + ["identb"], writes=pbk(bt, 2))
            copy_op(evac_eng(), kt[:, :, :], pst[:, :].rearrange("p (c m) -> p c m", c=8), pbk(bt, 2), [ktk])
            for h in range(4):
                for dc in range(2):
                    P.op("pe", lambda e, h=h, dc=dc, s=s, kt=kt: e.matmul(pb(bs, 2)[0:NST, h * 256:(h + 1) * 256], lhsT=qmask[:, s, 2 * h + dc, :], rhs=kt[:, 2 * h + dc, :],
                                                                     start=(s == 0 and dc == 0 and h % 2 == 0), stop=(s == NS - 1 and dc == 1)),
                         reads=[("Y", "qmask"), ktk], writes=pbk(bs, 2))
        sc = pb(bs, 2)[0:NST, :].rearrange("p (h m) -> p h m", h=4)
        P.op("dve", lambda e: e.tensor_reduce(out=smy[0:NST, 0, 0:4], in_=sc, axis=AX.X, op=ALU.max), reads=pbk(bs, 2), writes=[("Y", "mxs")])
        P.op("dve", lambda e: e.tensor_scalar(out=smy[0:NST, 1, 0:4], in0=smy[0:NST, 0, 0:4], scalar1=-1.0, scalar2=0.0, op0=ALU.mult, op1=ALU.add), reads=[("Y", "mxs")], writes=[("Y", "nmxs")])
        for h in range(4):
            P.op("act", lambda e, h=h: e.activation(out=Pes[0:NST, h, :], in_=sc[:, h, :], func=AF.Exp, bias=smy[0:NST, 1, h:h + 1], scale=1.0, accum_out=smy[0:NST, 2, h:h + 1]),
                 reads=pbk(bs, 2) + [("Y", "nmxs")], writes=[("Y", "Pes", h), ("Y", "rss", h)])
        P.op("dve", lambda e: e.reciprocal(out=smy[0:NST, 3, 0:4], in_=smy[0:NST, 2, 0:4]), reads=[("Y", "rss", h) for h in range(4)], writes=[("Y", "rrs")])
        for h in range(4):
            P.op("dve", lambda e, h=h: e.tensor_scalar(out=Pns[0:NST, h, :], in0=Pes[0:NST, h, :], scalar1=smy[0:NST, 3, h:h + 1], scalar2=0.0, op0=ALU.mult, op1=ALU.add),
                 reads=[("Y", "Pes", h), ("Y", "rrs")], writes=[("Y", "Pns", h)])
        b2 = bank()
        reserved.difference_update((bs, bs + 1))
        pst = pb(b2).bitcast(BF16)
        for h in range(4):
            for mc in range(2):
                ch = 2 * h + mc
                P.op("pe", lambda e, h=h, mc=mc, ch=ch, pst=pst: e.transpose(pst[:, ch * NST:(ch + 1) * NST], Pns[0:NST, h, mc * 128:(mc + 1) * 128], identb[0:NST, 0:NST]),
                     reads=[("Y", "Pns", h), "identb"], writes=pbk(b2))
        copy_op("dve", PTs[:, :, :], pst[:, 0:8 * NST].rearrange("p (c t) -> p c t", c=8), pbk(b2), [("Y", "PTs")])
        for s in range(NS):
            P.op("pool", lambda e, s=s: e.tensor_copy(out=PTm[:, s, :, s * ST:(s + 1) * ST], in_=PTs[:, :, s * ST:(s + 1) * ST]), reads=[("Y", "PTs"), ("Y", "PTm")], writes=[("Y", "PTm")])
        bo = bank()
        reserved.add(bo)
        for s in range(NS):
            slot, sk = WS.get([(lambda sl: sl[:, 0:2048].rearrange("p (m c) -> p m c", m=2), T["cv"][l, s0 + s].rearrange("(m p) c -> p m c", p=128))])
            vv = slot[:, 0:2048].rearrange("p (m c) -> p m c", m=2)
            for c in range(8):
                for mc in range(2):
                    P.op("pe", lambda e, c=c, mc=mc, s=s, vv=vv: e.matmul(pb(bo)[:, c * NST:(c + 1) * NST], lhsT=vv[:, mc, c * 128:(c + 1) * 128], rhs=PTm[:, s, (c // 2) * 2 + mc, :],
                                                                     start=(s == 0 and mc == 0 and c == 0), stop=(s == NS - 1 and mc == 1)),
                         reads=sk + [("Y", "PTm")], writes=pbk(bo))
        reserved.discard(bo)
        tcs = Y.alloc([8, NST], F32)
        P.op("dve", lambda e: e.tensor_tensor(out=tcs[:, :, :], in0=pb(bo)[:, 0:8 * NST].rearrange("p (c t) -> p c t", c=8), in1=sigCs[:, :, :], op=ALU.mult),
             reads=pbk(bo) + [("Y", "sigCs")], writes=[("Y", "tcs")])
        P.op("pool", lambda e: e.tensor_tensor(out=merged[:, :, TP:NTOK], in0=merged[:, :, TP:NTOK], in1=tcs[:, :, :], op=ALU.add),
             reads=[("Y", "tcs")] + [("BIG", "m", c, TP) for c in range(8)], writes=[("BIG", "m", c, TP) for c in range(8)])

        P.cur = "wout"
        for k in range(8):
            for (t0, n) in TBLK:
                copy_op(evac_eng(), hT[:, k, t0:t0 + n], merged[:, k, t0:t0 + n], mk(k, t0), hk(k, t0))
        for q in range(2):
            slot, sk = WS.get([(kx_view(512, 0, 512), T["w_out"][l][:, q * 512:(q + 1) * 512].rearrange("(k p) c -> p k c", p=128))])
            for cc in range(4):
                c = 4 * q + cc
                for (t0, n) in TBLK:
                    b = proj_kx(slot, sk, cc * 128, (t0, n), h_fn, hk)
                    P.op("dve", lambda e, b=b, c=c, t0=t0, n=n: e.tensor_tensor(out=xT[:, c, t0:t0 + n], in0=pb(b)[:, 0:n], in1=xT[:, c, t0:t0 + n], op=ALU.add),
                         reads=pbk(b) + xk(c, t0), writes=xk(c, t0))

    for p in range(DBG_PASSES):
        P.cur = "xload"
        srcs = [(T["xp"][p * TP + i * 128:p * TP + (i + 1) * 128, :], 128, i * 128) for i in range(TP // 128)]
        srcs.append((T["xs"][p * NST:(p + 1) * NST, :], NST, TP))
        for (src, nr, c0) in srcs:
            t_io, kio = ioslot()
            P.dma("sp", lambda e, t_io=t_io, src=src, nr=nr: e.dma_start(out=t_io[0:nr, :], in_=src), kio, writes=[kio])
            for hlf in range(2):
                b = bank()
                for kk in range(4):
                    k = hlf * 4 + kk
                    P.op("pe", lambda e, k=k, kk=kk, b=b, t_io=t_io, nr=nr: e.transpose(pb(b)[:, kk * nr:(kk + 1) * nr], t_io[0:nr, k * 128:(k + 1) * 128], identf[0:nr, 0:nr]),
                         reads=[kio, "identf"], writes=pbk(b))
                t0b = min(c0 // 512 * 512, TP)
                copy_op(evac_eng(), xT[:, hlf * 4:(hlf + 1) * 4, c0:c0 + nr], pb(b)[:, 0:4 * nr].rearrange("p (k f) -> p k f", k=4),
                        pbk(b) + [("x", k, t0b) for k in range(hlf * 4, hlf * 4 + 4)], [("x", k, t0b) for k in range(hlf * 4, hlf * 4 + 4)])
        for l in range(DBG_LAYERS):
            ffn(l, 1)
            if DBG_STAGE >= 2:
                mixer(l, p)
            if DBG_STAGE >= 3:
                ffn(l, 2)
        P.cur = "final"
        Y.reset()
        tmp = norm_tmp()
        BIG.reset()
        yT = BIG.alloc([8, NTOK], F32)
        rmsnorm(x_fn, xk, TBLK, "final_norm", 0, lambda k, t0, n: yT[:, k, t0:t0 + n], lambda k, t0: [("BIG", "y", k, t0)], tmp)
        for i in range(TP // 128):
            t0b = i * 128 // 512 * 512
            transpose_out(lambda k, i=i: yT[:, k, i * 128:(i + 1) * 128], [("BIG", "y", k, t0b) for k in range(8)], 128,
                          [(0, 128, T["o_yp"][p * TP + i * 128:p * TP + (i + 1) * 128, :], ("oy", i % 2))])
        transpose_out(lambda k: yT[:, k, TP:NTOK], [("BIG", "y", k, TP) for k in range(8)], NST,
                      [(0, NST, T["o_ys"][p * NST:(p + 1) * NST, :], ("oys", 0))])
    return P, WS, finals


def build_nc():
    nc = bass.Bass("TRN2", target_bir_lowering=False)
    T = {}

    def din(name, shape):
        T[name] = nc.dram_tensor(name, list(shape), F32, kind="ExternalInput").ap()

    def dout(name, shape):
        T[name] = nc.dram_tensor(name, list(shape), F32, kind="ExternalOutput").ap()

    din("xp", (SEQ, D)); din("xs", (NSEQ * ST, D)); din("mem", (NMEM, D))
    din("spool", (DEPTH, NSEQ, PB, D)); din("sconv", (DEPTH, NSEQ, CB, D))
    din("ck", (DEPTH, NSEQ, NMEM, D)); din("cv", (DEPTH, NSEQ, NMEM, D))
    for nm in VROWS:
        din(nm, (DEPTH, D))
    din("final_norm", (1, D))
    din("ffn1_w_gu", (DEPTH, D, 2 * DFF)); din("ffn1_w_down", (DEPTH, DFF, D))
    din("ffn2_w_gu", (DEPTH, D, 2 * DFF)); din("ffn2_w_down", (DEPTH, DFF, D))
    din("w_in", (DEPTH, D, INC)); din("w_mem_kv", (DEPTH, D, 2 * D)); din("w_pool", (DEPTH, 4, 256, 256))
    din("w_dw", (DEPTH, 31, D)); din("w_pw", (DEPTH, D, D)); din("w_out", (DEPTH, D, D))
    dout("o_yp", (SEQ, D)); dout("o_ys", (NSEQ * ST, D))
    dout("o_pool_p", (DEPTH, PB, D)); dout("o_conv_p", (DEPTH, CB, D))
    dout("o_mk", (DEPTH, NMEM, D)); dout("o_mv", (DEPTH, NMEM, D))
    dout("o_pool_s", (DEPTH, NSEQ, PB, D)); dout("o_conv_s", (DEPTH, NSEQ, CB, D))

    A = {}
    A["xT"] = nc.alloc_sbuf_tensor("xT", [128, 8, NTOK], F32)
    A["hT"] = nc.alloc_sbuf_tensor("hT", [128, 8, NTOK], BF16)
    A["slots"] = [nc.alloc_sbuf_tensor("ws%d" % i, [128, SLOTE], BF16) for i in range(NSLOT)]
    A["BIG_bytes"] = 8 * NTOK * 4
    A["BIG"] = nc.alloc_sbuf_tensor("BIG", [128, A["BIG_bytes"] // 2], BF16)
    A["Y_bytes"] = int(os.environ.get("DBG_YKB", 54)) * 1024
    A["Y"] = nc.alloc_sbuf_tensor("Yar", [128, A["Y_bytes"] // 2], BF16)
    A["identf"] = nc.alloc_sbuf_tensor("identf", [128, 128], F32)
    A["identb"] = nc.alloc_sbuf_tensor("identb", [128, 128], BF16)
    A["onesm"] = nc.alloc_sbuf_tensor("onesm", [128, 128], BF16)
    A["epsT"] = nc.alloc_sbuf_tensor("epsT", [128, 1], F32)
    A["vecT"] = nc.alloc_sbuf_tensor("vecT", [128, 8, 33], F32)
    A["wdwT"] = nc.alloc_sbuf_tensor("wdwT", [128, 8, 124], F32)
    A["rcnt"] = nc.alloc_sbuf_tensor("rcnt", [128, 4, 16], F32)
    A["memT"] = nc.alloc_sbuf_tensor("memT", [128, 8, NMEM], F32)
    A["poolst"] = nc.alloc_sbuf_tensor("poolst", [128, DEPTH, 8, PB], F32)
    A["convst"] = nc.alloc_sbuf_tensor("convst", [128, DEPTH, 8, CB], F32)
    A["io"] = [nc.alloc_sbuf_tensor("io%d" % i, [128, D], F32) for i in range(2)]
    A["vstage"] = A["io"][0]
    A["wstage"] = A["io"][1]
    A["psall"] = nc.alloc_psum_tensor("psall", [128, 4096], F32)
    T["alloc"] = A
    Pd, WSd, _ = build_program(nc, T, None)
    plan = WSd.rec
    P, WS, finals = build_program(nc, T, plan)
    assert len(WS.rec) == len(plan)
    P.emit(nc, finals)
    nc._dbgP = P
    return nc


_NC_CACHE = {}


def kernel(**inp):
    f = lambda a: np.ascontiguousarray(np.asarray(a, dtype=np.float32))
    if "nc" not in _NC_CACHE:
        _NC_CACHE["nc"] = build_nc()
    nc = _NC_CACHE["nc"]
    shared = {nm: f(inp[nm]) for nm in VROWS}
    shared["final_norm"] = f(inp["final_norm"]).reshape(1, D)
    for nm in ("ffn1_w_gu", "ffn1_w_down", "ffn2_w_gu", "ffn2_w_down", "w_in", "w_mem_kv", "w_pool", "w_dw", "w_pw", "w_out"):
        shared[nm] = f(inp[nm])
    xp, xs, mem = f(inp["x_prompt"]), f(inp["x_sample"]), f(inp["mem_prompt"])
    sp, sc = f(inp["state_pool"]), f(inp["state_conv"])
    ck = f(inp["cache_mem_k"]).reshape(DEPTH, 128, NMEM, D)
    cv = f(inp["cache_mem_v"]).reshape(DEPTH, 128, NMEM, D)
    in_maps = []
    for c in range(NCORES):
        m = dict(shared)
        sl = slice(c * NSEQ, (c + 1) * NSEQ)
        m["xp"] = xp[c]
        m["xs"] = np.ascontiguousarray(xs[sl].reshape(NSEQ * ST, D))
        m["mem"] = mem[c]
        m["spool"] = np.ascontiguousarray(sp[:, sl])
        m["sconv"] = np.ascontiguousarray(sc[:, sl])
        m["ck"] = np.ascontiguousarray(ck[:, sl])
        m["cv"] = np.ascontiguousarray(cv[:, sl])
        in_maps.append(m)
    res = run_bass_kernel_spmd(nc, in_maps[:DBG_CORES], core_ids=list(range(DBG_CORES)))
    R = list(res.results)
    while len(R) < NCORES:
        R.append(R[0])
    y_p = np.stack([R[c]["o_yp"] for c in range(NCORES)], 0)
    y_s = np.concatenate([R[c]["o_ys"].reshape(NSEQ, ST, D) for c in range(NCORES)], 0)
    pool_p = np.stack([R[c]["o_pool_p"] for c in range(NCORES)], 1)
    conv_p = np.stack([R[c]["o_conv_p"] for c in range(NCORES)], 1)
    mk = np.stack([R[c]["o_mk"] for c in range(NCORES)], 1).reshape(DEPTH, NCORES, NMEM, 4, 256)
    mv = np.stack([R[c]["o_mv"] for c in range(NCORES)], 1).reshape(DEPTH, NCORES, NMEM, 4, 256)
    pool_s = np.concatenate([R[c]["o_pool_s"] for c in range(NCORES)], 1)
    conv_s = np.concatenate([R[c]["o_conv_s"] for c in range(NCORES)], 1)
    return (y_p.astype(np.float32), y_s.astype(np.float32), pool_p.astype(np.float32), conv_p.astype(np.float32),
            mk.astype(np.float32), mv.astype(np.float32), pool_s.astype(np.float32), conv_s.astype(np.float32))
```
